# Optimizing a Trainium2 kernel written in Bass

```python
import math
import jax, jax.numpy as jnp
from jax import lax
import numpy as np

D_MODEL = 1024
BATCH = 4
SEQ = 8192
DEPTH = 1

D_MIX = D_MODEL
CONV_WIDTH = D_MIX // 2
CONV_GROUPS = 8
CONV_K = 3
DN_WIDTH = D_MIX - CONV_WIDTH
DN_HEADS = 4
DN_HEAD_DIM = DN_WIDTH // DN_HEADS
DN_CONV_K = 4
DN_CHUNK = 64
IN_COLS = 3 * CONV_WIDTH + 4 * DN_WIDTH + 2 * DN_HEADS
MEM_LEN = 256
X_HEADS = 4
X_HEAD_DIM = D_MODEL // X_HEADS
PEER_HEADS = 8
PEER_KEYS = 128
PEER_EXPERTS = PEER_KEYS * PEER_KEYS
PEER_TOPK = 16
PEER_QDIM = 128
PEER_HALF = PEER_QDIM // 2
PEER_BLOCK = 128
EPS = 1e-6

kernel_name = "hybrid_conv_deltanet_peer_layer"


def rms_norm(x, gain, eps=EPS):
    xf = x.astype(jnp.float32)
    y = xf * lax.rsqrt(jnp.mean(xf * xf, axis=-1, keepdims=True) + eps)
    return (y * gain.astype(jnp.float32)).astype(x.dtype)


def l2_normalize(t, eps=EPS):
    tf = t.astype(jnp.float32)
    return tf * lax.rsqrt(jnp.sum(tf * tf, axis=-1, keepdims=True) + eps)


def causal_depthwise_conv(x, w):
    k = w.shape[0]
    return lax.conv_general_dilated(
        x, w[:, None, :].astype(x.dtype), window_strides=(1,), padding=[(k - 1, 0)],
        dimension_numbers=("NWC", "WIO", "NWC"), feature_group_count=x.shape[-1])


def chunk_gated_delta_rule(q, k, v, g, beta):
    f32 = jnp.float32
    bsz, seq, nh, dk = q.shape
    dv = v.shape[-1]
    nc = seq // DN_CHUNK

    def to_chunks(t):
        t = t.astype(f32).reshape(bsz, nc, DN_CHUNK, nh, *t.shape[3:])
        return jnp.moveaxis(t, 3, 1)

    q = to_chunks(q) * (dk ** -0.5)
    k = to_chunks(k)
    v = to_chunks(v)
    beta = to_chunks(beta)
    g = jnp.cumsum(to_chunks(g), axis=-1)

    causal = jnp.tril(jnp.ones((DN_CHUNK, DN_CHUNK), dtype=bool))
    strict = jnp.tril(jnp.ones((DN_CHUNK, DN_CHUNK), dtype=bool), -1)
    diff = g[..., :, None] - g[..., None, :]
    decay = jnp.where(causal, jnp.exp(jnp.where(causal, diff, 0.0)), 0.0)

    k_beta = k * beta[..., None]
    v_beta = v * beta[..., None]
    a = jnp.where(strict, jnp.einsum("bhnid,bhnjd->bhnij", k_beta, k) * decay, 0.0)
    eye = jnp.eye(DN_CHUNK, dtype=f32)
    t_mat = lax.linalg.triangular_solve(a + eye, jnp.broadcast_to(eye, a.shape),
                                        left_side=True, lower=True, unit_diagonal=True)
    u = jnp.einsum("bhnij,bhnjd->bhnid", t_mat, v_beta)
    w = jnp.einsum("bhnij,bhnjd->bhnid", t_mat, k_beta * jnp.exp(g)[..., None])
    intra = jnp.einsum("bhnid,bhnjd->bhnij", q, k) * decay
    q_dec = q * jnp.exp(g)[..., None]
    k_dec = k * jnp.exp(g[..., -1:] - g)[..., None]
    g_last = jnp.exp(g[..., -1])

    xs = tuple(jnp.moveaxis(t, 2, 0) for t in (q_dec, k_dec, u, w, intra, g_last))

    def step(state, inp):
        q_c, k_c, u_c, w_c, a_c, gl = inp
        v_new = u_c - jnp.einsum("bhcd,bhde->bhce", w_c, state)
        o = jnp.einsum("bhcd,bhde->bhce", q_c, state) + jnp.einsum("bhij,bhje->bhie", a_c, v_new)
        state = state * gl[..., None, None] + jnp.einsum("bhcd,bhce->bhde", k_c, v_new)
        return state, o

    s0 = jnp.zeros((bsz, nh, dk, dv), f32)
    _, o = lax.scan(step, s0, xs)
    return o.transpose(1, 0, 3, 2, 4).reshape(bsz, seq, nh, dv)


def hybrid_mixer(xn, w_in, conv_w, dn_conv_w, dn_a_log, dn_dt_bias, dn_norm, w_out):
    bsz, seq, _ = xn.shape
    proj = xn @ w_in
    o1 = CONV_WIDTH
    o2 = 2 * CONV_WIDTH
    o3 = 3 * CONV_WIDTH
    o4 = o3 + 3 * DN_WIDTH
    o5 = o4 + DN_WIDTH
    o6 = o5 + DN_HEADS
    b_gate, c_gate, h_conv, qkv, z, beta_logit, a_logit = jnp.split(
        proj, [o1, o2, o3, o4, o5, o6], axis=-1)

    y_conv = b_gate * causal_depthwise_conv(c_gate * h_conv, conv_w)

    qkv = jax.nn.silu(causal_depthwise_conv(qkv, dn_conv_w))
    q, k, v = jnp.split(qkv, 3, axis=-1)
    hs = (bsz, seq, DN_HEADS, DN_HEAD_DIM)
    q = l2_normalize(q.reshape(hs))
    k = l2_normalize(k.reshape(hs))
    v = v.reshape(hs)
    beta = jax.nn.sigmoid(beta_logit.astype(jnp.float32))
    g = -jnp.exp(dn_a_log.astype(jnp.float32)) * jax.nn.softplus(
        a_logit.astype(jnp.float32) + dn_dt_bias.astype(jnp.float32))
    o = chunk_gated_delta_rule(q, k, v, g, beta)
    o = rms_norm(o, dn_norm) * jax.nn.silu(z.reshape(hs).astype(jnp.float32))
    y_dn = o.reshape(bsz, seq, DN_WIDTH).astype(xn.dtype)

    return jnp.concatenate([y_conv, y_dn], axis=-1) @ w_out


def memory_cross_attention(xn, mem_n, w_q, w_kv, w_o):
    bsz, seq, _ = xn.shape
    q = (xn @ w_q).reshape(bsz, seq, X_HEADS, X_HEAD_DIM)
    kv = (mem_n @ w_kv).reshape(bsz, mem_n.shape[1], 2, X_HEADS, X_HEAD_DIM)
    k, v = kv[:, :, 0], kv[:, :, 1]
    s = jnp.einsum("bshd,bmhd->bhsm", q, k).astype(jnp.float32) * (X_HEAD_DIM ** -0.5)
    p = jax.nn.softmax(s, axis=-1).astype(xn.dtype)
    o = jnp.einsum("bhsm,bmhd->bshd", p, v).reshape(bsz, seq, D_MODEL)
    return o @ w_o


def peer_ffn(xn, w_query, sub_keys, expert_u, expert_v):
    bsz, seq, d = xn.shape
    q = (xn @ w_query).reshape(bsz, seq, PEER_HEADS, 2, PEER_HALF)
    scores = jnp.einsum("bshpd,hpnd->bshpn", q, sub_keys).astype(jnp.float32)
    s_top, i_top = lax.top_k(scores, PEER_TOPK)
    cand = (s_top[..., 0, :, None] + s_top[..., 1, None, :]).reshape(
        bsz, seq, PEER_HEADS, PEER_TOPK * PEER_TOPK)
    cand_idx = (i_top[..., 0, :, None] * PEER_KEYS + i_top[..., 1, None, :]).reshape(
        bsz, seq, PEER_HEADS, PEER_TOPK * PEER_TOPK)
    best, pos = lax.top_k(cand, PEER_TOPK)
    idx = jnp.take_along_axis(cand_idx, pos, axis=-1)
    gates = jax.nn.softmax(best, axis=-1).astype(xn.dtype)

    n_blk = (bsz * seq) // PEER_BLOCK
    sel = PEER_HEADS * PEER_TOPK
    xb = xn.reshape(n_blk, PEER_BLOCK, d)
    ib = idx.reshape(n_blk, PEER_BLOCK, sel)
    gb = gates.reshape(n_blk, PEER_BLOCK, sel)

    def block(args):
        xt, it, gt = args
        u = expert_u[it]
        v = expert_v[it]
        act = jax.nn.gelu(jnp.einsum("td,ted->te", xt, u), approximate=False)
        return jnp.einsum("te,ted->td", gt * act, v)

    y = lax.map(block, (xb, ib, gb))
    return y.reshape(bsz, seq, d)


def setup_inputs(seed: int = 0) -> dict:
    key = jax.random.key(seed)
    ks = iter(jax.random.split(key, 32))
    f32 = jnp.float32
    L = DEPTH

    def nrm(shape, scale):
        return jax.random.normal(next(ks), shape, f32) * scale

    def gain(shape):
        return 1.0 + 0.02 * jax.random.normal(next(ks), shape, f32)

    x = nrm((BATCH, SEQ, D_MODEL), 1.0)
    mem = nrm((BATCH, MEM_LEN, D_MODEL), 1.0)
    norm_mix = gain((L, D_MODEL))
    w_in = nrm((L, D_MODEL, IN_COLS), D_MODEL ** -0.5)
    conv_w = nrm((L, CONV_K, CONV_WIDTH), CONV_K ** -0.5)
    dn_conv_w = nrm((L, DN_CONV_K, 3 * DN_WIDTH), DN_CONV_K ** -0.5)
    dn_a_log = jnp.log(jax.random.uniform(next(ks), (L, DN_HEADS), f32, 1.0, 16.0))
    dt = jnp.exp(jax.random.uniform(next(ks), (L, DN_HEADS), f32,
                                    math.log(1e-3), math.log(1e-1)))
    dn_dt_bias = dt + jnp.log(-jnp.expm1(-dt))
    dn_norm = gain((L, DN_HEAD_DIM))
    w_out = nrm((L, D_MIX, D_MODEL), D_MIX ** -0.5)
    norm_x = gain((L, D_MODEL))
    norm_mem = gain((L, D_MODEL))
    w_xq = nrm((L, D_MODEL, D_MODEL), D_MODEL ** -0.5)
    w_xkv = nrm((L, D_MODEL, 2 * D_MODEL), D_MODEL ** -0.5)
    w_xo = nrm((L, D_MODEL, D_MODEL), D_MODEL ** -0.5)
    norm_ffn = gain((L, D_MODEL))
    w_pq = nrm((L, D_MODEL, PEER_HEADS * PEER_QDIM), D_MODEL ** -0.5)
    peer_keys = nrm((L, PEER_HEADS, 2, PEER_KEYS, PEER_HALF), PEER_HALF ** -0.5)
    expert_u = nrm((L, PEER_EXPERTS, D_MODEL), D_MODEL ** -0.5)
    expert_v = nrm((L, PEER_EXPERTS, D_MODEL), PEER_TOPK ** -0.5)
    norm_final = gain((D_MODEL,))
    return {"x": x, "mem": mem, "norm_mix": norm_mix, "w_in": w_in, "conv_w": conv_w,
            "dn_conv_w": dn_conv_w, "dn_a_log": dn_a_log, "dn_dt_bias": dn_dt_bias,
            "dn_norm": dn_norm, "w_out": w_out, "norm_x": norm_x, "norm_mem": norm_mem,
            "w_xq": w_xq, "w_xkv": w_xkv, "w_xo": w_xo, "norm_ffn": norm_ffn,
            "w_pq": w_pq, "peer_keys": peer_keys, "expert_u": expert_u,
            "expert_v": expert_v, "norm_final": norm_final}


def reference(x, mem, norm_mix, w_in, conv_w, dn_conv_w, dn_a_log, dn_dt_bias, dn_norm,
              w_out, norm_x, norm_mem, w_xq, w_xkv, w_xo, norm_ffn, w_pq, peer_keys,
              expert_u, expert_v, norm_final):
    h = x
    for layer in range(DEPTH):
        h = h + hybrid_mixer(rms_norm(h, norm_mix[layer]), w_in[layer], conv_w[layer],
                             dn_conv_w[layer], dn_a_log[layer], dn_dt_bias[layer],
                             dn_norm[layer], w_out[layer])
        h = h + memory_cross_attention(rms_norm(h, norm_x[layer]),
                                       rms_norm(mem, norm_mem[layer]),
                                       w_xq[layer], w_xkv[layer], w_xo[layer])
        h = h + peer_ffn(rms_norm(h, norm_ffn[layer]), w_pq[layer], peer_keys[layer],
                         expert_u[layer], expert_v[layer])
    return rms_norm(h, norm_final)
```

```python
import types
import numpy as np
import concourse.bass as bass
import concourse.mybir as mybir
from concourse.bass_utils import run_bass_kernel_spmd
from contextlib import ExitStack

F32 = mybir.dt.float32
BF16 = mybir.dt.bfloat16
U32 = mybir.dt.uint32
AF = mybir.ActivationFunctionType
ALU = mybir.AluOpType
AX = mybir.AxisListType

ENGS = ("pe", "dve", "act", "pool", "sp")
EPS = 1e-6
D = 1024
NEG = -1.0e30


def _freeze(fn):
    if fn is None or fn.__closure__ is None:
        return fn
    cells = tuple(types.CellType(c.cell_contents) for c in fn.__closure__)
    return types.FunctionType(fn.__code__, fn.__globals__, fn.__name__, fn.__defaults__, cells)


class Tr:
    __slots__ = ("name", "w", "r")

    def __init__(self, name=""):
        self.name = name
        self.w = {}
        self.r = {}


class Prog:
    def __init__(self, nc, es):
        self.nc = nc
        self.es = es
        self.ins = {e: [] for e in ENGS}
        self.known = {e: {} for e in ENGS}
        self.targets = {e: set() for e in ENGS}
        self.esem = {e: es.enter_context(nc.semaphore("sem_" + e)) for e in ENGS if e != "sp"}
        self.dsems = []
        self.dead = False

    def dsem(self):
        n = len(self.dsems) + 1
        s = self.es.enter_context(self.nc.semaphore(f"dsem{n}"))
        d = {"sem": s, "val": 0, "key": ("d", n)}
        self.dsems.append(d)
        return d

    def _need(self, eng, waits, key, val):
        if key == ("e", "pe") and eng == "pe":
            return
        if self.known[eng].get(key, 0) >= val:
            return
        if waits.get(key, 0) < val:
            waits[key] = val

    def _deps(self, eng, r, w, pw):
        waits = {}
        for t in r:
            for k, v in t.w.items():
                self._need(eng, waits, k, v)
        for t in w:
            for k, v in t.r.items():
                self._need(eng, waits, k, v)
            for k, v in t.w.items():
                self._need(eng, waits, k, v)
        for t in pw:
            for k, v in t.r.items():
                self._need(eng, waits, k, v)
        for k, v in waits.items():
            self.known[eng][k] = v
            if k[0] == "e":
                self.targets[k[1]].add(v)
        return waits

    def _commit(self, key, val, r, w, pw):
        for t in r:
            if t.r.get(key, 0) < val:
                t.r[key] = val
        for t in w:
            t.w = {key: val}
            t.r = {}
        for t in pw:
            if t.r:
                t.r = {}
                t.w = {key: val}
            else:
                t.w[key] = val

    def op(self, eng, fn, r=(), w=(), pw=(), xr=()):
        if self.dead:
            return
        waits = self._deps(eng, r, list(w) + list(xr), pw)
        seq = len(self.ins[eng]) + 1
        self.ins[eng].append([_freeze(fn), waits, None])
        self._commit(("e", eng), seq, list(r) + list(xr), w, pw)

    def dma(self, eng, out, in_, ds, r=(), w=(), pw=(), **kw):
        if self.dead:
            return
        waits = self._deps(eng, r, w, pw)
        ds["val"] += 16

        def fn(e):
            return e.dma_start(out=out, in_=in_, **kw)
        self.ins[eng].append([fn, waits, (ds["sem"], 16)])
        self._commit(ds["key"], ds["val"], r, w, pw)

    def barrier(self):
        last = {}
        for e in ENGS:
            if e == "sp":
                continue
            for idx in range(len(self.ins[e]) - 1, -1, -1):
                if self.ins[e][idx][0] is not None:
                    last[("e", e)] = idx + 1
                    break
        for d in self.dsems:
            if d["val"]:
                last[d["key"]] = d["val"]
        for e in ENGS:
            waits = {}
            for k, v in last.items():
                if k == ("e", e):
                    continue
                if self.known[e].get(k, 0) < v:
                    waits[k] = v
                    self.known[e][k] = v
                    if k[0] == "e":
                        self.targets[k[1]].add(v)
            if waits:
                self.ins[e].append([None, waits, None])

    def emit(self):
        nc = self.nc
        rank = {e: {s: i + 1 for i, s in enumerate(sorted(self.targets[e]))}
                for e in ENGS if e != "sp"}
        dmap = {d["key"]: d["sem"] for d in self.dsems}

        def run(name, e):
            rk = rank.get(name, {})
            for idx, (fn, waits, inc) in enumerate(self.ins[name]):
                for k, v in waits.items():
                    if k[0] == "e":
                        e.wait_ge(self.esem[k[1]], rank[k[1]][v])
                    else:
                        e.wait_ge(dmap[k], v)
                if fn is None:
                    continue
                i = fn(e)
                if inc is not None:
                    i.then_inc(inc[0], inc[1])
                elif (idx + 1) in rk:
                    i.then_inc(self.esem[name], 1)

        with nc.Block() as block:
            @block.tensor
            def _(e):
                run("pe", e)

            @block.vector
            def _(e):
                run("dve", e)

            @block.scalar
            def _(e):
                run("act", e)

            @block.gpsimd
            def _(e):
                run("pool", e)

            @block.sync
            def _(e):
                run("sp", e)


class Rot:
    def __init__(self, slots):
        self.slots = slots
        self.i = 0

    def get(self):
        s = self.slots[self.i % len(self.slots)]
        self.i += 1
        return s


class _Stop(Exception):
    pass


import os
DNSTOP = int(os.environ.get("DNSTOP", "99"))


def build(NT, NPF, phases="ABC", dbg=False, stop=99, ext_scratch=False):
    nc = bass.Bass("TRN2", target_bir_lowering=False)
    dt_in = lambda n, s: nc.dram_tensor(n, s, F32, kind="ExternalInput").ap()
    xm = dt_in("xm", [NT, D])
    xp = dt_in("xp", [NPF, D])
    mem = dt_in("mem", [256, D])
    norm_mix = dt_in("norm_mix", [D])
    w_in = dt_in("w_in", [D, 3592])
    conv_w = dt_in("conv_w", [3, 512])
    dn_conv_w = dt_in("dn_conv_w", [4, 1536])
    dn_a_log = dt_in("dn_a_log", [4])
    dn_dt_bias = dt_in("dn_dt_bias", [4])
    dn_norm = dt_in("dn_norm", [128])
    w_out = dt_in("w_out", [D, D])
    norm_x = dt_in("norm_x", [D])
    norm_mem = dt_in("norm_mem", [D])
    w_xq = dt_in("w_xq", [D, D])
    w_xkv = dt_in("w_xkv", [D, 2 * D])
    w_xo = dt_in("w_xo", [D, D])
    norm_ffn = dt_in("norm_ffn", [D])
    w_pq = dt_in("w_pq", [D, D])
    peer_keys = dt_in("peer_keys", [8, 2, 128, 64])
    expert_u = dt_in("expert_u", [16384, D])
    expert_v = dt_in("expert_v", [16384, D])
    norm_final = dt_in("norm_final", [D])
    out = nc.dram_tensor("out", [NT, D], F32, kind="ExternalOutput").ap()
    dbgk = nc.dram_tensor("dbgk", [NT // 256, 128, 3, 256], F32, kind="ExternalOutput").ap() if dbg else None
    kind_s = "ExternalOutput" if (dbg or ext_scratch) else "Internal"
    h1s = nc.dram_tensor("h1s", [NT, D], F32, kind=kind_s).ap()
    h2s = nc.dram_tensor("h2s", [NT, D], F32, kind=kind_s).ap()
    uts = nc.dram_tensor("uts", [128, 128, D], BF16, kind=kind_s).ap()
    vbs = nc.dram_tensor("vbs", [128, 128, D], BF16, kind=kind_s).ap()

    es0 = ExitStack()
    with es0:
        P = Prog(nc, es0)
        T_h1s, T_h2s, T_uts, T_vbs = Tr("h1s"), Tr("h2s"), Tr("uts"), Tr("vbs")

        def sbuf(es, name, shape, dt=F32):
            return es.enter_context(nc.sbuf_tensor(name, shape, dt))

        def psum(es, name, shape, dt=F32):
            return es.enter_context(nc.psum_tensor(name, shape, dt))

        def rot(es, name, n, shape, dt=F32, ps=False):
            if ps and dt == F32 and list(shape) == [128]:
                t = psum(es, name, [128, n, 512], dt)
                return Rot([(t[:, i, 0:128], Tr(f"{name}{i}")) for i in range(n)])
            t = (psum if ps else sbuf)(es, name, [128, n] + list(shape), dt)
            return Rot([(t[:, i], Tr(f"{name}{i}")) for i in range(n)])

        idt = sbuf(es0, "idt", [128, 128]); T_idt = Tr()
        idb = sbuf(es0, "idb", [128, 128], BF16); T_idb = Tr()
        ones_f = sbuf(es0, "ones_f", [128, 128]); T_ones = Tr()
        ones_b = sbuf(es0, "ones_b", [128, 128], BF16); T_onesb = Tr()
        m_ui = sbuf(es0, "m_ui", [128, 128]); T_mui = Tr()
        m_lsbd = sbuf(es0, "m_lsbd", [128, 128]); T_mlsbd = Tr()
        m_off = sbuf(es0, "m_off", [128, 128]); T_moff = Tr()
        iota_f = sbuf(es0, "iota_f", [128, 128]); T_iota = Tr()
        iota_b = sbuf(es0, "iota_b", [128, 128], BF16)

        P.op("pool", lambda e: e.memset(idt[:], 0.0), w=[T_idt])
        P.op("pool", lambda e: e.affine_select(out=idt[:], in_=idt[:], pattern=[[-1, 128]], compare_op=ALU.not_equal,
                                               fill=1.0, base=0, channel_multiplier=1), w=[T_idt])
        P.op("pool", lambda e: e.tensor_copy(out=idb[:], in_=idt[:]), r=[T_idt], w=[T_idb])
        P.op("pool", lambda e: e.memset(ones_f[:], 1.0), w=[T_ones])
        P.op("pool", lambda e: e.memset(ones_b[:], 1.0), w=[T_onesb])
        P.op("pool", lambda e: e.memset(m_ui[:], 1.0), w=[T_mui])
        P.op("pool", lambda e: e.affine_select(out=m_ui[:], in_=m_ui[:], pattern=[[1, 128]], compare_op=ALU.is_ge,
                                               fill=0.0, base=0, channel_multiplier=-1), w=[T_mui])
        P.op("pool", lambda e: e.memset(m_lsbd[:], 1.0), w=[T_mlsbd])
        P.op("pool", lambda e: e.affine_select(out=m_lsbd[:], in_=m_lsbd[:], pattern=[[-1, 128]], compare_op=ALU.is_gt,
                                               fill=0.0, base=0, channel_multiplier=1), w=[T_mlsbd])
        P.op("pool", lambda e: e.memset(m_off[:], 0.0), w=[T_moff])
        P.op("pool", lambda e: e.memset(m_off[64:128, 0:64], 1.0), w=[T_moff])
        P.op("pool", lambda e: e.memset(m_lsbd[64:128, 0:64], 0.0), w=[T_mlsbd])
        P.op("pool", lambda e: e.iota(out=iota_f[:], pattern=[[1, 128]], base=0, channel_multiplier=0,
                                      allow_small_or_imprecise_dtypes=True), w=[T_iota])
        P.op("pool", lambda e: e.tensor_copy(out=iota_b[:], in_=iota_f[:]), r=[T_iota], w=[T_iota])

        dW = P.dsem()
        dX = [P.dsem() for _ in range(3)]
        dS1 = [P.dsem(), P.dsem()]
        dS2 = [P.dsem(), P.dsem()]
        dOut = [P.dsem() for _ in range(2)]

        def load_cols(es, name, aps, ps_slot):
            raw = sbuf(es, name + "_raw", [128, 128]); traw = Tr()
            colT = sbuf(es, name, [128, 128]); tcol = Tr()
            P.op("pool", lambda e: e.memset(raw[:], 0.0), w=[traw])
            r0 = 0
            dsc = P.dsem()
            for a in aps:
                P.dma("sp", raw[r0:r0 + a.shape[0], :], a, dsc, r=[traw], pw=[traw])
                r0 += a.shape[0]
            pp, tpp = ps_slot
            if os.environ.get("NOF32") == "1":
                P.op("act", lambda e: e.copy(out=colT[:], in_=raw[:]), r=[traw], w=[tcol])
            else:
                P.op("pe", lambda e: e.transpose(out=pp, in_=raw[:], identity=idt[:]), r=[traw, T_idt], w=[tpp])
                P.op("act", lambda e: e.copy(out=colT[:], in_=pp), r=[tpp], w=[tcol])
            return colT, tcol

        wstage = rot(es0, "wstage", 2, [512])
        dWs = [P.dsem() for _ in range(2)]

        def load_w(es, name, w, ncols, c0=0):
            npad = (ncols + 127) // 128 * 128
            wt = sbuf(es, name, [128, 8, npad], BF16); t = Tr()
            k = 0
            for c in range(8):
                for n0 in range(0, ncols, 512):
                    n1 = min(n0 + 512, ncols)
                    st, tst = wstage.get()
                    P.dma("sp", st[:, 0:n1 - n0], w[c * 128:(c + 1) * 128, c0 + n0:c0 + n1], dWs[(wstage.i - 1) % 2], w=[tst])
                    eng = "act" if k % 2 == 0 else "pool"
                    if eng == "act":
                        P.op("act", lambda e, st=st, c=c, n0=n0, n1=n1: e.copy(out=wt[:, c, n0:n1], in_=st[:, 0:n1 - n0]),
                             r=[tst], pw=[t])
                    else:
                        P.op("pool", lambda e, st=st, c=c, n0=n0, n1=n1: e.tensor_copy(out=wt[:, c, n0:n1], in_=st[:, 0:n1 - n0]),
                             r=[tst], pw=[t])
                    k += 1
            return wt, t

        def rstd_from_ss(ss_ap, tss, n, rs_ap, trs, shape_note=None):
            P.op("dve", lambda e: e.tensor_scalar(out=rs_ap, in0=ss_ap, scalar1=1.0 / n, scalar2=EPS,
                                                  op0=ALU.mult, op1=ALU.add), r=[tss], w=[trs])
            P.op("act", lambda e: e.activation(out=rs_ap, in_=rs_ap, func=AF.Ln), r=[trs], w=[trs])
            P.op("act", lambda e: e.activation(out=rs_ap, in_=rs_ap, func=AF.Exp, scale=-0.5), r=[trs], w=[trs])

        def norm_T(es_tmp, pools, src_dram_rows, gain, tgain, dst, tdst, col0, dq):
            xs, txs = pools["xs"].get()
            P.dma("sp", xs, src_dram_rows, dq, w=[txs])
            junk, tj = pools["junk"].get()
            ss, tss = pools["ss"].get()
            P.op("act", lambda e: e.activation(out=junk, in_=xs, func=AF.Square, accum_out=ss[:, 0:1]),
                 r=[txs], w=[tj, tss])
            rstd_from_ss(ss[:, 0:1], tss, D, ss[:, 1:2], tss)
            xn, txn = pools["xn"].get()
            P.op("dve", lambda e: e.tensor_scalar(out=xn, in0=xs, scalar1=ss[:, 1:2], scalar2=None, op0=ALU.mult),
                 r=[txs, tss], w=[txn])
            pt, tpt = pools["ptb"].get()
            for c in range(8):
                P.op("pe", lambda e, c=c: e.transpose(out=pt[:, c, :], in_=xn[:, c * 128:(c + 1) * 128], identity=idb[:]),
                     r=[txn, T_idb], pw=[tpt])
            P.op("dve", lambda e: e.tensor_tensor(out=dst[:, :, col0:col0 + 128], in0=pt,
                                                  in1=gain.unsqueeze(2).broadcast_to([128, 8, 128]), op=ALU.mult),
                 r=[tpt, tgain], pw=[tdst])
            return xs, txs

        if "A" in phases:
          try:
            with ExitStack() as es:
                if stop <= -1:
                    P.dead = True
                win, T_win = load_w(es, "win", w_in, 3592)
                wout, T_wout = load_w(es, "wout", w_out, D)
                if stop <= -1:
                    P.dead = False
                pm = rot(es, "pm", 1, [128], ps=True)
                colsA, T_colsA = load_cols(es, "colsA", [norm_mix.rearrange("(c p) -> c p", p=128)]
                                           + [conv_w[k].rearrange("(j p) -> j p", p=128) for k in range(3)]
                                           + [dn_conv_w[k].rearrange("(j p) -> j p", p=128) for k in range(4)]
                                           + [dn_norm.rearrange("(o p) -> o p", o=1)], pm.get())
                g_mix, T_gmix = colsA[:, 0:8], T_colsA
                T_cw = T_dcw = T_dnn = T_colsA
                cwv = lambda j, k: colsA[:, 8 + k * 4 + j:9 + k * 4 + j]
                dcwv = lambda j, k: colsA[:, 20 + k * 12 + j:21 + k * 12 + j]
                dnn = colsA[:, 68:69]
                if stop <= -2:
                    P.dead = True
                if stop <= -3:
                    P.dead = True
                alog = sbuf(es, "alog", [128, 4]); T_alog = Tr()
                P.dma("sp", alog[:], dn_a_log.partition_broadcast(128), P.dsem(), w=[T_alog])
                dtb = sbuf(es, "dtb", [128, 4]); T_dtb = Tr()
                P.dma("sp", dtb[:], dn_dt_bias.partition_broadcast(128), P.dsem(), w=[T_dtb])
                if stop <= -4:
                    P.dead = True
                negA = sbuf(es, "negA", [128, 4]); T_negA = Tr()
                P.op("act", lambda e: e.activation(out=negA[:], in_=alog[:], func=AF.Exp), r=[T_alog], w=[T_negA])
                P.op("dve", lambda e: e.tensor_scalar(out=negA[:], in0=negA[:], scalar1=-1.0, scalar2=None, op0=ALU.mult),
                     r=[T_negA], w=[T_negA])

                if stop <= 0:
                    P.dead = True
                pools = {
                    "xs": rot(es, "xs", 2, [D]), "junk": rot(es, "junk", 1, [D], BF16), "ss": rot(es, "ss", 4, [2]),
                    "xn": rot(es, "xn", 2, [D], BF16), "ptb": rot(es, "ptb", 1, [8, 128], BF16, ps=True),
                }
                xnT = sbuf(es, "xnT", [128, 8, 512], BF16); T_xnT = Tr()
                pj = rot(es, "pj", 2, [512], ps=True)
                pd = rot(es, "pd", 4, [128], ps=True)
                psn = pd
                CH = sbuf(es, "CH", [128, 4, 514]); T_CH = [Tr() for _ in range(4)]
                HALO = sbuf(es, "HALO", [128, 12, 3]); T_HALO = [Tr() for _ in range(12)]
                P.op("pool", lambda e: e.memset(CH[:], 0.0), w=T_CH)
                P.op("pool", lambda e: e.memset(HALO[:], 0.0), w=T_HALO)
                tmpc = rot(es, "tmpc", 1, [512])
                tcv = rot(es, "tcv", 1, [515])
                acc = rot(es, "acc", 2, [512])
                sqb = rot(es, "sqb", 1, [512], BF16)
                rnb = rot(es, "rnb", 1, [512])
                QKV = sbuf(es, "QKV", [128, 12, 512], BF16); T_QKV = [Tr() for _ in range(12)]
                ZS = sbuf(es, "ZS", [128, 4, 512], BF16); T_ZS = [Tr() for _ in range(4)]
                ycat = sbuf(es, "ycat", [128, 8, 512], BF16); T_ycat = [Tr() for _ in range(8)]
                OT = sbuf(es, "OT", [128, 4, 512]); T_OT = [Tr() for _ in range(4)]
                gl = sbuf(es, "gl", [128, 4, 128]); T_gl = Tr()
                beta = sbuf(es, "beta", [128, 4, 4]); T_beta = Tr()
                gg = sbuf(es, "gg", [128, 4, 4]); T_gg = Tr()
                Sf = sbuf(es, "Sf", [128, 4, 128]); T_Sf = [Tr() for _ in range(4)]
                Sb = sbuf(es, "Sb", [128, 4, 128], BF16); T_Sb = [Tr() for _ in range(4)]
                P.op("pool", lambda e: e.memset(Sf[:], 0.0), w=T_Sf)
                P.op("pool", lambda e: e.memset(Sb[:], 0.0), w=T_Sb)
                sm = rot(es, "sm", 24, [16])
                sht = rot(es, "sht", 24, [128])
                lng = rot(es, "lng", 16, [128])
                dbl = rot(es, "dbl", 24, [128])
                b16t = rot(es, "b16t", 32, [128], BF16)
                h1t = rot(es, "h1t", 1, [D])

                def proj_fm(col0, ncol=128):
                    pp, tpp = pj.get()
                    for c in range(8):
                        P.op("pe", lambda e, c=c: e.matmul(pp[0:ncol, :], lhsT=win[:, c, col0:col0 + ncol], rhs=xnT[:, c, :],
                                                          start=(c == 0), stop=(c == 7)),
                             r=[T_win, T_xnT], **({"w": [tpp]} if c == 0 else {"pw": [tpp]}))
                    return pp, tpp

                def dn_chunk(s, main):
                    sl = slice(s * 128, (s + 1) * 128)
                    pg, tpg = pm.get()
                    P.op("pe", lambda e: e.matmul(pg[:, 0:4], lhsT=m_ui[:], rhs=gg[:, s, :], start=True, stop=True),
                         r=[T_mui, T_gg], w=[tpg])
                    sc, tsc = sm.get()
                    P.op("act", lambda e: e.copy(out=sc[:, 0:4], in_=pg[:, 0:4]), r=[tpg], w=[tsc])
                    eg, teg = sm.get()
                    P.op("act", lambda e: e.activation(out=eg[:, 0:4], in_=sc[:, 0:4], func=AF.Exp), r=[tsc], w=[teg])
                    P.op("dve", lambda e: e.tensor_tensor(out=sc[:, 4:8], in0=eg[:, 0:4], in1=beta[:, s, :], op=ALU.mult),
                         r=[teg, T_beta, tsc], w=[tsc])
                    def head(h):
                        tq, tk, tv = T_QKV[h], T_QKV[4 + h], T_QKV[8 + h]
                        qT = QKV[:, h, sl]; kT = QKV[:, 4 + h, sl]; vT = QKV[:, 8 + h, sl]
                        pt, tpt = pools["ptb"].get()
                        P.op("pe", lambda e: e.transpose(out=pt[:, 0, :], in_=kT, identity=idb[:]), r=[tk, T_idb], w=[tpt])
                        P.op("pe", lambda e: e.transpose(out=pt[:, 1, :], in_=vT, identity=idb[:]), r=[tv, T_idb], pw=[tpt])
                        if DNSTOP <= 0:
                            P.dead = True
                        ug, tug = sht.get()
                        P.op("dve", lambda e: e.tensor_scalar(out=ug, in0=m_ui[:], scalar1=gg[:, s, h:h + 1], scalar2=None,
                                                              op0=ALU.mult), r=[T_mui, T_gg], w=[tug])
                        pgb, tpgb = pm.get()
                        P.op("pe", lambda e: e.matmul(pgb, lhsT=ones_f[:], rhs=ug, start=True, stop=True),
                             r=[T_ones, tug], w=[tpgb])
                        if DNSTOP <= 1:
                            P.dead = True
                        elo, telo = sht.get()
                        P.op("dve", lambda e: e.tensor_scalar(out=elo, in0=pgb, scalar1=sc[:, h:h + 1], scalar2=0.0,
                                                              op0=ALU.subtract, op1=ALU.max), r=[tpgb, tsc], w=[telo])
                        P.op("act", lambda e: e.activation(out=elo, in_=elo, func=AF.Exp, scale=-1.0), r=[telo], w=[telo])
                        eup, teup = lng.get()
                        P.op("dve", lambda e: e.tensor_scalar(out=eup, in0=pgb, scalar1=sc[:, h:h + 1], scalar2=0.0,
                                                              op0=ALU.subtract, op1=ALU.min), r=[tpgb, tsc], w=[teup])
                        P.op("act", lambda e: e.activation(out=eup, in_=eup, func=AF.Exp), r=[teup], w=[teup])
                        P.op("pool", lambda e: e.tensor_tensor(out=eup, in0=eup, in1=m_ui[:], op=ALU.mult),
                             r=[teup, T_mui], w=[teup])
                        md, tmd = sht.get()
                        mo, tmo = sht.get()
                        P.op("pool", lambda e: e.tensor_tensor(out=md, in0=elo, in1=m_lsbd[:], op=ALU.mult),
                             r=[telo, T_mlsbd], w=[tmd])
                        P.op("pool", lambda e: e.tensor_tensor(out=mo, in0=elo, in1=m_off[:], op=ALU.mult),
                             r=[telo, T_moff], w=[tmo])
                        if DNSTOP <= 2:
                            P.dead = True
                        s2, ts2 = sm.get()
                        P.op("dve", lambda e: e.tensor_copy(out=s2[:, 0:1], in_=pgb[:, 127:128]), r=[tpgb], w=[ts2])
                        P.op("act", lambda e: e.activation(out=s2[:, 1:2], in_=sc[:, h:h + 1], func=AF.Exp, scale=-1.0,
                                                           bias=s2[:, 0:1]), r=[tsc, ts2], w=[ts2])
                        P.op("act", lambda e: e.activation(out=s2[:, 2:3], in_=s2[:, 0:1], func=AF.Exp), r=[ts2], w=[ts2])
                        if main:
                            egb, tegb = lng.get()
                            P.op("act", lambda e: e.activation(out=egb, in_=pgb, func=AF.Exp), xr=[tpgb], w=[tegb])
                        if DNSTOP <= 3:
                            P.dead = True
                        kbe, tkbe = b16t.get(); kdec, tkdec = b16t.get(); vb, tvb = b16t.get()
                        P.op("dve", lambda e: e.tensor_scalar(out=kbe, in0=pt[:, 0, :], scalar1=sc[:, 4 + h:5 + h], scalar2=None,
                                                              op0=ALU.mult), r=[tpt, tsc], w=[tkbe])
                        P.op("dve", lambda e: e.tensor_scalar(out=kdec, in0=pt[:, 0, :], scalar1=s2[:, 1:2], scalar2=None, op0=ALU.mult),
                             r=[tpt, ts2], w=[tkdec])
                        P.op("dve", lambda e: e.tensor_scalar(out=vb, in0=pt[:, 1, :], scalar1=beta[:, s, h:h + 1], scalar2=None,
                                                              op0=ALU.mult), r=[tpt, T_beta], w=[tvb])
                        if DNSTOP <= 4:
                            P.dead = True
                        pkk, tpkk = pd.get()
                        P.op("pe", lambda e: e.matmul(pkk, lhsT=kT, rhs=kT, start=True, stop=True), r=[tk], w=[tpkk])
                        yield
                        Ad, tAd = sht.get(); Ao, tAo = lng.get()
                        P.op("dve", lambda e: e.scalar_tensor_tensor(out=Ad, in0=pkk, scalar=beta[:, s, h:h + 1], in1=md,
                                                                     op0=ALU.mult, op1=ALU.mult), r=[tpkk, T_beta, tmd], w=[tAd])
                        P.op("dve", lambda e: e.scalar_tensor_tensor(out=Ao, in0=pkk, scalar=beta[:, s, h:h + 1], in1=mo,
                                                                     op0=ALU.mult, op1=ALU.mult), r=[tpkk, T_beta, tmo], w=[tAo])
                        if DNSTOP <= 5:
                            P.dead = True
                        pat, tpat = pd.get()
                        P.op("pe", lambda e: e.transpose(out=pat, in_=Ad, identity=idt[:]), r=[tAd, T_idt], w=[tpat])
                        yield
                        PT_, tPT = sht.get(); RT, tRT = sht.get()
                        P.op("dve", lambda e: e.tensor_copy(out=PT_, in_=pat), r=[tpat], w=[tPT])
                        P.op("dve", lambda e: e.tensor_tensor(out=RT, in0=idt[:], in1=pat, op=ALU.subtract),
                             r=[T_idt, tpat], w=[tRT])
                        Pk, tPk = Ad, tAd
                        for lvl in range(1, 6):
                            pp1, tpp1 = pd.get()
                            P.op("pe", lambda e, a=PT_, b=Pk, o=pp1: e.matmul(o, lhsT=a, rhs=b, start=True, stop=True),
                                 r=[tPT, tPk], w=[tpp1])
                            yield
                            nPk, tnPk = dbl.get()
                            P.op("act", lambda e, o=nPk, i=pp1: e.copy(out=o, in_=i), r=[tpp1], w=[tnPk])
                            if lvl < 5:
                                pp2, tpp2 = pd.get()
                                P.op("pe", lambda e, a=Pk, b=PT_, o=pp2: e.matmul(o, lhsT=a, rhs=b, start=True, stop=True),
                                     r=[tPT, tPk], w=[tpp2])
                                yield
                                nPT, tnPT = dbl.get()
                                P.op("dve", lambda e, o=nPT, i=pp2: e.tensor_copy(out=o, in_=i), r=[tpp2], w=[tnPT])
                            pp3, tpp3 = pd.get()
                            P.op("pe", lambda e, a=nPk, b=RT, o=pp3: e.matmul(o, lhsT=a, rhs=b, start=True, stop=True),
                                 r=[tnPk, tRT], w=[tpp3])
                            yield
                            nRT, tnRT = dbl.get()
                            P.op("dve", lambda e, o=nRT, a=RT, b=pp3: e.tensor_tensor(out=o, in0=a, in1=b, op=ALU.add),
                                 r=[tRT, tpp3], w=[tnRT])
                            Pk, tPk = nPk, tnPk
                            if lvl < 5:
                                PT_, tPT = nPT, tnPT
                            RT, tRT = nRT, tnRT
                        if DNSTOP <= 6:
                            P.dead = True
                        ptd, tptd = pd.get()
                        P.op("pe", lambda e: e.transpose(out=ptd, in_=RT, identity=idt[:]), r=[tRT, T_idt], w=[tptd])
                        yield
                        Td, tTd = sht.get()
                        P.op("act", lambda e: e.copy(out=Td, in_=ptd), r=[tptd], w=[tTd])
                        pm1, tpm1 = pd.get()
                        P.op("pe", lambda e: e.matmul(pm1, lhsT=Ao, rhs=RT, start=True, stop=True), r=[tAo, tRT], w=[tpm1])
                        yield
                        M1, tM1 = sht.get()
                        P.op("act", lambda e: e.copy(out=M1, in_=pm1), r=[tpm1], w=[tM1])
                        pm2, tpm2 = pd.get()
                        P.op("pe", lambda e: e.matmul(pm2, lhsT=Td, rhs=M1, start=True, stop=True), r=[tTd, tM1], w=[tpm2])
                        yield
                        TT, tTT = b16t.get()
                        P.op("dve", lambda e: e.tensor_tensor(out=TT, in0=RT, in1=pm2, op=ALU.subtract), r=[tRT, tpm2], w=[tTT])
                        if DNSTOP <= 7:
                            P.dead = True
                        pu, tpu = pd.get()
                        P.op("pe", lambda e: e.matmul(pu, lhsT=TT, rhs=vb, start=True, stop=True), r=[tTT, tvb], w=[tpu])
                        yield
                        u, tu = lng.get()
                        P.op("act", lambda e: e.copy(out=u, in_=pu), r=[tpu], w=[tu])
                        pw_, tpw = pd.get()
                        P.op("pe", lambda e: e.matmul(pw_, lhsT=kbe, rhs=TT, start=True, stop=True), r=[tkbe, tTT], w=[tpw])
                        yield
                        wT, twT = b16t.get()
                        P.op("act", lambda e: e.copy(out=wT, in_=pw_), r=[tpw], w=[twT])
                        if main:
                            pqk, tpqk = pd.get()
                            P.op("pe", lambda e: e.matmul(pqk, lhsT=kT, rhs=qT, start=True, stop=True), r=[tk, tq], w=[tpqk])
                            yield
                            aqk, taqk = b16t.get()
                            P.op("dve", lambda e: e.tensor_tensor(out=aqk, in0=pqk, in1=eup, op=ALU.mult),
                                 r=[tpqk, teup], w=[taqk])
                            qd, tqd = b16t.get()
                            P.op("dve", lambda e: e.tensor_tensor(out=qd, in0=qT, in1=egb, op=ALU.mult),
                                 r=[tq, tegb], w=[tqd])
                        if DNSTOP <= 8:
                            P.dead = True
                        pws, tpws = psn.get()
                        P.op("pe", lambda e: e.matmul(pws, lhsT=wT, rhs=Sb[:, h, :], start=True, stop=True),
                             r=[twT, T_Sb[h]], w=[tpws])
                        yield
                        vn, tvn = b16t.get()
                        P.op("dve", lambda e: e.tensor_tensor(out=vn, in0=u, in1=pws, op=ALU.subtract), r=[tu, tpws], w=[tvn])
                        if main:
                            po, tpo = psn.get()
                            P.op("pe", lambda e: e.matmul(po, lhsT=Sb[:, h, :], rhs=qd, start=True, stop=False),
                                 r=[T_Sb[h], tqd], w=[tpo])
                            yield
                            P.op("pe", lambda e: e.matmul(po, lhsT=vn, rhs=aqk, start=False, stop=True),
                                 r=[tvn, taqk], pw=[tpo])
                            yield
                            P.op("act", lambda e: e.copy(out=OT[:, h, sl], in_=po), r=[tpo], pw=[T_OT[h]])
                        pds, tpds = psn.get()
                        P.op("pe", lambda e: e.matmul(pds, lhsT=kdec, rhs=vn, start=True, stop=True), r=[tkdec, tvn], w=[tpds])
                        yield
                        P.op("dve", lambda e: e.scalar_tensor_tensor(out=Sf[:, h, :], in0=Sf[:, h, :], scalar=s2[:, 2:3], in1=pds,
                                                                     op0=ALU.mult, op1=ALU.add),
                             r=[tpds, ts2], w=[T_Sf[h]])
                        P.op("pool", lambda e: e.tensor_copy(out=Sb[:, h, :], in_=Sf[:, h, :]), r=[T_Sf[h]], w=[T_Sb[h]])

                    for h0 in (0,):
                        gens = [head(h_) for h_ in range(4)]
                        while gens:
                            for g_ in list(gens):
                                try:
                                    next(g_)
                                except StopIteration:
                                    gens.remove(g_)
                print("SBUF remaining after phase A alloc:", nc.sbuf_bytes_remaining)
                tiles = [("p", i) for i in range(NPF // 512)] + [("m", i) for i in range(NT // 512)]
                for ti, (kind, i) in enumerate(tiles):
                    main = kind == "m"
                    full = main or (ti + 1 < len(tiles) and tiles[ti + 1][0] == "m")
                    src = xm if main else xp
                    for s in range(4):
                        r0 = i * 512 + s * 128
                        norm_T(es, pools, src[r0:r0 + 128, :], g_mix, T_gmix, xnT, T_xnT, s * 128, dX[s % 3])
                    if stop <= 1:
                        P.dead = True
                    for s in range(4):
                        pg, tpg = pm.get()
                        for c in range(8):
                            P.op("pe", lambda e, c=c, s=s: e.matmul(pg, lhsT=xnT[:, c, s * 128:(s + 1) * 128],
                                                                  rhs=win[:, c, 3464:3592], start=(c == 0), stop=(c == 7)),
                                 r=[T_win, T_xnT], **({"w": [tpg]} if c == 0 else {"pw": [tpg]}))
                        if os.environ.get("SUB2") != "c":
                            P.op("dve", lambda e, s=s: e.tensor_copy(out=gl[:, s, :], in_=pg), r=[tpg], pw=[T_gl])
                    if stop == 2 and NPF == 512:
                        if os.environ.get("SUB") == "a":
                            P.dead = True
                    P.op("act", lambda e: e.activation(out=beta[:], in_=gl[:, :, 120:124], func=AF.Exp, scale=-1.0), r=[T_gl], w=[T_beta])
                    P.op("dve", lambda e: e.tensor_scalar(out=beta[:], in0=beta[:], scalar1=1.0, scalar2=None, op0=ALU.add),
                         r=[T_beta], w=[T_beta])
                    P.op("dve", lambda e: e.reciprocal(out=beta[:], in_=beta[:]), r=[T_beta], w=[T_beta])
                    P.op("dve", lambda e: e.tensor_tensor(out=gg[:], in0=gl[:, :, 124:128],
                                                          in1=dtb[:].unsqueeze(1).broadcast_to([128, 4, 4]), op=ALU.add),
                         r=[T_gl, T_dtb], w=[T_gg])
                    P.op("act", lambda e: e.activation(out=gg[:], in_=gg[:], func=AF.Exp), r=[T_gg], w=[T_gg])
                    P.op("dve", lambda e: e.tensor_scalar(out=gg[:], in0=gg[:], scalar1=1.0, scalar2=None, op0=ALU.add),
                         r=[T_gg], w=[T_gg])
                    P.op("act", lambda e: e.activation(out=gg[:], in_=gg[:], func=AF.Ln), r=[T_gg], w=[T_gg])
                    P.op("dve", lambda e: e.tensor_tensor(out=gg[:], in0=gg[:],
                                                          in1=negA[:].unsqueeze(1).broadcast_to([128, 4, 4]), op=ALU.mult),
                         r=[T_gg, T_negA], w=[T_gg])
                    if stop <= 2:
                        P.dead = True
                    if full:
                        for j in range(4):
                            pc, tpc = proj_fm(512 + j * 128)
                            tc_, ttc = tmpc.get()
                            P.op("act", lambda e, o=tc_, i_=pc: e.copy(out=o, in_=i_), r=[tpc], w=[ttc])
                            ph, tph = proj_fm(1024 + j * 128)
                            P.op("dve", lambda e, j=j, a=tc_, b=ph: e.tensor_tensor(out=CH[:, j, 2:514], in0=a, in1=b, op=ALU.mult),
                                 r=[ttc, tph], w=[T_CH[j]])
                            if main:
                                a_, ta = acc.get()
                                P.op("dve", lambda e, j=j, a_=a_: e.tensor_scalar(out=a_, in0=CH[:, j, 0:512], scalar1=cwv(j, 0),
                                                                                   scalar2=None, op0=ALU.mult), r=[T_CH[j], T_cw], w=[ta])
                                for k in (1, 2):
                                    P.op("dve", lambda e, j=j, a_=a_, k=k: e.scalar_tensor_tensor(
                                        out=a_, in0=CH[:, j, k:k + 512], scalar=cwv(j, k), in1=a_, op0=ALU.mult, op1=ALU.add),
                                        r=[T_CH[j], T_cw], w=[ta])
                                pb, tpb = proj_fm(j * 128)
                                P.op("dve", lambda e, j=j, a_=a_, pb=pb: e.tensor_tensor(out=ycat[:, j, :], in0=a_, in1=pb, op=ALU.mult),
                                     r=[ta, tpb], w=[T_ycat[j]])
                            P.op("pool", lambda e, j=j: e.tensor_copy(out=CH[:, j, 0:2], in_=CH[:, j, 512:514]), r=[], w=[T_CH[j]])
                    if stop <= 3:
                        P.dead = True
                    pend_l2 = []
                    for j in range(12):
                        pq, tpq = proj_fm(1536 + j * 128)
                        while pend_l2:
                            pend_l2.pop(0)()
                        tv_, ttv = tcv.get()
                        P.op("act", lambda e, o=tv_, i_=pq: e.copy(out=o[:, 3:515], in_=i_), r=[tpq], w=[ttv])
                        P.op("pool", lambda e, o=tv_, j=j: e.tensor_copy(out=o[:, 0:3], in_=HALO[:, j, :]), r=[T_HALO[j], ttv], pw=[ttv])
                        a_, ta = acc.get()
                        P.op("dve", lambda e, a_=a_, t=tv_, j=j: e.tensor_scalar(out=a_, in0=t[:, 0:512], scalar1=dcwv(j, 0),
                                                                               scalar2=None, op0=ALU.mult), r=[ttv, T_dcw], w=[ta])
                        for k in (1, 2, 3):
                            P.op("dve", lambda e, a_=a_, t=tv_, j=j, k=k: e.scalar_tensor_tensor(
                                out=a_, in0=t[:, k:k + 512], scalar=dcwv(j, k), in1=a_, op0=ALU.mult, op1=ALU.add),
                                r=[ttv, T_dcw], w=[ta])
                        P.op("pool", lambda e, t=tv_, j=j: e.tensor_copy(out=HALO[:, j, :], in_=t[:, 512:515]), r=[ttv], w=[T_HALO[j]])
                        if j >= 8:
                            P.op("act", lambda e, a_=a_, j=j: e.activation(out=QKV[:, j, :], in_=a_, func=AF.Silu), r=[ta], w=[T_QKV[j]])
                        else:
                            P.op("act", lambda e, a_=a_: e.activation(out=a_, in_=a_, func=AF.Silu), r=[ta], w=[ta])
                            def fin(j=j, a_=a_, ta=ta):
                                sq, tsq = sqb.get()
                                P.op("pool", lambda e, a_=a_, sq=sq: e.tensor_tensor(out=sq, in0=a_, in1=a_, op=ALU.mult), r=[ta], w=[tsq])
                                pss, tpss = pj.get()
                                P.op("pe", lambda e, sq=sq, o=pss: e.matmul(o, lhsT=ones_b[:], rhs=sq, start=True, stop=True),
                                     r=[T_onesb, tsq], w=[tpss])
                                rn, trn = rnb.get()
                                P.op("dve", lambda e, rn=rn, i_=pss: e.tensor_scalar(out=rn, in0=i_, scalar1=EPS, scalar2=None, op0=ALU.add),
                                     r=[tpss], w=[trn])
                                P.op("act", lambda e, rn=rn: e.activation(out=rn, in_=rn, func=AF.Ln), r=[trn], w=[trn])
                                P.op("act", lambda e, rn=rn: e.activation(out=rn, in_=rn, func=AF.Exp, scale=-0.5), r=[trn], w=[trn])
                                qs = (128.0 ** -0.5) if j < 4 else 1.0
                                P.op("dve", lambda e, a_=a_, rn=rn, j=j, qs=qs: e.scalar_tensor_tensor(
                                    out=QKV[:, j, :], in0=a_, scalar=qs, in1=rn, op0=ALU.mult, op1=ALU.mult), r=[ta, trn], w=[T_QKV[j]])
                            pend_l2.append(fin)
                    while pend_l2:
                        pend_l2.pop(0)()
                    if main:
                        for j in range(4):
                            pz, tpz = proj_fm(3072 + j * 128)
                            P.op("act", lambda e, j=j, i_=pz: e.activation(out=ZS[:, j, :], in_=i_, func=AF.Silu), r=[tpz], w=[T_ZS[j]])
                    if stop <= 4:
                        P.dead = True
                    for s in range(4):
                        dn_chunk(s, main)
                    if stop <= 5:
                        P.dead = True
                    if not main:
                        continue
                    for h in range(4):
                        sq, tsq = sqb.get()
                        P.op("act", lambda e, h=h, sq=sq: e.activation(out=sq, in_=OT[:, h, :], func=AF.Square), r=[T_OT[h]], w=[tsq])
                        pss, tpss = pj.get()
                        P.op("pe", lambda e, sq=sq, o=pss: e.matmul(o, lhsT=ones_b[:], rhs=sq, start=True, stop=True),
                             r=[T_onesb, tsq], w=[tpss])
                        rn, trn = rnb.get()
                        rstd_from_ss(pss, tpss, 128, rn, trn)
                        P.op("dve", lambda e, h=h, rn=rn: e.tensor_tensor(out=rn, in0=rn, in1=OT[:, h, :], op=ALU.mult),
                             r=[T_OT[h], trn], w=[trn])
                        P.op("dve", lambda e, h=h, rn=rn: e.scalar_tensor_tensor(out=ycat[:, 4 + h, :], in0=rn, scalar=dnn,
                                                                               in1=ZS[:, h, :], op0=ALU.mult, op1=ALU.mult),
                             r=[trn, T_dnn, T_ZS[h]], w=[T_ycat[4 + h]])
                    for s in range(4):
                        r0 = i * 512 + s * 128
                        xs, txs = pools["xs"].get()
                        P.dma("sp", xs, xm[r0:r0 + 128, :], dX[s % 3], w=[txs])
                        ho, tho = h1t.get()
                        for hf in range(2):
                            pp, tpp = pj.get()
                            for c in range(8):
                                P.op("pe", lambda e, c=c, s=s, hf=hf, pp=pp: e.matmul(
                                    pp, lhsT=ycat[:, c, s * 128:(s + 1) * 128], rhs=wout[:, c, hf * 512:(hf + 1) * 512],
                                    start=(c == 0), stop=(c == 7)), r=[T_wout, T_ycat[c]], **({"w": [tpp]} if c == 0 else {"pw": [tpp]}))
                            P.op("dve", lambda e, hf=hf, pp=pp, ho=ho, xs=xs: e.tensor_tensor(
                                out=ho[:, hf * 512:(hf + 1) * 512], in0=xs[:, hf * 512:(hf + 1) * 512], in1=pp, op=ALU.add),
                                r=[txs, tpp], **({"w": [tho]} if hf == 0 else {"pw": [tho]}))
                        P.dma("sp", h1s[r0:r0 + 128, :], ho, dS1[0], r=[tho], pw=[T_h1s])
          except _Stop:
            pass
          P.dead = False
          P.barrier()

        if "B" in phases:
            with ExitStack() as es:
                pmB = rot(es, "pmB", 1, [128], ps=True)
                colsB, T_colsB = load_cols(es, "colsB", [norm_x.rearrange("(c p) -> c p", p=128),
                                                         norm_mem.rearrange("(c p) -> c p", p=128)], pmB.get())
                g_x, g_mem = colsB[:, 0:8], colsB[:, 8:16]
                wq, T_wq = load_w(es, "wq", w_xq, D)
                wo, T_wo = load_w(es, "wo", w_xo, D)
                pools = {
                    "xs": rot(es, "xsB", 2, [D]), "junk": rot(es, "junkB", 1, [D], BF16), "ss": rot(es, "ssB", 4, [2]),
                    "xn": rot(es, "xnB", 2, [D], BF16), "ptb": rot(es, "ptbB", 1, [8, 128], BF16, ps=True),
                }
                pj = rot(es, "pjB", 2, [512], ps=True)
                pS = rot(es, "pSB", 2, [512], ps=True)
                KT = sbuf(es, "KT", [128, 8, 256], BF16); T_KT = Tr()
                Vm = sbuf(es, "Vm", [128, 2, D], BF16); T_Vm = Tr()
                with ExitStack() as es2:
                    wkv, T_wkv = load_w(es2, "wkv", w_xkv, 2 * D)
                    memT = sbuf(es2, "memT", [128, 8, 256], BF16); T_memT = Tr()
                    for mc in range(2):
                        norm_T(es2, pools, mem[mc * 128:(mc + 1) * 128, :], g_mem, T_colsB, memT, T_memT, mc * 128, dX[mc])
                    for oc in range(8):
                        pp, tpp = pj.get()
                        for c in range(8):
                            P.op("pe", lambda e, c=c, oc=oc, pp=pp: e.matmul(pp[:, 0:256], lhsT=wkv[:, c, oc * 128:(oc + 1) * 128],
                                                                           rhs=memT[:, c, :], start=(c == 0), stop=(c == 7)),
                                 r=[T_wkv, T_memT], **({"w": [tpp]} if c == 0 else {"pw": [tpp]}))
                        P.op("act", lambda e, oc=oc, pp=pp: e.activation(out=KT[:, oc, :], in_=pp[:, 0:256], func=AF.Copy, scale=0.0625),
                             r=[tpp], pw=[T_KT])
                    for mc in range(2):
                        for hf in range(2):
                            pp, tpp = pj.get()
                            for c in range(8):
                                P.op("pe", lambda e, c=c, mc=mc, hf=hf, pp=pp: e.matmul(
                                    pp, lhsT=memT[:, c, mc * 128:(mc + 1) * 128], rhs=wkv[:, c, D + hf * 512:D + (hf + 1) * 512],
                                    start=(c == 0), stop=(c == 7)), r=[T_wkv, T_memT], **({"w": [tpp]} if c == 0 else {"pw": [tpp]}))
                            P.op("act", lambda e, mc=mc, hf=hf, pp=pp: e.copy(out=Vm[:, mc, hf * 512:(hf + 1) * 512], in_=pp),
                                 r=[tpp], pw=[T_Vm])
                    P.barrier()
                xnT = sbuf(es, "xnTB", [128, 8, 512], BF16); T_xnT = Tr()
                qT = sbuf(es, "qTB", [128, 8, 512], BF16); T_qT = [Tr() for _ in range(8)]
                PTm = sbuf(es, "PTm", [128, 4, 2, 512], BF16); T_PTm = [Tr() for _ in range(4)]
                OTb = sbuf(es, "OTb", [128, 8, 512], BF16); T_OTb = [Tr() for _ in range(8)]
                smB = rot(es, "smB", 8, [4])
                Ef = rot(es, "Ef", 2, [256])
                Pn = rot(es, "Pn", 2, [256], BF16)
                h2t = rot(es, "h2t", 2, [D])
                pend_b = []
                for i in range(NT // 512):
                    for s in range(4):
                        r0 = i * 512 + s * 128
                        norm_T(es, pools, h1s[r0:r0 + 128, :], g_x, T_colsB, xnT, T_xnT, s * 128, dX[s % 3])
                    for oc in range(8):
                        pp, tpp = pj.get()
                        for c in range(8):
                            P.op("pe", lambda e, c=c, oc=oc, pp=pp: e.matmul(pp, lhsT=wq[:, c, oc * 128:(oc + 1) * 128], rhs=xnT[:, c, :],
                                                                           start=(c == 0), stop=(c == 7)),
                                 r=[T_wq, T_xnT], **({"w": [tpp]} if c == 0 else {"pw": [tpp]}))
                        P.op("act", lambda e, oc=oc, pp=pp: e.copy(out=qT[:, oc, :], in_=pp), r=[tpp], w=[T_qT[oc]])
                    for s in range(4):
                        sl = slice(s * 128, (s + 1) * 128)
                        for h in range(4):
                            ps_, tps = pS.get()
                            for c2 in range(2):
                                P.op("pe", lambda e, c2=c2, h=h, ps_=ps_, sl=sl: e.matmul(
                                    ps_[:, 0:256], lhsT=qT[:, 2 * h + c2, sl], rhs=KT[:, 2 * h + c2, :], start=(c2 == 0), stop=(c2 == 1)),
                                    r=[T_qT[2 * h + c2], T_KT], **({"w": [tps]} if c2 == 0 else {"pw": [tps]}))
                            while pend_b:
                                pend_b.pop(0)()
                            m_, tm = smB.get()
                            P.op("dve", lambda e, m_=m_, ps_=ps_: e.reduce_max(out=m_[:, 0:1], in_=ps_[:, 0:256], axis=AX.X), r=[tps], w=[tm])
                            P.op("dve", lambda e, m_=m_: e.tensor_scalar(out=m_[:, 1:2], in0=m_[:, 0:1], scalar1=-1.0, scalar2=None, op0=ALU.mult),
                                 r=[tm], w=[tm])
                            ef, tef = Ef.get()
                            P.op("act", lambda e, ef=ef, ps_=ps_, m_=m_: e.activation(out=ef, in_=ps_[:, 0:256], func=AF.Exp, bias=m_[:, 1:2],
                                                                                   accum_out=m_[:, 2:3]), xr=[tps], r=[tm], w=[tef, tm])
                            P.op("dve", lambda e, m_=m_: e.reciprocal(out=m_[:, 3:4], in_=m_[:, 2:3]), r=[tm], w=[tm])
                            pn, tpn = Pn.get()
                            P.op("dve", lambda e, pn=pn, ef=ef, m_=m_: e.tensor_scalar(out=pn, in0=ef, scalar1=m_[:, 3:4], scalar2=None, op0=ALU.mult),
                                 r=[tef, tm], w=[tpn])
                            def fin_t(h=h, pn=pn, tpn=tpn, sl=sl):
                                pt, tpt = pools["ptb"].get()
                                for mc in range(2):
                                    P.op("pe", lambda e, mc=mc, pt=pt, pn=pn: e.transpose(out=pt[:, mc, :], in_=pn[:, mc * 128:(mc + 1) * 128], identity=idb[:]),
                                         r=[tpn, T_idb], **({"w": [tpt]} if mc == 0 else {"pw": [tpt]}))
                                P.op("dve", lambda e, h=h, pt=pt, sl=sl: e.tensor_copy(out=PTm[:, h, :, sl], in_=pt[:, 0:2, :]), r=[tpt], pw=[T_PTm[h]])
                            pend_b.append(fin_t)
                    while pend_b:
                        pend_b.pop(0)()
                    for h in range(4):
                        for dvc in range(2):
                            pp, tpp = pj.get()
                            for mc in range(2):
                                P.op("pe", lambda e, mc=mc, h=h, dvc=dvc, pp=pp: e.matmul(
                                    pp, lhsT=Vm[:, mc, h * 256 + dvc * 128:h * 256 + (dvc + 1) * 128], rhs=PTm[:, h, mc, :],
                                    start=(mc == 0), stop=(mc == 1)), r=[T_Vm, T_PTm[h]], **({"w": [tpp]} if mc == 0 else {"pw": [tpp]}))
                            P.op("act", lambda e, h=h, dvc=dvc, pp=pp: e.copy(out=OTb[:, 2 * h + dvc, :], in_=pp), r=[tpp], w=[T_OTb[2 * h + dvc]])
                    for s in range(4):
                        r0 = i * 512 + s * 128
                        xs, txs = pools["xs"].get()
                        P.dma("sp", xs, h1s[r0:r0 + 128, :], dX[s % 3], w=[txs])
                        ho, tho = h2t.get()
                        for hf in range(2):
                            pp, tpp = pj.get()
                            for c in range(8):
                                P.op("pe", lambda e, c=c, s=s, hf=hf, pp=pp: e.matmul(
                                    pp, lhsT=OTb[:, c, s * 128:(s + 1) * 128], rhs=wo[:, c, hf * 512:(hf + 1) * 512],
                                    start=(c == 0), stop=(c == 7)), r=[T_wo, T_OTb[c]], **({"w": [tpp]} if c == 0 else {"pw": [tpp]}))
                            P.op("dve", lambda e, hf=hf, pp=pp, ho=ho, xs=xs: e.tensor_tensor(
                                out=ho[:, hf * 512:(hf + 1) * 512], in0=xs[:, hf * 512:(hf + 1) * 512], in1=pp, op=ALU.add),
                                r=[txs, tpp], **({"w": [tho]} if hf == 0 else {"pw": [tho]}))
                        P.dma("sp", h2s[r0:r0 + 128, :], ho, dS2[(h2t.i - 1) % 2], r=[tho], pw=[T_h2s])
            P.barrier()

        if "C" in phases:
            with ExitStack() as es:
                NB = 8
                pmisT = psum(es, "pmisC", [128, 512]); T_pmis = Tr()
                pmis = Rot([(pmisT[:, 0:128], T_pmis)])
                pfr = Rot([(pmisT[:, 0:512], T_pmis)])
                ptbC = rot(es, "ptbC", 1, [8, 128], BF16, ps=True)
                py = rot(es, "pyC", 4, [512], ps=True)
                pH = rot(es, "pHC", 2, [512], ps=True)
                colsC, T_colsC = load_cols(es, "colsC", [norm_ffn.rearrange("(c p) -> c p", p=128)], pmis.get())
                g_ffn = colsC[:, 0:8]
                gfin = sbuf(es, "gfin", [128, D]); T_gfin = Tr()
                P.dma("sp", gfin[:], norm_final.partition_broadcast(128), P.dsem(), w=[T_gfin])
                wpq, T_wpq = load_w(es, "wpq", w_pq, D)
                pools = {
                    "xs": rot(es, "xsC", 2, [D]), "junk": rot(es, "junkC", 1, [D], BF16), "ss": rot(es, "ssC", 4, [2]),
                    "xn": rot(es, "xnC", 2, [D], BF16), "ptb": ptbC,
                }
                keysBD = sbuf(es, "keysBD", [128, 8, 256], BF16); T_kbd = Tr()
                P.op("pool", lambda e: e.memset(keysBD[:], 0.0), w=[T_kbd])
                kraw = rot(es, "kraw", 2, [128])
                dK = [P.dsem(), P.dsem()]
                for h in range(8):
                    kr, tkr = kraw.get()
                    for p_ in range(2):
                        P.dma("sp", kr[:, p_ * 64:(p_ + 1) * 64], peer_keys[h, p_], dK[h % 2], **({"w": [tkr]} if p_ == 0 else {"pw": [tkr]}))
                    pk, tpk = pmis.get()
                    P.op("pe", lambda e, kr=kr, pk=pk: e.transpose(out=pk, in_=kr, identity=idt[:]), r=[tkr, T_idt], w=[tpk])
                    P.op("act", lambda e, h=h, pk=pk: e.copy(out=keysBD[0:64, h, 0:128], in_=pk[0:64, :]), r=[tpk, T_kbd], pw=[T_kbd])
                    P.op("act", lambda e, h=h, pk=pk: e.copy(out=keysBD[64:128, h, 128:256], in_=pk[64:128, :]), r=[tpk], pw=[T_kbd])
                with ExitStack() as es2:
                    NS = 4
                    usp = rot(es2, "usp", NS, [D]); vst = rot(es2, "vst", NS, [D])
                    ub = rot(es2, "ub", NS, [D], BF16)
                    utt = rot(es2, "utt", NS, [8, 128], BF16)
                    vb_ = rot(es2, "vbp", NS, [D], BF16)
                    dU = [P.dsem() for _ in range(NS)]; dV = [P.dsem() for _ in range(NS)]
                    dUo = [P.dsem() for _ in range(NS)]; dVo = [P.dsem() for _ in range(NS)]
                    loaded = {}

                    def pre_load(i1):
                        us, tus = usp.get()
                        P.dma("sp", us, expert_u[i1 * 128:(i1 + 1) * 128, :], dU[i1 % NS], w=[tus])
                        vs, tvs = vst.get()
                        P.dma("sp", vs, expert_v[i1 * 128:(i1 + 1) * 128, :], dV[i1 % NS], w=[tvs])
                        loaded[i1] = (us, tus, vs, tvs)

                    def pre_block(i1):
                        us, tus, vs, tvs = loaded.pop(i1)
                        u16, tu16 = ub.get()
                        P.op("act", lambda e: e.copy(out=u16, in_=us), r=[tus], w=[tu16])
                        pt, tpt = ptbC.get()
                        for c in range(8):
                            P.op("pe", lambda e, c=c: e.transpose(out=pt[:, c, :], in_=u16[:, c * 128:(c + 1) * 128], identity=idb[:]),
                                 r=[tu16, T_idb], **({"w": [tpt]} if c == 0 else {"pw": [tpt]}))
                        ut_, tut = utt.get()
                        P.op("dve", lambda e: e.tensor_copy(out=ut_, in_=pt), r=[tpt], w=[tut])
                        v16, tv16 = vb_.get()
                        P.op("dve", lambda e: e.tensor_copy(out=v16, in_=vs), r=[tvs], w=[tv16])
                        P.dma("sp", uts[i1].rearrange("p (c e) -> p c e", c=8), ut_, dUo[i1 % NS], r=[tut], pw=[T_uts])
                        P.dma("sp", vbs[i1], v16, dVo[i1 % NS], r=[tv16], pw=[T_vbs])

                    for i1 in range(2):
                        pre_load(i1)
                    for i1 in range(128):
                        if i1 + 2 < 128:
                            pre_load(i1 + 2)
                        pre_block(i1)
                    P.barrier()
                Gt = sbuf(es, "Gt", [128, 128, 256], BF16); T_G = Tr()
                xnTs = [sbuf(es, f"xnTC{k}", [128, 8, 256], BF16) for k in range(2)]; T_xnTs = [Tr(), Tr()]
                qT = sbuf(es, "qTC", [128, 8, 256], BF16); T_qT = [Tr() for _ in range(8)]
                sc = sbuf(es, "scC", [128, 8, 256]); T_sc = [Tr() for _ in range(8)]
                a1v = sbuf(es, "a1v", [128, 8, 2, 16]); T_a1v = Tr()
                a1i = sbuf(es, "a1i", [128, 8, 2, 16], U32); T_a1i = Tr()
                a1f = sbuf(es, "a1f", [128, 8, 2, 16]); T_a1f = Tr()
                wk128 = rot(es, "wk128", 2, [128])
                cand = sbuf(es, "cand", [128, 8, 256]); T_cand = Tr()
                cwk = rot(es, "cwk", 2, [256])
                oh = sbuf(es, "ohC", [128, 8, 256]); T_oh = Tr()
                bv = sbuf(es, "bv", [128, 8, 16]); T_bv = Tr()
                bp = sbuf(es, "bp", [128, 8, 16], U32); T_bp = Tr()
                au = sbuf(es, "au", [128, 8, 16], U32); T_au = Tr()
                posf = sbuf(es, "posf", [128, 8, 16]); T_posf = Tr()
                af = sbuf(es, "af", [128, 8, 16]); T_af = Tr()
                bf_ = sbuf(es, "bf_", [128, 8, 16]); T_bf = Tr()
                ge = sbuf(es, "ge", [128, 8, 16]); T_ge = Tr()
                gs = sbuf(es, "gs", [128, 8, 2]); T_gs = Tr()
                tk3 = rot(es, "tk3", 3, [128])
                kT3 = sbuf(es, "kT3", [128, 3, 256]); T_kT3 = [Tr() for _ in range(3)]
                P1p = rot(es, "P1p", 2, [NB, 128], BF16)
                P2p = rot(es, "P2p", 2, [NB, 128], BF16)
                P2gp = rot(es, "P2gp", 2, [NB, 128], BF16)
                utp = rot(es, "utp", 2, [2, D], BF16)
                vtp = rot(es, "vtp", 2, [2, D], BF16)
                dUT = [P.dsem(), P.dsem()]; dVT = [P.dsem(), P.dsem()]
                Hgp = rot(es, "Hgp", 3, [256], BF16)
                Ap = rot(es, "Ap", 3, [256], BF16)
                fin = rot(es, "fin", 2, [D])
                print("SBUF remaining after phase C alloc:", nc.sbuf_bytes_remaining)
                cand4 = cand[:].rearrange("p h (a b) -> p h a b", a=16)
                oh4 = oh[:].rearrange("p h (a b) -> p h a b", a=16)
                iota16 = iota_f[:, 0:16].unsqueeze(1).unsqueeze(1).broadcast_to([128, 8, 16, 16])

                def front(i):
                    for s in range(2):
                        r0 = i * 256 + s * 128
                        norm_T(es, pools, h2s[r0:r0 + 128, :], g_ffn, T_colsC, xnTs[i % 2], T_xnTs[i % 2], s * 128, dX[s % 3])
                        yield
                    for oc in range(8):
                        yield
                        pp, tpp = pfr.get()
                        for c in range(8):
                            P.op("pe", lambda e, c=c, oc=oc, pp=pp: e.matmul(pp[:, 0:256], lhsT=wpq[:, c, oc * 128:(oc + 1) * 128], rhs=xnTs[i % 2][:, c, :],
                                                                           start=(c == 0), stop=(c == 7)),
                                 r=[T_wpq, T_xnTs[i % 2]], **({"w": [tpp]} if c == 0 else {"pw": [tpp]}))
                        P.op("act", lambda e, oc=oc, pp=pp: e.copy(out=qT[:, oc, :], in_=pp[:, 0:256]), r=[tpp], w=[T_qT[oc]])
                    for s in range(2):
                        sl = slice(s * 128, (s + 1) * 128)
                        for h in range(8):
                            yield
                            pp, tpp = pfr.get()
                            P.op("pe", lambda e, h=h, pp=pp, sl=sl: e.matmul(pp[:, 0:256], lhsT=qT[:, h, sl], rhs=keysBD[:, h, :], start=True, stop=True),
                                 r=[T_qT[h], T_kbd], w=[tpp])
                            P.op("act", lambda e, h=h, pp=pp: e.copy(out=sc[:, h, :], in_=pp[:, 0:256]), r=[tpp], w=[T_sc[h]])
                        for h in range(8):
                            for p_ in range(2):
                                yield
                                src = sc[:, h, p_ * 128:(p_ + 1) * 128]
                                P.op("dve", lambda e, h=h, p_=p_, src=src: e.max(out=a1v[:, h, p_, 0:8], in_=src), r=[T_sc[h]], pw=[T_a1v])
                                P.op("dve", lambda e, h=h, p_=p_, src=src: e.max_index(out=a1i[:, h, p_, 0:8], in_max=a1v[:, h, p_, 0:8], in_values=src),
                                     r=[T_sc[h], T_a1v], pw=[T_a1i])
                                wk, twk = wk128.get()
                                P.op("dve", lambda e, h=h, p_=p_, src=src, wk=wk: e.match_replace(out=wk, in_to_replace=a1v[:, h, p_, 0:8], in_values=src, imm_value=NEG),
                                     r=[T_sc[h], T_a1v], w=[twk])
                                P.op("dve", lambda e, h=h, p_=p_, wk=wk: e.max(out=a1v[:, h, p_, 8:16], in_=wk), r=[twk], pw=[T_a1v])
                                P.op("dve", lambda e, h=h, p_=p_, wk=wk: e.max_index(out=a1i[:, h, p_, 8:16], in_max=a1v[:, h, p_, 8:16], in_values=wk),
                                     r=[twk, T_a1v], pw=[T_a1i])
                        P.op("dve", lambda e: e.tensor_tensor(out=cand4, in0=a1v[:, :, 0, :].unsqueeze(3).broadcast_to([128, 8, 16, 16]),
                                                              in1=a1v[:, :, 1, :].unsqueeze(2).broadcast_to([128, 8, 16, 16]), op=ALU.add),
                             r=[T_a1v], w=[T_cand])
                        for h in range(8):
                            yield
                            csrc = cand[:, h, :]
                            P.op("dve", lambda e, h=h, csrc=csrc: e.max(out=bv[:, h, 0:8], in_=csrc), r=[T_cand], pw=[T_bv])
                            P.op("dve", lambda e, h=h, csrc=csrc: e.max_index(out=bp[:, h, 0:8], in_max=bv[:, h, 0:8], in_values=csrc),
                                 r=[T_cand, T_bv], pw=[T_bp])
                            ck, tck = cwk.get()
                            P.op("dve", lambda e, h=h, csrc=csrc, ck=ck: e.match_replace(out=ck, in_to_replace=bv[:, h, 0:8], in_values=csrc, imm_value=NEG),
                                 r=[T_cand, T_bv], w=[tck])
                            P.op("dve", lambda e, h=h, ck=ck: e.max(out=bv[:, h, 8:16], in_=ck), r=[tck], pw=[T_bv])
                            P.op("dve", lambda e, h=h, ck=ck: e.max_index(out=bp[:, h, 8:16], in_max=bv[:, h, 8:16], in_values=ck),
                                 r=[tck, T_bv], pw=[T_bp])
                        yield
                        P.op("dve", lambda e: e.tensor_single_scalar(out=au[:], in_=bp[:], scalar=4, op=ALU.logical_shift_right), r=[T_bp], w=[T_au])
                        P.op("dve", lambda e: e.tensor_copy(out=posf[:], in_=bp[:]), r=[T_bp], w=[T_posf])
                        P.op("dve", lambda e: e.tensor_copy(out=af[:], in_=au[:]), r=[T_au], w=[T_af])
                        P.op("dve", lambda e: e.scalar_tensor_tensor(out=bf_[:], in0=af[:], scalar=-16.0, in1=posf[:], op0=ALU.mult, op1=ALU.add),
                             r=[T_af, T_posf], w=[T_bf])
                        P.op("dve", lambda e: e.tensor_copy(out=a1f[:], in_=a1i[:]), r=[T_a1i], w=[T_a1f])
                        idx_t = []
                        for which, (sel, tsel) in enumerate(((af, T_af), (bf_, T_bf))):
                            P.op("dve", lambda e, sel=sel: e.tensor_tensor(out=oh4, in0=iota16, in1=sel[:].unsqueeze(3).broadcast_to([128, 8, 16, 16]),
                                                                          op=ALU.is_equal), r=[T_iota, tsel], w=[T_oh])
                            P.op("dve", lambda e, which=which: e.tensor_tensor(out=oh4, in0=oh4,
                                                                              in1=a1f[:, :, which, :].unsqueeze(2).broadcast_to([128, 8, 16, 16]), op=ALU.mult),
                                 r=[T_a1f], w=[T_oh])
                            it, tit = tk3.get()
                            P.op("dve", lambda e, it=it: e.tensor_reduce(out=it.rearrange("p (h j) -> p h j", h=8), in_=oh4, axis=AX.X, op=ALU.add),
                                 r=[T_oh], w=[tit])
                            idx_t.append((it, tit))
                        yield
                        P.op("dve", lambda e: e.tensor_tensor(out=ge[:], in0=bv[:], in1=bv[:, :, 0:1].broadcast_to([128, 8, 16]), op=ALU.subtract),
                             r=[T_bv], w=[T_ge])
                        P.op("act", lambda e: e.activation(out=ge[:], in_=ge[:], func=AF.Exp), r=[T_ge], w=[T_ge])
                        P.op("dve", lambda e: e.tensor_reduce(out=gs[:, :, 0], in_=ge[:], axis=AX.X, op=ALU.add), r=[T_ge], w=[T_gs])
                        P.op("dve", lambda e: e.reciprocal(out=gs[:, :, 1], in_=gs[:, :, 0]), r=[T_gs], w=[T_gs])
                        gt_, tgt = tk3.get()
                        P.op("dve", lambda e, gt_=gt_: e.tensor_tensor(out=gt_.rearrange("p (h j) -> p h j", h=8), in0=ge[:],
                                                                      in1=gs[:, :, 1:2].broadcast_to([128, 8, 16]), op=ALU.mult),
                             r=[T_ge, T_gs], w=[tgt])
                        idx_t.append((gt_, tgt))
                        yield
                        for q, (it, tit) in enumerate(idx_t):
                            pk, tpk = pmis.get()
                            P.op("pe", lambda e, it=it, pk=pk: e.transpose(out=pk, in_=it, identity=idt[:]), r=[tit, T_idt], w=[tpk])
                            P.op("act", lambda e, q=q, pk=pk, sl=sl: e.copy(out=kT3[:, q, sl], in_=pk), r=[tpk], pw=[T_kT3[q]])
                def gbuild(i):
                    for b0 in range(0, 256 if os.environ.get("SKIP_GBUILD") != "1" else NB, NB):
                        p1, tp1 = P1p.get(); p2g, tp2g = P2gp.get()
                        for j in range(NB):
                            t_ = b0 + j
                            P.op("dve", lambda e, p1=p1, j=j, t_=t_: e.tensor_scalar(
                                out=p1[:, j, :], in0=iota_b[:], scalar1=kT3[:, 0, t_:t_ + 1], scalar2=None, op0=ALU.is_equal),
                                r=[T_iota, T_kT3[0]], **({"w": [tp1]} if j == 0 else {"pw": [tp1]}))
                            P.op("dve", lambda e, p2g=p2g, j=j, t_=t_: e.tensor_scalar(
                                out=p2g[:, j, :], in0=iota_b[:], scalar1=kT3[:, 1, t_:t_ + 1], scalar2=kT3[:, 2, t_:t_ + 1],
                                op0=ALU.is_equal, op1=ALU.mult),
                                r=[T_iota, T_kT3[1], T_kT3[2]], **({"w": [tp2g]} if j == 0 else {"pw": [tp2g]}))
                        for q0 in range(0, NB, 4):
                            pg_, tpg_ = pH.get()
                            for q in range(4):
                                P.op("pe", lambda e, q=q, q0=q0, pg_=pg_, p1=p1, p2g=p2g: e.matmul(
                                    pg_[:, q * 128:(q + 1) * 128], lhsT=p2g[:, q0 + q, :], rhs=p1[:, q0 + q, :], start=True, stop=True),
                                    r=[tp1, tp2g], **({"w": [tpg_]} if q == 0 else {"pw": [tpg_]}))
                            P.op("act", lambda e, b0=b0, q0=q0, pg_=pg_: e.copy(out=Gt[:, :, b0 + q0:b0 + q0 + 4],
                                                                              in_=pg_.rearrange("p (t i) -> p i t", t=4)), r=[tpg_], pw=[T_G])
                def sweep_final(i, gen):
                    Y = [py.get() for _ in range(4)]

                    def emit_Y(i1, a_, ta, vt_, tvt, k2):
                        for ts in range(2):
                            for hf in range(2):
                                yb, tyb = Y[ts * 2 + hf]
                                P.op("pe", lambda e, ts=ts, hf=hf, yb=yb, a_=a_, vt_=vt_, k2=k2, i1=i1: e.matmul(
                                    yb, lhsT=a_[:, ts * 128:(ts + 1) * 128], rhs=vt_[:, k2, hf * 512:(hf + 1) * 512],
                                    start=(i1 == 0), stop=(i1 == 127)), r=[ta, tvt], **({"w": [tyb]} if i1 == 0 else {"pw": [tyb]}))
                    pend = None
                    for i1 in range(128):
                        if gen is not None:
                            next(gen, None)
                        if i1 % 2 == 0 and not (os.environ.get("SKIP_SWEEP_DMA") == "1" and i1 >= 4):
                            ut_, tut = utp.get(); vt_, tvt = vtp.get()
                            P.dma("sp", ut_, uts[i1:i1 + 2].rearrange("n p f -> p n f"), dUT[(utp.i - 1) % 2], r=[T_uts], w=[tut])
                            P.dma("sp", vt_, vbs[i1:i1 + 2].rearrange("n p f -> p n f"), dVT[(vtp.i - 1) % 2], r=[T_vbs], w=[tvt])
                        k2 = i1 % 2
                        ph_, tph = pH.get()
                        for c in range(8):
                            P.op("pe", lambda e, c=c, k2=k2, ph_=ph_, ut_=ut_: e.matmul(ph_[:, 0:256], lhsT=ut_[:, k2, c * 128:(c + 1) * 128], rhs=xnTs[i % 2][:, c, :],
                                                                                     start=(c == 0), stop=(c == 7)),
                                 r=[tut, T_xnTs[i % 2]], **({"w": [tph]} if c == 0 else {"pw": [tph]}))
                        hg, thg = Hgp.get()
                        P.op("act", lambda e, hg=hg, ph_=ph_: e.activation(out=hg, in_=ph_[:, 0:256], func=AF.Gelu), r=[tph], w=[thg])
                        a_, ta = Ap.get()
                        P.op("dve", lambda e, a_=a_, hg=hg, i1=i1: e.tensor_tensor(out=a_, in0=hg, in1=Gt[:, i1, :], op=ALU.mult), r=[thg, T_G], w=[ta])
                        if pend is not None:
                            emit_Y(*pend)
                        pend = (i1, a_, ta, vt_, tvt, k2)
                    emit_Y(*pend)
                    if gen is not None:
                        for _ in gen:
                            pass
                    for ts in range(2):
                        r0 = i * 256 + ts * 128
                        xs, txs = pools["xs"].get()
                        P.dma("sp", xs, h2s[r0:r0 + 128, :], dX[ts % 3], w=[txs])
                        fo, tfo = fin.get()
                        for hf in range(2):
                            yb, tyb = Y[ts * 2 + hf]
                            P.op("dve", lambda e, hf=hf, yb=yb, fo=fo, xs=xs: e.tensor_tensor(
                                out=fo[:, hf * 512:(hf + 1) * 512], in0=xs[:, hf * 512:(hf + 1) * 512], in1=yb, op=ALU.add),
                                r=[txs, tyb], **({"w": [tfo]} if hf == 0 else {"pw": [tfo]}))
                        junk, tj = pools["junk"].get()
                        ss, tss = pools["ss"].get()
                        P.op("act", lambda e, junk=junk, fo=fo, ss=ss: e.activation(out=junk, in_=fo, func=AF.Square, accum_out=ss[:, 0:1]),
                             r=[tfo], w=[tj, tss])
                        rstd_from_ss(ss[:, 0:1], tss, D, ss[:, 1:2], tss)
                        P.op("dve", lambda e, fo=fo, ss=ss: e.scalar_tensor_tensor(out=fo, in0=fo, scalar=ss[:, 1:2], in1=gfin[:], op0=ALU.mult, op1=ALU.mult),
                             r=[tss, T_gfin], w=[tfo])
                        P.dma("sp", out[r0:r0 + 128, :], fo, dOut[(fin.i - 1) % 2], r=[tfo])
                ntile = NT // 256
                for _ in front(0):
                    pass
                for i in range(ntile):
                    gbuild(i)
                    sweep_final(i, front(i + 1) if i + 1 < ntile else None)
            P.barrier()

        P.barrier()
        P.emit()
    return nc


def make_in_maps(inputs, NT, NPF, n_cores, seq):
    x = np.asarray(inputs["x"], dtype=np.float32)
    memf = np.asarray(inputs["mem"], dtype=np.float32)
    shared = {}
    for k in ("norm_mix", "w_in", "conv_w", "dn_conv_w", "dn_a_log", "dn_dt_bias", "dn_norm", "w_out", "norm_x",
              "norm_mem", "w_xq", "w_xkv", "w_xo", "norm_ffn", "w_pq", "peer_keys", "expert_u", "expert_v"):
        shared[k] = np.ascontiguousarray(np.asarray(inputs[k], dtype=np.float32)[0])
    shared["norm_final"] = np.ascontiguousarray(np.asarray(inputs["norm_final"], dtype=np.float32))
    maps = []
    for c in range(n_cores):
        b, s = c // 2, c % 2
        m = dict(shared)
        m["xm"] = np.ascontiguousarray(x[b, s * NT:(s + 1) * NT])
        m["xp"] = np.ascontiguousarray(x[b, 0:NPF]) if s == 1 else np.zeros((NPF, D), np.float32)
        m["mem"] = np.ascontiguousarray(memf[b])
        maps.append(m)
    return maps


def kernel(**inputs):
    NT = NPF = 4096
    nc = build(NT, NPF)
    maps = make_in_maps(inputs, NT, NPF, 8, 8192)
    res = run_bass_kernel_spmd(nc, maps, core_ids=list(range(8)))
    outp = np.zeros((4, 8192, D), np.float32)
    for c in range(8):
        b, s = c // 2, c % 2
        outp[b, s * NT:(s + 1) * NT] = res.results[c]["out"]
    return outp
```

```python
import types
import numpy as np
import concourse.bass as bass
import concourse.mybir as mybir
from concourse.bass_utils import run_bass_kernel_spmd
from contextlib import ExitStack

F32 = mybir.dt.float32
BF16 = mybir.dt.bfloat16
U32 = mybir.dt.uint32
AF = mybir.ActivationFunctionType
ALU = mybir.AluOpType
AX = mybir.AxisListType

ENGS = ("pe", "dve", "act", "pool", "sp")
EPS = 1e-6
D = 1024
NEG = -1.0e30


def _freeze(fn):
    if fn is None or fn.__closure__ is None:
        return fn
    cells = tuple(types.CellType(c.cell_contents) for c in fn.__closure__)
    return types.FunctionType(fn.__code__, fn.__globals__, fn.__name__, fn.__defaults__, cells)


class Tr:
    __slots__ = ("name", "w", "r")

    def __init__(self, name=""):
        self.name = name
        self.w = {}
        self.r = {}


class Prog:
    def __init__(self, nc, es):
        self.nc = nc
        self.es = es
        self.ins = {e: [] for e in ENGS}
        self.known = {e: {} for e in ENGS}
        self.targets = {e: set() for e in ENGS}
        self.esem = {e: es.enter_context(nc.semaphore("sem_" + e)) for e in ENGS if e != "sp"}
        self.dsems = []
        self.dead = False

    def dsem(self):
        n = len(self.dsems) + 1
        s = self.es.enter_context(self.nc.semaphore(f"dsem{n}"))
        d = {"sem": s, "val": 0, "key": ("d", n)}
        self.dsems.append(d)
        return d

    def _need(self, eng, waits, key, val):
        if key == ("e", "pe") and eng == "pe":
            return
        if self.known[eng].get(key, 0) >= val:
            return
        if waits.get(key, 0) < val:
            waits[key] = val

    def _deps(self, eng, r, w, pw):
        waits = {}
        for t in r:
            for k, v in t.w.items():
                self._need(eng, waits, k, v)
        for t in w:
            for k, v in t.r.items():
                self._need(eng, waits, k, v)
            for k, v in t.w.items():
                self._need(eng, waits, k, v)
        for t in pw:
            for k, v in t.r.items():
                self._need(eng, waits, k, v)
        for k, v in waits.items():
            self.known[eng][k] = v
            if k[0] == "e":
                self.targets[k[1]].add(v)
        return waits

    def _commit(self, key, val, r, w, pw):
        for t in r:
            if t.r.get(key, 0) < val:
                t.r[key] = val
        for t in w:
            t.w = {key: val}
            t.r = {}
        for t in pw:
            if t.r:
                t.r = {}
                t.w = {key: val}
            else:
                t.w[key] = val

    def op(self, eng, fn, r=(), w=(), pw=(), xr=()):
        if self.dead:
            return
        waits = self._deps(eng, r, list(w) + list(xr), pw)
        seq = len(self.ins[eng]) + 1
        self.ins[eng].append([_freeze(fn), waits, None])
        self._commit(("e", eng), seq, list(r) + list(xr), w, pw)

    def dma(self, eng, out, in_, ds, r=(), w=(), pw=(), **kw):
        if self.dead:
            return
        waits = self._deps(eng, r, w, pw)
        ds["val"] += 16

        def fn(e):
            return e.dma_start(out=out, in_=in_, **kw)
        self.ins[eng].append([fn, waits, (ds["sem"], 16)])
        self._commit(ds["key"], ds["val"], r, w, pw)

    def barrier(self):
        last = {}
        for e in ENGS:
            if e == "sp":
                continue
            for idx in range(len(self.ins[e]) - 1, -1, -1):
                if self.ins[e][idx][0] is not None:
                    last[("e", e)] = idx + 1
                    break
        for d in self.dsems:
            if d["val"]:
                last[d["key"]] = d["val"]
        for e in ENGS:
            waits = {}
            for k, v in last.items():
                if k == ("e", e):
                    continue
                if self.known[e].get(k, 0) < v:
                    waits[k] = v
                    self.known[e][k] = v
                    if k[0] == "e":
                        self.targets[k[1]].add(v)
            if waits:
                self.ins[e].append([None, waits, None])

    def emit(self):
        nc = self.nc
        rank = {e: {s: i + 1 for i, s in enumerate(sorted(self.targets[e]))}
                for e in ENGS if e != "sp"}
        dmap = {d["key"]: d["sem"] for d in self.dsems}

        def run(name, e):
            rk = rank.get(name, {})
            for idx, (fn, waits, inc) in enumerate(self.ins[name]):
                for k, v in waits.items():
                    if k[0] == "e":
                        e.wait_ge(self.esem[k[1]], rank[k[1]][v])
                    else:
                        e.wait_ge(dmap[k], v)
                if fn is None:
                    continue
                i = fn(e)
                if inc is not None:
                    i.then_inc(inc[0], inc[1])
                elif (idx + 1) in rk:
                    i.then_inc(self.esem[name], 1)

        with nc.Block() as block:
            @block.tensor
            def _(e):
                run("pe", e)

            @block.vector
            def _(e):
                run("dve", e)

            @block.scalar
            def _(e):
                run("act", e)

            @block.gpsimd
            def _(e):
                run("pool", e)

            @block.sync
            def _(e):
                run("sp", e)


class Rot:
    def __init__(self, slots):
        self.slots = slots
        self.i = 0

    def get(self):
        s = self.slots[self.i % len(self.slots)]
        self.i += 1
        return s


class _Stop(Exception):
    pass


import os
DNSTOP = int(os.environ.get("DNSTOP", "99"))


def build(NT, NPF, phases="ABC", dbg=False, stop=99, ext_scratch=False):
    nc = bass.Bass("TRN2", target_bir_lowering=False)
    dt_in = lambda n, s: nc.dram_tensor(n, s, F32, kind="ExternalInput").ap()
    xm = dt_in("xm", [NT, D])
    xp = dt_in("xp", [NPF, D])
    mem = dt_in("mem", [256, D])
    norm_mix = dt_in("norm_mix", [D])
    w_in = dt_in("w_in", [D, 3592])
    conv_w = dt_in("conv_w", [3, 512])
    dn_conv_w = dt_in("dn_conv_w", [4, 1536])
    dn_a_log = dt_in("dn_a_log", [4])
    dn_dt_bias = dt_in("dn_dt_bias", [4])
    dn_norm = dt_in("dn_norm", [128])
    w_out = dt_in("w_out", [D, D])
    norm_x = dt_in("norm_x", [D])
    norm_mem = dt_in("norm_mem", [D])
    w_xq = dt_in("w_xq", [D, D])
    w_xkv = dt_in("w_xkv", [D, 2 * D])
    w_xo = dt_in("w_xo", [D, D])
    norm_ffn = dt_in("norm_ffn", [D])
    w_pq = dt_in("w_pq", [D, D])
    peer_keys = dt_in("peer_keys", [8, 2, 128, 64])
    expert_u = dt_in("expert_u", [16384, D])
    expert_v = dt_in("expert_v", [16384, D])
    norm_final = dt_in("norm_final", [D])
    out = nc.dram_tensor("out", [NT, D], F32, kind="ExternalOutput").ap()
    dbgk = nc.dram_tensor("dbgk", [NT // 256, 128, 3, 256], F32, kind="ExternalOutput").ap() if dbg else None
    kind_s = "ExternalOutput" if (dbg or ext_scratch) else "Internal"
    h1s = nc.dram_tensor("h1s", [NT, D], F32, kind=kind_s).ap()
    h2s = nc.dram_tensor("h2s", [NT, D], F32, kind=kind_s).ap()
    uts = nc.dram_tensor("uts", [128, 128, D], BF16, kind=kind_s).ap()
    vbs = nc.dram_tensor("vbs", [128, 128, D], BF16, kind=kind_s).ap()

    es0 = ExitStack()
    with es0:
        P = Prog(nc, es0)
        T_h1s, T_h2s, T_uts, T_vbs = Tr("h1s"), Tr("h2s"), Tr("uts"), Tr("vbs")

        def sbuf(es, name, shape, dt=F32):
            return es.enter_context(nc.sbuf_tensor(name, shape, dt))

        def psum(es, name, shape, dt=F32):
            return es.enter_context(nc.psum_tensor(name, shape, dt))

        def rot(es, name, n, shape, dt=F32, ps=False):
            if ps and dt == F32 and list(shape) == [128]:
                t = psum(es, name, [128, n, 512], dt)
                return Rot([(t[:, i, 0:128], Tr(f"{name}{i}")) for i in range(n)])
            t = (psum if ps else sbuf)(es, name, [128, n] + list(shape), dt)
            return Rot([(t[:, i], Tr(f"{name}{i}")) for i in range(n)])

        idt = sbuf(es0, "idt", [128, 128]); T_idt = Tr()
        idb = sbuf(es0, "idb", [128, 128], BF16); T_idb = Tr()
        ones_f = sbuf(es0, "ones_f", [128, 128]); T_ones = Tr()
        ones_b = sbuf(es0, "ones_b", [128, 128], BF16); T_onesb = Tr()
        m_ui = sbuf(es0, "m_ui", [128, 128]); T_mui = Tr()
        m_lsbd = sbuf(es0, "m_lsbd", [128, 128]); T_mlsbd = Tr()
        m_off = sbuf(es0, "m_off", [128, 128]); T_moff = Tr()
        iota_f = sbuf(es0, "iota_f", [128, 128]); T_iota = Tr()
        iota_b = sbuf(es0, "iota_b", [128, 128], BF16)

        P.op("pool", lambda e: e.memset(idt[:], 0.0), w=[T_idt])
        P.op("pool", lambda e: e.affine_select(out=idt[:], in_=idt[:], pattern=[[-1, 128]], compare_op=ALU.not_equal,
                                               fill=1.0, base=0, channel_multiplier=1), w=[T_idt])
        P.op("pool", lambda e: e.tensor_copy(out=idb[:], in_=idt[:]), r=[T_idt], w=[T_idb])
        P.op("pool", lambda e: e.memset(ones_f[:], 1.0), w=[T_ones])
        P.op("pool", lambda e: e.memset(ones_b[:], 1.0), w=[T_onesb])
        P.op("pool", lambda e: e.memset(m_ui[:], 1.0), w=[T_mui])
        P.op("pool", lambda e: e.affine_select(out=m_ui[:], in_=m_ui[:], pattern=[[1, 128]], compare_op=ALU.is_ge,
                                               fill=0.0, base=0, channel_multiplier=-1), w=[T_mui])
        P.op("pool", lambda e: e.memset(m_lsbd[:], 1.0), w=[T_mlsbd])
        P.op("pool", lambda e: e.affine_select(out=m_lsbd[:], in_=m_lsbd[:], pattern=[[-1, 128]], compare_op=ALU.is_gt,
                                               fill=0.0, base=0, channel_multiplier=1), w=[T_mlsbd])
        P.op("pool", lambda e: e.memset(m_off[:], 0.0), w=[T_moff])
        P.op("pool", lambda e: e.memset(m_off[64:128, 0:64], 1.0), w=[T_moff])
        P.op("pool", lambda e: e.memset(m_lsbd[64:128, 0:64], 0.0), w=[T_mlsbd])
        P.op("pool", lambda e: e.iota(out=iota_f[:], pattern=[[1, 128]], base=0, channel_multiplier=0,
                                      allow_small_or_imprecise_dtypes=True), w=[T_iota])
        P.op("pool", lambda e: e.tensor_copy(out=iota_b[:], in_=iota_f[:]), r=[T_iota], w=[T_iota])

        dW = P.dsem()
        dX = [P.dsem() for _ in range(3)]
        dS1 = [P.dsem(), P.dsem()]
        dS2 = [P.dsem(), P.dsem()]
        dOut = [P.dsem() for _ in range(2)]

        def load_cols(es, name, aps, ps_slot):
            raw = sbuf(es, name + "_raw", [128, 128]); traw = Tr()
            colT = sbuf(es, name, [128, 128]); tcol = Tr()
            P.op("pool", lambda e: e.memset(raw[:], 0.0), w=[traw])
            r0 = 0
            dsc = P.dsem()
            for a in aps:
                P.dma("sp", raw[r0:r0 + a.shape[0], :], a, dsc, r=[traw], pw=[traw])
                r0 += a.shape[0]
            pp, tpp = ps_slot
            if os.environ.get("NOF32") == "1":
                P.op("act", lambda e: e.copy(out=colT[:], in_=raw[:]), r=[traw], w=[tcol])
            else:
                P.op("pe", lambda e: e.transpose(out=pp, in_=raw[:], identity=idt[:]), r=[traw, T_idt], w=[tpp])
                P.op("act", lambda e: e.copy(out=colT[:], in_=pp), r=[tpp], w=[tcol])
            return colT, tcol

        wstage = rot(es0, "wstage", 2, [512])
        dWs = [P.dsem() for _ in range(2)]

        def load_w(es, name, w, ncols, c0=0):
            npad = (ncols + 127) // 128 * 128
            wt = sbuf(es, name, [128, 8, npad], BF16); t = Tr()
            k = 0
            for c in range(8):
                for n0 in range(0, ncols, 512):
                    n1 = min(n0 + 512, ncols)
                    st, tst = wstage.get()
                    P.dma("sp", st[:, 0:n1 - n0], w[c * 128:(c + 1) * 128, c0 + n0:c0 + n1], dWs[(wstage.i - 1) % 2], w=[tst])
                    eng = "act" if k % 2 == 0 else "pool"
                    if eng == "act":
                        P.op("act", lambda e, st=st, c=c, n0=n0, n1=n1: e.copy(out=wt[:, c, n0:n1], in_=st[:, 0:n1 - n0]),
                             r=[tst], pw=[t])
                    else:
                        P.op("pool", lambda e, st=st, c=c, n0=n0, n1=n1: e.tensor_copy(out=wt[:, c, n0:n1], in_=st[:, 0:n1 - n0]),
                             r=[tst], pw=[t])
                    k += 1
            return wt, t

        def rstd_from_ss(ss_ap, tss, n, rs_ap, trs, shape_note=None):
            P.op("dve", lambda e: e.tensor_scalar(out=rs_ap, in0=ss_ap, scalar1=1.0 / n, scalar2=EPS,
                                                  op0=ALU.mult, op1=ALU.add), r=[tss], w=[trs])
            P.op("act", lambda e: e.activation(out=rs_ap, in_=rs_ap, func=AF.Ln), r=[trs], w=[trs])
            P.op("act", lambda e: e.activation(out=rs_ap, in_=rs_ap, func=AF.Exp, scale=-0.5), r=[trs], w=[trs])

        def norm_T(es_tmp, pools, src_dram_rows, gain, tgain, dst, tdst, col0, dq):
            xs, txs = pools["xs"].get()
            P.dma("sp", xs, src_dram_rows, dq, w=[txs])
            junk, tj = pools["junk"].get()
            ss, tss = pools["ss"].get()
            P.op("act", lambda e: e.activation(out=junk, in_=xs, func=AF.Square, accum_out=ss[:, 0:1]),
                 r=[txs], w=[tj, tss])
            rstd_from_ss(ss[:, 0:1], tss, D, ss[:, 1:2], tss)
            xn, txn = pools["xn"].get()
            P.op("dve", lambda e: e.tensor_scalar(out=xn, in0=xs, scalar1=ss[:, 1:2], scalar2=None, op0=ALU.mult),
                 r=[txs, tss], w=[txn])
            pt, tpt = pools["ptb"].get()
            for c in range(8):
                P.op("pe", lambda e, c=c: e.transpose(out=pt[:, c, :], in_=xn[:, c * 128:(c + 1) * 128], identity=idb[:]),
                     r=[txn, T_idb], pw=[tpt])
            P.op("dve", lambda e: e.tensor_tensor(out=dst[:, :, col0:col0 + 128], in0=pt,
                                                  in1=gain.unsqueeze(2).broadcast_to([128, 8, 128]), op=ALU.mult),
                 r=[tpt, tgain], pw=[tdst])
            return xs, txs

        if "A" in phases:
          try:
            with ExitStack() as es:
                if stop <= -1:
                    P.dead = True
                win, T_win = load_w(es, "win", w_in, 3592)
                wout, T_wout = load_w(es, "wout", w_out, D)
                if stop <= -1:
                    P.dead = False
                pm = rot(es, "pm", 1, [128], ps=True)
                colsA, T_colsA = load_cols(es, "colsA", [norm_mix.rearrange("(c p) -> c p", p=128)]
                                           + [conv_w[k].rearrange("(j p) -> j p", p=128) for k in range(3)]
                                           + [dn_conv_w[k].rearrange("(j p) -> j p", p=128) for k in range(4)]
                                           + [dn_norm.rearrange("(o p) -> o p", o=1)], pm.get())
                g_mix, T_gmix = colsA[:, 0:8], T_colsA
                T_cw = T_dcw = T_dnn = T_colsA
                cwv = lambda j, k: colsA[:, 8 + k * 4 + j:9 + k * 4 + j]
                dcwv = lambda j, k: colsA[:, 20 + k * 12 + j:21 + k * 12 + j]
                dnn = colsA[:, 68:69]
                if stop <= -2:
                    P.dead = True
                if stop <= -3:
                    P.dead = True
                alog = sbuf(es, "alog", [128, 4]); T_alog = Tr()
                P.dma("sp", alog[:], dn_a_log.partition_broadcast(128), P.dsem(), w=[T_alog])
                dtb = sbuf(es, "dtb", [128, 4]); T_dtb = Tr()
                P.dma("sp", dtb[:], dn_dt_bias.partition_broadcast(128), P.dsem(), w=[T_dtb])
                if stop <= -4:
                    P.dead = True
                negA = sbuf(es, "negA", [128, 4]); T_negA = Tr()
                P.op("act", lambda e: e.activation(out=negA[:], in_=alog[:], func=AF.Exp), r=[T_alog], w=[T_negA])
                P.op("dve", lambda e: e.tensor_scalar(out=negA[:], in0=negA[:], scalar1=-1.0, scalar2=None, op0=ALU.mult),
                     r=[T_negA], w=[T_negA])

                if stop <= 0:
                    P.dead = True
                pools = {
                    "xs": rot(es, "xs", 2, [D]), "junk": rot(es, "junk", 1, [D], BF16), "ss": rot(es, "ss", 4, [2]),
                    "xn": rot(es, "xn", 2, [D], BF16), "ptb": rot(es, "ptb", 1, [8, 128], BF16, ps=True),
                }
                xnT = sbuf(es, "xnT", [128, 8, 512], BF16); T_xnT = Tr()
                pj = rot(es, "pj", 2, [512], ps=True)
                pd = rot(es, "pd", 4, [128], ps=True)
                psn = pd
                CH = sbuf(es, "CH", [128, 4, 514]); T_CH = [Tr() for _ in range(4)]
                HALO = sbuf(es, "HALO", [128, 12, 3]); T_HALO = [Tr() for _ in range(12)]
                P.op("pool", lambda e: e.memset(CH[:], 0.0), w=T_CH)
                P.op("pool", lambda e: e.memset(HALO[:], 0.0), w=T_HALO)
                tmpc = rot(es, "tmpc", 1, [512])
                tcv = rot(es, "tcv", 1, [515])
                acc = rot(es, "acc", 2, [512])
                sqb = rot(es, "sqb", 1, [512], BF16)
                rnb = rot(es, "rnb", 1, [512])
                QKV = sbuf(es, "QKV", [128, 12, 512], BF16); T_QKV = [Tr() for _ in range(12)]
                ZS = sbuf(es, "ZS", [128, 4, 512], BF16); T_ZS = [Tr() for _ in range(4)]
                ycat = sbuf(es, "ycat", [128, 8, 512], BF16); T_ycat = [Tr() for _ in range(8)]
                OT = sbuf(es, "OT", [128, 4, 512]); T_OT = [Tr() for _ in range(4)]
                gl = sbuf(es, "gl", [128, 4, 128]); T_gl = Tr()
                beta = sbuf(es, "beta", [128, 4, 4]); T_beta = Tr()
                gg = sbuf(es, "gg", [128, 4, 4]); T_gg = Tr()
                Sf = sbuf(es, "Sf", [128, 4, 128]); T_Sf = [Tr() for _ in range(4)]
                Sb = sbuf(es, "Sb", [128, 4, 128], BF16); T_Sb = [Tr() for _ in range(4)]
                P.op("pool", lambda e: e.memset(Sf[:], 0.0), w=T_Sf)
                P.op("pool", lambda e: e.memset(Sb[:], 0.0), w=T_Sb)
                sm = rot(es, "sm", 24, [16])
                sht = rot(es, "sht", 24, [128])
                lng = rot(es, "lng", 16, [128])
                dbl = rot(es, "dbl", 24, [128])
                b16t = rot(es, "b16t", 32, [128], BF16)
                h1t = rot(es, "h1t", 1, [D])

                def proj_fm(col0, ncol=128):
                    pp, tpp = pj.get()
                    for c in range(8):
                        P.op("pe", lambda e, c=c: e.matmul(pp[0:ncol, :], lhsT=win[:, c, col0:col0 + ncol], rhs=xnT[:, c, :],
                                                          start=(c == 0), stop=(c == 7)),
                             r=[T_win, T_xnT], **({"w": [tpp]} if c == 0 else {"pw": [tpp]}))
                    return pp, tpp

                def dn_chunk(s, main):
                    sl = slice(s * 128, (s + 1) * 128)
                    pg, tpg = pm.get()
                    P.op("pe", lambda e: e.matmul(pg[:, 0:4], lhsT=m_ui[:], rhs=gg[:, s, :], start=True, stop=True),
                         r=[T_mui, T_gg], w=[tpg])
                    sc, tsc = sm.get()
                    P.op("act", lambda e: e.copy(out=sc[:, 0:4], in_=pg[:, 0:4]), r=[tpg], w=[tsc])
                    eg, teg = sm.get()
                    P.op("act", lambda e: e.activation(out=eg[:, 0:4], in_=sc[:, 0:4], func=AF.Exp), r=[tsc], w=[teg])
                    P.op("dve", lambda e: e.tensor_tensor(out=sc[:, 4:8], in0=eg[:, 0:4], in1=beta[:, s, :], op=ALU.mult),
                         r=[teg, T_beta, tsc], w=[tsc])
                    def head(h):
                        tq, tk, tv = T_QKV[h], T_QKV[4 + h], T_QKV[8 + h]
                        qT = QKV[:, h, sl]; kT = QKV[:, 4 + h, sl]; vT = QKV[:, 8 + h, sl]
                        pt, tpt = pools["ptb"].get()
                        P.op("pe", lambda e: e.transpose(out=pt[:, 0, :], in_=kT, identity=idb[:]), r=[tk, T_idb], w=[tpt])
                        P.op("pe", lambda e: e.transpose(out=pt[:, 1, :], in_=vT, identity=idb[:]), r=[tv, T_idb], pw=[tpt])
                        if DNSTOP <= 0:
                            P.dead = True
                        ug, tug = sht.get()
                        P.op("dve", lambda e: e.tensor_scalar(out=ug, in0=m_ui[:], scalar1=gg[:, s, h:h + 1], scalar2=None,
                                                              op0=ALU.mult), r=[T_mui, T_gg], w=[tug])
                        pgb, tpgb = pm.get()
                        P.op("pe", lambda e: e.matmul(pgb, lhsT=ones_f[:], rhs=ug, start=True, stop=True),
                             r=[T_ones, tug], w=[tpgb])
                        if DNSTOP <= 1:
                            P.dead = True
                        elo, telo = sht.get()
                        P.op("dve", lambda e: e.tensor_scalar(out=elo, in0=pgb, scalar1=sc[:, h:h + 1], scalar2=0.0,
                                                              op0=ALU.subtract, op1=ALU.max), r=[tpgb, tsc], w=[telo])
                        P.op("act", lambda e: e.activation(out=elo, in_=elo, func=AF.Exp, scale=-1.0), r=[telo], w=[telo])
                        eup, teup = lng.get()
                        P.op("dve", lambda e: e.tensor_scalar(out=eup, in0=pgb, scalar1=sc[:, h:h + 1], scalar2=0.0,
                                                              op0=ALU.subtract, op1=ALU.min), r=[tpgb, tsc], w=[teup])
                        P.op("act", lambda e: e.activation(out=eup, in_=eup, func=AF.Exp), r=[teup], w=[teup])
                        P.op("pool", lambda e: e.tensor_tensor(out=eup, in0=eup, in1=m_ui[:], op=ALU.mult),
                             r=[teup, T_mui], w=[teup])
                        md, tmd = sht.get()
                        mo, tmo = sht.get()
                        P.op("pool", lambda e: e.tensor_tensor(out=md, in0=elo, in1=m_lsbd[:], op=ALU.mult),
                             r=[telo, T_mlsbd], w=[tmd])
                        P.op("pool", lambda e: e.tensor_tensor(out=mo, in0=elo, in1=m_off[:], op=ALU.mult),
                             r=[telo, T_moff], w=[tmo])
                        if DNSTOP <= 2:
                            P.dead = True
                        s2, ts2 = sm.get()
                        P.op("dve", lambda e: e.tensor_copy(out=s2[:, 0:1], in_=pgb[:, 127:128]), r=[tpgb], w=[ts2])
                        P.op("act", lambda e: e.activation(out=s2[:, 1:2], in_=sc[:, h:h + 1], func=AF.Exp, scale=-1.0,
                                                           bias=s2[:, 0:1]), r=[tsc, ts2], w=[ts2])
                        P.op("act", lambda e: e.activation(out=s2[:, 2:3], in_=s2[:, 0:1], func=AF.Exp), r=[ts2], w=[ts2])
                        if main:
                            egb, tegb = lng.get()
                            P.op("act", lambda e: e.activation(out=egb, in_=pgb, func=AF.Exp), xr=[tpgb], w=[tegb])
                        if DNSTOP <= 3:
                            P.dead = True
                        kbe, tkbe = b16t.get(); kdec, tkdec = b16t.get(); vb, tvb = b16t.get()
                        P.op("dve", lambda e: e.tensor_scalar(out=kbe, in0=pt[:, 0, :], scalar1=sc[:, 4 + h:5 + h], scalar2=None,
                                                              op0=ALU.mult), r=[tpt, tsc], w=[tkbe])
                        P.op("dve", lambda e: e.tensor_scalar(out=kdec, in0=pt[:, 0, :], scalar1=s2[:, 1:2], scalar2=None, op0=ALU.mult),
                             r=[tpt, ts2], w=[tkdec])
                        P.op("dve", lambda e: e.tensor_scalar(out=vb, in0=pt[:, 1, :], scalar1=beta[:, s, h:h + 1], scalar2=None,
                                                              op0=ALU.mult), r=[tpt, T_beta], w=[tvb])
                        if DNSTOP <= 4:
                            P.dead = True
                        pkk, tpkk = pd.get()
                        P.op("pe", lambda e: e.matmul(pkk, lhsT=kT, rhs=kT, start=True, stop=True), r=[tk], w=[tpkk])
                        yield
                        Ad, tAd = sht.get(); Ao, tAo = lng.get()
                        P.op("dve", lambda e: e.scalar_tensor_tensor(out=Ad, in0=pkk, scalar=beta[:, s, h:h + 1], in1=md,
                                                                     op0=ALU.mult, op1=ALU.mult), r=[tpkk, T_beta, tmd], w=[tAd])
                        P.op("dve", lambda e: e.scalar_tensor_tensor(out=Ao, in0=pkk, scalar=beta[:, s, h:h + 1], in1=mo,
                                                                     op0=ALU.mult, op1=ALU.mult), r=[tpkk, T_beta, tmo], w=[tAo])
                        if DNSTOP <= 5:
                            P.dead = True
                        pat, tpat = pd.get()
                        P.op("pe", lambda e: e.transpose(out=pat, in_=Ad, identity=idt[:]), r=[tAd, T_idt], w=[tpat])
                        yield
                        PT_, tPT = sht.get(); RT, tRT = sht.get()
                        P.op("dve", lambda e: e.tensor_copy(out=PT_, in_=pat), r=[tpat], w=[tPT])
                        P.op("dve", lambda e: e.tensor_tensor(out=RT, in0=idt[:], in1=pat, op=ALU.subtract),
                             r=[T_idt, tpat], w=[tRT])
                        Pk, tPk = Ad, tAd
                        for lvl in range(1, 6):
                            pp1, tpp1 = pd.get()
                            P.op("pe", lambda e, a=PT_, b=Pk, o=pp1: e.matmul(o, lhsT=a, rhs=b, start=True, stop=True),
                                 r=[tPT, tPk], w=[tpp1])
                            yield
                            nPk, tnPk = dbl.get()
                            P.op("act", lambda e, o=nPk, i=pp1: e.copy(out=o, in_=i), r=[tpp1], w=[tnPk])
                            if lvl < 5:
                                pp2, tpp2 = pd.get()
                                P.op("pe", lambda e, a=Pk, b=PT_, o=pp2: e.matmul(o, lhsT=a, rhs=b, start=True, stop=True),
                                     r=[tPT, tPk], w=[tpp2])
                                yield
                                nPT, tnPT = dbl.get()
                                P.op("dve", lambda e, o=nPT, i=pp2: e.tensor_copy(out=o, in_=i), r=[tpp2], w=[tnPT])
                            pp3, tpp3 = pd.get()
                            P.op("pe", lambda e, a=nPk, b=RT, o=pp3: e.matmul(o, lhsT=a, rhs=b, start=True, stop=True),
                                 r=[tnPk, tRT], w=[tpp3])
                            yield
                            nRT, tnRT = dbl.get()
                            P.op("dve", lambda e, o=nRT, a=RT, b=pp3: e.tensor_tensor(out=o, in0=a, in1=b, op=ALU.add),
                                 r=[tRT, tpp3], w=[tnRT])
                            Pk, tPk = nPk, tnPk
                            if lvl < 5:
                                PT_, tPT = nPT, tnPT
                            RT, tRT = nRT, tnRT
                        if DNSTOP <= 6:
                            P.dead = True
                        ptd, tptd = pd.get()
                        P.op("pe", lambda e: e.transpose(out=ptd, in_=RT, identity=idt[:]), r=[tRT, T_idt], w=[tptd])
                        yield
                        Td, tTd = sht.get()
                        P.op("act", lambda e: e.copy(out=Td, in_=ptd), r=[tptd], w=[tTd])
                        pm1, tpm1 = pd.get()
                        P.op("pe", lambda e: e.matmul(pm1, lhsT=Ao, rhs=RT, start=True, stop=True), r=[tAo, tRT], w=[tpm1])
                        yield
                        M1, tM1 = sht.get()
                        P.op("act", lambda e: e.copy(out=M1, in_=pm1), r=[tpm1], w=[tM1])
                        pm2, tpm2 = pd.get()
                        P.op("pe", lambda e: e.matmul(pm2, lhsT=Td, rhs=M1, start=True, stop=True), r=[tTd, tM1], w=[tpm2])
                        yield
                        TT, tTT = b16t.get()
                        P.op("dve", lambda e: e.tensor_tensor(out=TT, in0=RT, in1=pm2, op=ALU.subtract), r=[tRT, tpm2], w=[tTT])
                        if DNSTOP <= 7:
                            P.dead = True
                        pu, tpu = pd.get()
                        P.op("pe", lambda e: e.matmul(pu, lhsT=TT, rhs=vb, start=True, stop=True), r=[tTT, tvb], w=[tpu])
                        yield
                        u, tu = lng.get()
                        P.op("act", lambda e: e.copy(out=u, in_=pu), r=[tpu], w=[tu])
                        pw_, tpw = pd.get()
                        P.op("pe", lambda e: e.matmul(pw_, lhsT=kbe, rhs=TT, start=True, stop=True), r=[tkbe, tTT], w=[tpw])
                        yield
                        wT, twT = b16t.get()
                        P.op("act", lambda e: e.copy(out=wT, in_=pw_), r=[tpw], w=[twT])
                        if main:
                            pqk, tpqk = pd.get()
                            P.op("pe", lambda e: e.matmul(pqk, lhsT=kT, rhs=qT, start=True, stop=True), r=[tk, tq], w=[tpqk])
                            yield
                            aqk, taqk = b16t.get()
                            P.op("dve", lambda e: e.tensor_tensor(out=aqk, in0=pqk, in1=eup, op=ALU.mult),
                                 r=[tpqk, teup], w=[taqk])
                            qd, tqd = b16t.get()
                            P.op("dve", lambda e: e.tensor_tensor(out=qd, in0=qT, in1=egb, op=ALU.mult),
                                 r=[tq, tegb], w=[tqd])
                        if DNSTOP <= 8:
                            P.dead = True
                        pws, tpws = psn.get()
                        P.op("pe", lambda e: e.matmul(pws, lhsT=wT, rhs=Sb[:, h, :], start=True, stop=True),
                             r=[twT, T_Sb[h]], w=[tpws])
                        yield
                        vn, tvn = b16t.get()
                        P.op("dve", lambda e: e.tensor_tensor(out=vn, in0=u, in1=pws, op=ALU.subtract), r=[tu, tpws], w=[tvn])
                        if main:
                            po, tpo = psn.get()
                            P.op("pe", lambda e: e.matmul(po, lhsT=Sb[:, h, :], rhs=qd, start=True, stop=False),
                                 r=[T_Sb[h], tqd], w=[tpo])
                            yield
                            P.op("pe", lambda e: e.matmul(po, lhsT=vn, rhs=aqk, start=False, stop=True),
                                 r=[tvn, taqk], pw=[tpo])
                            yield
                            P.op("act", lambda e: e.copy(out=OT[:, h, sl], in_=po), r=[tpo], pw=[T_OT[h]])
                        pds, tpds = psn.get()
                        P.op("pe", lambda e: e.matmul(pds, lhsT=kdec, rhs=vn, start=True, stop=True), r=[tkdec, tvn], w=[tpds])
                        yield
                        P.op("dve", lambda e: e.scalar_tensor_tensor(out=Sf[:, h, :], in0=Sf[:, h, :], scalar=s2[:, 2:3], in1=pds,
                                                                     op0=ALU.mult, op1=ALU.add),
                             r=[tpds, ts2], w=[T_Sf[h]])
                        P.op("pool", lambda e: e.tensor_copy(out=Sb[:, h, :], in_=Sf[:, h, :]), r=[T_Sf[h]], w=[T_Sb[h]])

                    for h0 in (0,):
                        gens = [head(h_) for h_ in range(4)]
                        while gens:
                            for g_ in list(gens):
                                try:
                                    next(g_)
                                except StopIteration:
                                    gens.remove(g_)
                print("SBUF remaining after phase A alloc:", nc.sbuf_bytes_remaining)
                tiles = [("p", i) for i in range(NPF // 512)] + [("m", i) for i in range(NT // 512)]
                for ti, (kind, i) in enumerate(tiles):
                    main = kind == "m"
                    full = main or (ti + 1 < len(tiles) and tiles[ti + 1][0] == "m")
                    src = xm if main else xp
                    for s in range(4):
                        r0 = i * 512 + s * 128
                        norm_T(es, pools, src[r0:r0 + 128, :], g_mix, T_gmix, xnT, T_xnT, s * 128, dX[s % 3])
                    if stop <= 1:
                        P.dead = True
                    for s in range(4):
                        pg, tpg = pm.get()
                        for c in range(8):
                            P.op("pe", lambda e, c=c, s=s: e.matmul(pg, lhsT=xnT[:, c, s * 128:(s + 1) * 128],
                                                                  rhs=win[:, c, 3464:3592], start=(c == 0), stop=(c == 7)),
                                 r=[T_win, T_xnT], **({"w": [tpg]} if c == 0 else {"pw": [tpg]}))
                        if os.environ.get("SUB2") != "c":
                            P.op("dve", lambda e, s=s: e.tensor_copy(out=gl[:, s, :], in_=pg), r=[tpg], pw=[T_gl])
                    if stop == 2 and NPF == 512:
                        if os.environ.get("SUB") == "a":
                            P.dead = True
                    P.op("act", lambda e: e.activation(out=beta[:], in_=gl[:, :, 120:124], func=AF.Exp, scale=-1.0), r=[T_gl], w=[T_beta])
                    P.op("dve", lambda e: e.tensor_scalar(out=beta[:], in0=beta[:], scalar1=1.0, scalar2=None, op0=ALU.add),
                         r=[T_beta], w=[T_beta])
                    P.op("dve", lambda e: e.reciprocal(out=beta[:], in_=beta[:]), r=[T_beta], w=[T_beta])
                    P.op("dve", lambda e: e.tensor_tensor(out=gg[:], in0=gl[:, :, 124:128],
                                                          in1=dtb[:].unsqueeze(1).broadcast_to([128, 4, 4]), op=ALU.add),
                         r=[T_gl, T_dtb], w=[T_gg])
                    P.op("act", lambda e: e.activation(out=gg[:], in_=gg[:], func=AF.Exp), r=[T_gg], w=[T_gg])
                    P.op("dve", lambda e: e.tensor_scalar(out=gg[:], in0=gg[:], scalar1=1.0, scalar2=None, op0=ALU.add),
                         r=[T_gg], w=[T_gg])
                    P.op("act", lambda e: e.activation(out=gg[:], in_=gg[:], func=AF.Ln), r=[T_gg], w=[T_gg])
                    P.op("dve", lambda e: e.tensor_tensor(out=gg[:], in0=gg[:],
                                                          in1=negA[:].unsqueeze(1).broadcast_to([128, 4, 4]), op=ALU.mult),
                         r=[T_gg, T_negA], w=[T_gg])
                    if stop <= 2:
                        P.dead = True
                    if full:
                        for j in range(4):
                            pc, tpc = proj_fm(512 + j * 128)
                            tc_, ttc = tmpc.get()
                            P.op("act", lambda e, o=tc_, i_=pc: e.copy(out=o, in_=i_), r=[tpc], w=[ttc])
                            ph, tph = proj_fm(1024 + j * 128)
                            P.op("dve", lambda e, j=j, a=tc_, b=ph: e.tensor_tensor(out=CH[:, j, 2:514], in0=a, in1=b, op=ALU.mult),
                                 r=[ttc, tph], w=[T_CH[j]])
                            if main:
                                a_, ta = acc.get()
                                P.op("dve", lambda e, j=j, a_=a_: e.tensor_scalar(out=a_, in0=CH[:, j, 0:512], scalar1=cwv(j, 0),
                                                                                   scalar2=None, op0=ALU.mult), r=[T_CH[j], T_cw], w=[ta])
                                for k in (1, 2):
                                    P.op("dve", lambda e, j=j, a_=a_, k=k: e.scalar_tensor_tensor(
                                        out=a_, in0=CH[:, j, k:k + 512], scalar=cwv(j, k), in1=a_, op0=ALU.mult, op1=ALU.add),
                                        r=[T_CH[j], T_cw], w=[ta])
                                pb, tpb = proj_fm(j * 128)
                                P.op("dve", lambda e, j=j, a_=a_, pb=pb: e.tensor_tensor(out=ycat[:, j, :], in0=a_, in1=pb, op=ALU.mult),
                                     r=[ta, tpb], w=[T_ycat[j]])
                            P.op("pool", lambda e, j=j: e.tensor_copy(out=CH[:, j, 0:2], in_=CH[:, j, 512:514]), r=[], w=[T_CH[j]])
                    if stop <= 3:
                        P.dead = True
                    pend_l2 = []
                    for j in range(12):
                        pq, tpq = proj_fm(1536 + j * 128)
                        while pend_l2:
                            pend_l2.pop(0)()
                        tv_, ttv = tcv.get()
                        P.op("act", lambda e, o=tv_, i_=pq: e.copy(out=o[:, 3:515], in_=i_), r=[tpq], w=[ttv])
                        P.op("pool", lambda e, o=tv_, j=j: e.tensor_copy(out=o[:, 0:3], in_=HALO[:, j, :]), r=[T_HALO[j], ttv], pw=[ttv])
                        a_, ta = acc.get()
                        P.op("dve", lambda e, a_=a_, t=tv_, j=j: e.tensor_scalar(out=a_, in0=t[:, 0:512], scalar1=dcwv(j, 0),
                                                                               scalar2=None, op0=ALU.mult), r=[ttv, T_dcw], w=[ta])
                        for k in (1, 2, 3):
                            P.op("dve", lambda e, a_=a_, t=tv_, j=j, k=k: e.scalar_tensor_tensor(
                                out=a_, in0=t[:, k:k + 512], scalar=dcwv(j, k), in1=a_, op0=ALU.mult, op1=ALU.add),
                                r=[ttv, T_dcw], w=[ta])
                        P.op("pool", lambda e, t=tv_, j=j: e.tensor_copy(out=HALO[:, j, :], in_=t[:, 512:515]), r=[ttv], w=[T_HALO[j]])
                        if j >= 8:
                            P.op("act", lambda e, a_=a_, j=j: e.activation(out=QKV[:, j, :], in_=a_, func=AF.Silu), r=[ta], w=[T_QKV[j]])
                        else:
                            P.op("act", lambda e, a_=a_: e.activation(out=a_, in_=a_, func=AF.Silu), r=[ta], w=[ta])
                            def fin(j=j, a_=a_, ta=ta):
                                sq, tsq = sqb.get()
                                P.op("pool", lambda e, a_=a_, sq=sq: e.tensor_tensor(out=sq, in0=a_, in1=a_, op=ALU.mult), r=[ta], w=[tsq])
                                pss, tpss = pj.get()
                                P.op("pe", lambda e, sq=sq, o=pss: e.matmul(o, lhsT=ones_b[:], rhs=sq, start=True, stop=True),
                                     r=[T_onesb, tsq], w=[tpss])
                                rn, trn = rnb.get()
                                P.op("dve", lambda e, rn=rn, i_=pss: e.tensor_scalar(out=rn, in0=i_, scalar1=EPS, scalar2=None, op0=ALU.add),
                                     r=[tpss], w=[trn])
                                P.op("act", lambda e, rn=rn: e.activation(out=rn, in_=rn, func=AF.Ln), r=[trn], w=[trn])
                                P.op("act", lambda e, rn=rn: e.activation(out=rn, in_=rn, func=AF.Exp, scale=-0.5), r=[trn], w=[trn])
                                qs = (128.0 ** -0.5) if j < 4 else 1.0
                                P.op("dve", lambda e, a_=a_, rn=rn, j=j, qs=qs: e.scalar_tensor_tensor(
                                    out=QKV[:, j, :], in0=a_, scalar=qs, in1=rn, op0=ALU.mult, op1=ALU.mult), r=[ta, trn], w=[T_QKV[j]])
                            pend_l2.append(fin)
                    while pend_l2:
                        pend_l2.pop(0)()
                    if main:
                        for j in range(4):
                            pz, tpz = proj_fm(3072 + j * 128)
                            P.op("act", lambda e, j=j, i_=pz: e.activation(out=ZS[:, j, :], in_=i_, func=AF.Silu), r=[tpz], w=[T_ZS[j]])
                    if stop <= 4:
                        P.dead = True
                    for s in range(4):
                        dn_chunk(s, main)
                    if stop <= 5:
                        P.dead = True
                    if not main:
                        continue
                    for h in range(4):
                        sq, tsq = sqb.get()
                        P.op("act", lambda e, h=h, sq=sq: e.activation(out=sq, in_=OT[:, h, :], func=AF.Square), r=[T_OT[h]], w=[tsq])
                        pss, tpss = pj.get()
                        P.op("pe", lambda e, sq=sq, o=pss: e.matmul(o, lhsT=ones_b[:], rhs=sq, start=True, stop=True),
                             r=[T_onesb, tsq], w=[tpss])
                        rn, trn = rnb.get()
                        rstd_from_ss(pss, tpss, 128, rn, trn)
                        P.op("dve", lambda e, h=h, rn=rn: e.tensor_tensor(out=rn, in0=rn, in1=OT[:, h, :], op=ALU.mult),
                             r=[T_OT[h], trn], w=[trn])
                        P.op("dve", lambda e, h=h, rn=rn: e.scalar_tensor_tensor(out=ycat[:, 4 + h, :], in0=rn, scalar=dnn,
                                                                               in1=ZS[:, h, :], op0=ALU.mult, op1=ALU.mult),
                             r=[trn, T_dnn, T_ZS[h]], w=[T_ycat[4 + h]])
                    for s in range(4):
                        r0 = i * 512 + s * 128
                        xs, txs = pools["xs"].get()
                        P.dma("sp", xs, xm[r0:r0 + 128, :], dX[s % 3], w=[txs])
                        ho, tho = h1t.get()
                        for hf in range(2):
                            pp, tpp = pj.get()
                            for c in range(8):
                                P.op("pe", lambda e, c=c, s=s, hf=hf, pp=pp: e.matmul(
                                    pp, lhsT=ycat[:, c, s * 128:(s + 1) * 128], rhs=wout[:, c, hf * 512:(hf + 1) * 512],
                                    start=(c == 0), stop=(c == 7)), r=[T_wout, T_ycat[c]], **({"w": [tpp]} if c == 0 else {"pw": [tpp]}))
                            P.op("dve", lambda e, hf=hf, pp=pp, ho=ho, xs=xs: e.tensor_tensor(
                                out=ho[:, hf * 512:(hf + 1) * 512], in0=xs[:, hf * 512:(hf + 1) * 512], in1=pp, op=ALU.add),
                                r=[txs, tpp], **({"w": [tho]} if hf == 0 else {"pw": [tho]}))
                        P.dma("sp", h1s[r0:r0 + 128, :], ho, dS1[0], r=[tho], pw=[T_h1s])
          except _Stop:
            pass
          P.dead = False
          P.barrier()

        if "B" in phases:
            with ExitStack() as es:
                pmB = rot(es, "pmB", 1, [128], ps=True)
                colsB, T_colsB = load_cols(es, "colsB", [norm_x.rearrange("(c p) -> c p", p=128),
                                                         norm_mem.rearrange("(c p) -> c p", p=128)], pmB.get())
                g_x, g_mem = colsB[:, 0:8], colsB[:, 8:16]
                wq, T_wq = load_w(es, "wq", w_xq, D)
                wo, T_wo = load_w(es, "wo", w_xo, D)
                pools = {
                    "xs": rot(es, "xsB", 2, [D]), "junk": rot(es, "junkB", 1, [D], BF16), "ss": rot(es, "ssB", 4, [2]),
                    "xn": rot(es, "xnB", 2, [D], BF16), "ptb": rot(es, "ptbB", 1, [8, 128], BF16, ps=True),
                }
                pj = rot(es, "pjB", 2, [512], ps=True)
                pS = rot(es, "pSB", 2, [512], ps=True)
                KT = sbuf(es, "KT", [128, 8, 256], BF16); T_KT = Tr()
                Vm = sbuf(es, "Vm", [128, 2, D], BF16); T_Vm = Tr()
                with ExitStack() as es2:
                    wkv, T_wkv = load_w(es2, "wkv", w_xkv, 2 * D)
                    memT = sbuf(es2, "memT", [128, 8, 256], BF16); T_memT = Tr()
                    for mc in range(2):
                        norm_T(es2, pools, mem[mc * 128:(mc + 1) * 128, :], g_mem, T_colsB, memT, T_memT, mc * 128, dX[mc])
                    for oc in range(8):
                        pp, tpp = pj.get()
                        for c in range(8):
                            P.op("pe", lambda e, c=c, oc=oc, pp=pp: e.matmul(pp[:, 0:256], lhsT=wkv[:, c, oc * 128:(oc + 1) * 128],
                                                                           rhs=memT[:, c, :], start=(c == 0), stop=(c == 7)),
                                 r=[T_wkv, T_memT], **({"w": [tpp]} if c == 0 else {"pw": [tpp]}))
                        P.op("act", lambda e, oc=oc, pp=pp: e.activation(out=KT[:, oc, :], in_=pp[:, 0:256], func=AF.Copy, scale=0.0625),
                             r=[tpp], pw=[T_KT])
                    for mc in range(2):
                        for hf in range(2):
                            pp, tpp = pj.get()
                            for c in range(8):
                                P.op("pe", lambda e, c=c, mc=mc, hf=hf, pp=pp: e.matmul(
                                    pp, lhsT=memT[:, c, mc * 128:(mc + 1) * 128], rhs=wkv[:, c, D + hf * 512:D + (hf + 1) * 512],
                                    start=(c == 0), stop=(c == 7)), r=[T_wkv, T_memT], **({"w": [tpp]} if c == 0 else {"pw": [tpp]}))
                            P.op("act", lambda e, mc=mc, hf=hf, pp=pp: e.copy(out=Vm[:, mc, hf * 512:(hf + 1) * 512], in_=pp),
                                 r=[tpp], pw=[T_Vm])
                    P.barrier()
                xnT = sbuf(es, "xnTB", [128, 8, 512], BF16); T_xnT = Tr()
                qT = sbuf(es, "qTB", [128, 8, 512], BF16); T_qT = [Tr() for _ in range(8)]
                PTm = sbuf(es, "PTm", [128, 4, 2, 512], BF16); T_PTm = [Tr() for _ in range(4)]
                OTb = sbuf(es, "OTb", [128, 8, 512], BF16); T_OTb = [Tr() for _ in range(8)]
                smB = rot(es, "smB", 8, [4])
                Ef = rot(es, "Ef", 2, [256])
                Pn = rot(es, "Pn", 2, [256], BF16)
                h2t = rot(es, "h2t", 2, [D])
                pend_b = []
                for i in range(NT // 512):
                    for s in range(4):
                        r0 = i * 512 + s * 128
                        norm_T(es, pools, h1s[r0:r0 + 128, :], g_x, T_colsB, xnT, T_xnT, s * 128, dX[s % 3])
                    for oc in range(8):
                        pp, tpp = pj.get()
                        for c in range(8):
                            P.op("pe", lambda e, c=c, oc=oc, pp=pp: e.matmul(pp, lhsT=wq[:, c, oc * 128:(oc + 1) * 128], rhs=xnT[:, c, :],
                                                                           start=(c == 0), stop=(c == 7)),
                                 r=[T_wq, T_xnT], **({"w": [tpp]} if c == 0 else {"pw": [tpp]}))
                        P.op("act", lambda e, oc=oc, pp=pp: e.copy(out=qT[:, oc, :], in_=pp), r=[tpp], w=[T_qT[oc]])
                    for s in range(4):
                        sl = slice(s * 128, (s + 1) * 128)
                        for h in range(4):
                            ps_, tps = pS.get()
                            for c2 in range(2):
                                P.op("pe", lambda e, c2=c2, h=h, ps_=ps_, sl=sl: e.matmul(
                                    ps_[:, 0:256], lhsT=qT[:, 2 * h + c2, sl], rhs=KT[:, 2 * h + c2, :], start=(c2 == 0), stop=(c2 == 1)),
                                    r=[T_qT[2 * h + c2], T_KT], **({"w": [tps]} if c2 == 0 else {"pw": [tps]}))
                            while pend_b:
                                pend_b.pop(0)()
                            m_, tm = smB.get()
                            P.op("dve", lambda e, m_=m_, ps_=ps_: e.reduce_max(out=m_[:, 0:1], in_=ps_[:, 0:256], axis=AX.X), r=[tps], w=[tm])
                            P.op("dve", lambda e, m_=m_: e.tensor_scalar(out=m_[:, 1:2], in0=m_[:, 0:1], scalar1=-1.0, scalar2=None, op0=ALU.mult),
                                 r=[tm], w=[tm])
                            ef, tef = Ef.get()
                            P.op("act", lambda e, ef=ef, ps_=ps_, m_=m_: e.activation(out=ef, in_=ps_[:, 0:256], func=AF.Exp, bias=m_[:, 1:2],
                                                                                   accum_out=m_[:, 2:3]), xr=[tps], r=[tm], w=[tef, tm])
                            P.op("dve", lambda e, m_=m_: e.reciprocal(out=m_[:, 3:4], in_=m_[:, 2:3]), r=[tm], w=[tm])
                            pn, tpn = Pn.get()
                            P.op("dve", lambda e, pn=pn, ef=ef, m_=m_: e.tensor_scalar(out=pn, in0=ef, scalar1=m_[:, 3:4], scalar2=None, op0=ALU.mult),
                                 r=[tef, tm], w=[tpn])
                            def fin_t(h=h, pn=pn, tpn=tpn, sl=sl):
                                pt, tpt = pools["ptb"].get()
                                for mc in range(2):
                                    P.op("pe", lambda e, mc=mc, pt=pt, pn=pn: e.transpose(out=pt[:, mc, :], in_=pn[:, mc * 128:(mc + 1) * 128], identity=idb[:]),
                                         r=[tpn, T_idb], **({"w": [tpt]} if mc == 0 else {"pw": [tpt]}))
                                P.op("dve", lambda e, h=h, pt=pt, sl=sl: e.tensor_copy(out=PTm[:, h, :, sl], in_=pt[:, 0:2, :]), r=[tpt], pw=[T_PTm[h]])
                            pend_b.append(fin_t)
                    while pend_b:
                        pend_b.pop(0)()
                    for h in range(4):
                        for dvc in range(2):
                            pp, tpp = pj.get()
                            for mc in range(2):
                                P.op("pe", lambda e, mc=mc, h=h, dvc=dvc, pp=pp: e.matmul(
                                    pp, lhsT=Vm[:, mc, h * 256 + dvc * 128:h * 256 + (dvc + 1) * 128], rhs=PTm[:, h, mc, :],
                                    start=(mc == 0), stop=(mc == 1)), r=[T_Vm, T_PTm[h]], **({"w": [tpp]} if mc == 0 else {"pw": [tpp]}))
                            P.op("act", lambda e, h=h, dvc=dvc, pp=pp: e.copy(out=OTb[:, 2 * h + dvc, :], in_=pp), r=[tpp], w=[T_OTb[2 * h + dvc]])
                    for s in range(4):
                        r0 = i * 512 + s * 128
                        xs, txs = pools["xs"].get()
                        P.dma("sp", xs, h1s[r0:r0 + 128, :], dX[s % 3], w=[txs])
                        ho, tho = h2t.get()
                        for hf in range(2):
                            pp, tpp = pj.get()
                            for c in range(8):
                                P.op("pe", lambda e, c=c, s=s, hf=hf, pp=pp: e.matmul(
                                    pp, lhsT=OTb[:, c, s * 128:(s + 1) * 128], rhs=wo[:, c, hf * 512:(hf + 1) * 512],
                                    start=(c == 0), stop=(c == 7)), r=[T_wo, T_OTb[c]], **({"w": [tpp]} if c == 0 else {"pw": [tpp]}))
                            P.op("dve", lambda e, hf=hf, pp=pp, ho=ho, xs=xs: e.tensor_tensor(
                                out=ho[:, hf * 512:(hf + 1) * 512], in0=xs[:, hf * 512:(hf + 1) * 512], in1=pp, op=ALU.add),
                                r=[txs, tpp], **({"w": [tho]} if hf == 0 else {"pw": [tho]}))
                        P.dma("sp", h2s[r0:r0 + 128, :], ho, dS2[(h2t.i - 1) % 2], r=[tho], pw=[T_h2s])
            P.barrier()

        if "C" in phases:
            with ExitStack() as es:
                NB = 8
                pmisT = psum(es, "pmisC", [128, 512]); T_pmis = Tr()
                pmis = Rot([(pmisT[:, 0:128], T_pmis)])
                pfr = Rot([(pmisT[:, 0:512], T_pmis)])
                ptbC = rot(es, "ptbC", 1, [8, 128], BF16, ps=True)
                py = rot(es, "pyC", 4, [512], ps=True)
                pH = rot(es, "pHC", 2, [512], ps=True)
                colsC, T_colsC = load_cols(es, "colsC", [norm_ffn.rearrange("(c p) -> c p", p=128)], pmis.get())
                g_ffn = colsC[:, 0:8]
                gfin = sbuf(es, "gfin", [128, D]); T_gfin = Tr()
                P.dma("sp", gfin[:], norm_final.partition_broadcast(128), P.dsem(), w=[T_gfin])
                wpq, T_wpq = load_w(es, "wpq", w_pq, D)
                pools = {
                    "xs": rot(es, "xsC", 2, [D]), "junk": rot(es, "junkC", 1, [D], BF16), "ss": rot(es, "ssC", 4, [2]),
                    "xn": rot(es, "xnC", 2, [D], BF16), "ptb": ptbC,
                }
                keysBD = sbuf(es, "keysBD", [128, 8, 256], BF16); T_kbd = Tr()
                P.op("pool", lambda e: e.memset(keysBD[:], 0.0), w=[T_kbd])
                kraw = rot(es, "kraw", 2, [128])
                dK = [P.dsem(), P.dsem()]
                for h in range(8):
                    kr, tkr = kraw.get()
                    for p_ in range(2):
                        P.dma("sp", kr[:, p_ * 64:(p_ + 1) * 64], peer_keys[h, p_], dK[h % 2], **({"w": [tkr]} if p_ == 0 else {"pw": [tkr]}))
                    pk, tpk = pmis.get()
                    P.op("pe", lambda e, kr=kr, pk=pk: e.transpose(out=pk, in_=kr, identity=idt[:]), r=[tkr, T_idt], w=[tpk])
                    P.op("act", lambda e, h=h, pk=pk: e.copy(out=keysBD[0:64, h, 0:128], in_=pk[0:64, :]), r=[tpk, T_kbd], pw=[T_kbd])
                    P.op("act", lambda e, h=h, pk=pk: e.copy(out=keysBD[64:128, h, 128:256], in_=pk[64:128, :]), r=[tpk], pw=[T_kbd])
                with ExitStack() as es2:
                    NS = 4
                    usp = rot(es2, "usp", NS, [D]); vst = rot(es2, "vst", NS, [D])
                    ub = rot(es2, "ub", NS, [D], BF16)
                    utt = rot(es2, "utt", NS, [8, 128], BF16)
                    vb_ = rot(es2, "vbp", NS, [D], BF16)
                    dU = [P.dsem() for _ in range(NS)]; dV = [P.dsem() for _ in range(NS)]
                    dUo = [P.dsem() for _ in range(NS)]; dVo = [P.dsem() for _ in range(NS)]
                    loaded = {}

                    def pre_load(i1):
                        us, tus = usp.get()
                        P.dma("sp", us, expert_u[i1 * 128:(i1 + 1) * 128, :], dU[i1 % NS], w=[tus])
                        vs, tvs = vst.get()
                        P.dma("sp", vs, expert_v[i1 * 128:(i1 + 1) * 128, :], dV[i1 % NS], w=[tvs])
                        loaded[i1] = (us, tus, vs, tvs)

                    def pre_block(i1):
                        us, tus, vs, tvs = loaded.pop(i1)
                        u16, tu16 = ub.get()
                        P.op("act", lambda e: e.copy(out=u16, in_=us), r=[tus], w=[tu16])
                        pt, tpt = ptbC.get()
                        for c in range(8):
                            P.op("pe", lambda e, c=c: e.transpose(out=pt[:, c, :], in_=u16[:, c * 128:(c + 1) * 128], identity=idb[:]),
                                 r=[tu16, T_idb], **({"w": [tpt]} if c == 0 else {"pw": [tpt]}))
                        ut_, tut = utt.get()
                        P.op("dve", lambda e: e.tensor_copy(out=ut_, in_=pt), r=[tpt], w=[tut])
                        v16, tv16 = vb_.get()
                        P.op("dve", lambda e: e.tensor_copy(out=v16, in_=vs), r=[tvs], w=[tv16])
                        P.dma("sp", uts[i1].rearrange("p (c e) -> p c e", c=8), ut_, dUo[i1 % NS], r=[tut], pw=[T_uts])
                        P.dma("sp", vbs[i1], v16, dVo[i1 % NS], r=[tv16], pw=[T_vbs])

                    for i1 in range(2):
                        pre_load(i1)
                    for i1 in range(128):
                        if i1 + 2 < 128:
                            pre_load(i1 + 2)
                        pre_block(i1)
                    P.barrier()
                Gt = sbuf(es, "Gt", [128, 128, 256], BF16); T_G = Tr()
                xnTs = [sbuf(es, f"xnTC{k}", [128, 8, 256], BF16) for k in range(2)]; T_xnTs = [Tr(), Tr()]
                qT = sbuf(es, "qTC", [128, 8, 256], BF16); T_qT = [Tr() for _ in range(8)]
                sc = sbuf(es, "scC", [128, 8, 256]); T_sc = [Tr() for _ in range(8)]
                a1v = sbuf(es, "a1v", [128, 8, 2, 16]); T_a1v = Tr()
                a1i = sbuf(es, "a1i", [128, 8, 2, 16], U32); T_a1i = Tr()
                a1f = sbuf(es, "a1f", [128, 8, 2, 16]); T_a1f = Tr()
                wk128 = rot(es, "wk128", 2, [128])
                cand = sbuf(es, "cand", [128, 8, 256]); T_cand = Tr()
                cwk = rot(es, "cwk", 2, [256])
                oh = sbuf(es, "ohC", [128, 8, 256]); T_oh = Tr()
                bv = sbuf(es, "bv", [128, 8, 16]); T_bv = Tr()
                bp = sbuf(es, "bp", [128, 8, 16], U32); T_bp = Tr()
                au = sbuf(es, "au", [128, 8, 16], U32); T_au = Tr()
                posf = sbuf(es, "posf", [128, 8, 16]); T_posf = Tr()
                af = sbuf(es, "af", [128, 8, 16]); T_af = Tr()
                bf_ = sbuf(es, "bf_", [128, 8, 16]); T_bf = Tr()
                ge = sbuf(es, "ge", [128, 8, 16]); T_ge = Tr()
                gs = sbuf(es, "gs", [128, 8, 2]); T_gs = Tr()
                tk3 = rot(es, "tk3", 3, [128])
                kT3 = sbuf(es, "kT3", [128, 3, 256]); T_kT3 = [Tr() for _ in range(3)]
                P1p = rot(es, "P1p", 2, [NB, 128], BF16)
                P2gp = rot(es, "P2gp", 2, [NB, 128], BF16)
                NSW = 6
                utp = rot(es, "utp", NSW, [D], BF16)
                vtp = rot(es, "vtp", NSW, [D], BF16)
                dUT = [P.dsem() for _ in range(NSW)]; dVT = [P.dsem() for _ in range(NSW)]
                Hgp = rot(es, "Hgp", 3, [256], BF16)
                Ap = rot(es, "Ap", 3, [256], BF16)
                fin = rot(es, "fin", 2, [D])
                print("SBUF remaining after phase C alloc:", nc.sbuf_bytes_remaining)
                cand4 = cand[:].rearrange("p h (a b) -> p h a b", a=16)
                oh4 = oh[:].rearrange("p h (a b) -> p h a b", a=16)
                iota16 = iota_f[:, 0:16].unsqueeze(1).unsqueeze(1).broadcast_to([128, 8, 16, 16])

                def front(i):
                    for s in range(2):
                        r0 = i * 256 + s * 128
                        norm_T(es, pools, h2s[r0:r0 + 128, :], g_ffn, T_colsC, xnTs[i % 2], T_xnTs[i % 2], s * 128, dX[s % 3])
                        yield
                    for oc in range(8):
                        yield
                        pp, tpp = pfr.get()
                        for c in range(8):
                            P.op("pe", lambda e, c=c, oc=oc, pp=pp: e.matmul(pp[:, 0:256], lhsT=wpq[:, c, oc * 128:(oc + 1) * 128], rhs=xnTs[i % 2][:, c, :],
                                                                           start=(c == 0), stop=(c == 7)),
                                 r=[T_wpq, T_xnTs[i % 2]], **({"w": [tpp]} if c == 0 else {"pw": [tpp]}))
                        P.op("act", lambda e, oc=oc, pp=pp: e.copy(out=qT[:, oc, :], in_=pp[:, 0:256]), r=[tpp], w=[T_qT[oc]])
                    for s in range(2):
                        sl = slice(s * 128, (s + 1) * 128)
                        for h in range(8):
                            yield
                            pp, tpp = pfr.get()
                            P.op("pe", lambda e, h=h, pp=pp, sl=sl: e.matmul(pp[:, 0:256], lhsT=qT[:, h, sl], rhs=keysBD[:, h, :], start=True, stop=True),
                                 r=[T_qT[h], T_kbd], w=[tpp])
                            P.op("act", lambda e, h=h, pp=pp: e.copy(out=sc[:, h, :], in_=pp[:, 0:256]), r=[tpp], w=[T_sc[h]])
                        for h in range(8):
                            for p_ in range(2):
                                yield
                                src = sc[:, h, p_ * 128:(p_ + 1) * 128]
                                P.op("dve", lambda e, h=h, p_=p_, src=src: e.max(out=a1v[:, h, p_, 0:8], in_=src), r=[T_sc[h]], pw=[T_a1v])
                                P.op("dve", lambda e, h=h, p_=p_, src=src: e.max_index(out=a1i[:, h, p_, 0:8], in_max=a1v[:, h, p_, 0:8], in_values=src),
                                     r=[T_sc[h], T_a1v], pw=[T_a1i])
                                wk, twk = wk128.get()
                                P.op("dve", lambda e, h=h, p_=p_, src=src, wk=wk: e.match_replace(out=wk, in_to_replace=a1v[:, h, p_, 0:8], in_values=src, imm_value=NEG),
                                     r=[T_sc[h], T_a1v], w=[twk])
                                P.op("dve", lambda e, h=h, p_=p_, wk=wk: e.max(out=a1v[:, h, p_, 8:16], in_=wk), r=[twk], pw=[T_a1v])
                                P.op("dve", lambda e, h=h, p_=p_, wk=wk: e.max_index(out=a1i[:, h, p_, 8:16], in_max=a1v[:, h, p_, 8:16], in_values=wk),
                                     r=[twk, T_a1v], pw=[T_a1i])
                        P.op("dve", lambda e: e.tensor_tensor(out=cand4, in0=a1v[:, :, 0, :].unsqueeze(3).broadcast_to([128, 8, 16, 16]),
                                                              in1=a1v[:, :, 1, :].unsqueeze(2).broadcast_to([128, 8, 16, 16]), op=ALU.add),
                             r=[T_a1v], w=[T_cand])
                        for h in range(8):
                            yield
                            csrc = cand[:, h, :]
                            P.op("dve", lambda e, h=h, csrc=csrc: e.max(out=bv[:, h, 0:8], in_=csrc), r=[T_cand], pw=[T_bv])
                            P.op("dve", lambda e, h=h, csrc=csrc: e.max_index(out=bp[:, h, 0:8], in_max=bv[:, h, 0:8], in_values=csrc),
                                 r=[T_cand, T_bv], pw=[T_bp])
                            ck, tck = cwk.get()
                            P.op("dve", lambda e, h=h, csrc=csrc, ck=ck: e.match_replace(out=ck, in_to_replace=bv[:, h, 0:8], in_values=csrc, imm_value=NEG),
                                 r=[T_cand, T_bv], w=[tck])
                            P.op("dve", lambda e, h=h, ck=ck: e.max(out=bv[:, h, 8:16], in_=ck), r=[tck], pw=[T_bv])
                            P.op("dve", lambda e, h=h, ck=ck: e.max_index(out=bp[:, h, 8:16], in_max=bv[:, h, 8:16], in_values=ck),
                                 r=[tck, T_bv], pw=[T_bp])
                        yield
                        P.op("dve", lambda e: e.tensor_single_scalar(out=au[:], in_=bp[:], scalar=4, op=ALU.logical_shift_right), r=[T_bp], w=[T_au])
                        P.op("dve", lambda e: e.tensor_copy(out=posf[:], in_=bp[:]), r=[T_bp], w=[T_posf])
                        P.op("dve", lambda e: e.tensor_copy(out=af[:], in_=au[:]), r=[T_au], w=[T_af])
                        P.op("dve", lambda e: e.scalar_tensor_tensor(out=bf_[:], in0=af[:], scalar=-16.0, in1=posf[:], op0=ALU.mult, op1=ALU.add),
                             r=[T_af, T_posf], w=[T_bf])
                        P.op("dve", lambda e: e.tensor_copy(out=a1f[:], in_=a1i[:]), r=[T_a1i], w=[T_a1f])
                        idx_t = []
                        for which, (sel, tsel) in enumerate(((af, T_af), (bf_, T_bf))):
                            P.op("dve", lambda e, sel=sel: e.tensor_tensor(out=oh4, in0=iota16, in1=sel[:].unsqueeze(3).broadcast_to([128, 8, 16, 16]),
                                                                          op=ALU.is_equal), r=[T_iota, tsel], w=[T_oh])
                            P.op("dve", lambda e, which=which: e.tensor_tensor(out=oh4, in0=oh4,
                                                                              in1=a1f[:, :, which, :].unsqueeze(2).broadcast_to([128, 8, 16, 16]), op=ALU.mult),
                                 r=[T_a1f], w=[T_oh])
                            it, tit = tk3.get()
                            P.op("dve", lambda e, it=it: e.tensor_reduce(out=it.rearrange("p (h j) -> p h j", h=8), in_=oh4, axis=AX.X, op=ALU.add),
                                 r=[T_oh], w=[tit])
                            idx_t.append((it, tit))
                        yield
                        P.op("dve", lambda e: e.tensor_tensor(out=ge[:], in0=bv[:], in1=bv[:, :, 0:1].broadcast_to([128, 8, 16]), op=ALU.subtract),
                             r=[T_bv], w=[T_ge])
                        P.op("act", lambda e: e.activation(out=ge[:], in_=ge[:], func=AF.Exp), r=[T_ge], w=[T_ge])
                        P.op("dve", lambda e: e.tensor_reduce(out=gs[:, :, 0], in_=ge[:], axis=AX.X, op=ALU.add), r=[T_ge], w=[T_gs])
                        P.op("dve", lambda e: e.reciprocal(out=gs[:, :, 1], in_=gs[:, :, 0]), r=[T_gs], w=[T_gs])
                        gt_, tgt = tk3.get()
                        P.op("dve", lambda e, gt_=gt_: e.tensor_tensor(out=gt_.rearrange("p (h j) -> p h j", h=8), in0=ge[:],
                                                                      in1=gs[:, :, 1:2].broadcast_to([128, 8, 16]), op=ALU.mult),
                             r=[T_ge, T_gs], w=[tgt])
                        idx_t.append((gt_, tgt))
                        yield
                        for q, (it, tit) in enumerate(idx_t):
                            pk, tpk = pmis.get()
                            P.op("pe", lambda e, it=it, pk=pk: e.transpose(out=pk, in_=it, identity=idt[:]), r=[tit, T_idt], w=[tpk])
                            P.op("act", lambda e, q=q, pk=pk, sl=sl: e.copy(out=kT3[:, q, sl], in_=pk), r=[tpk], pw=[T_kT3[q]])
                def gbuild(i):
                    for b0 in range(0, 256 if os.environ.get("SKIP_GBUILD") != "1" else NB, NB):
                        p1, tp1 = P1p.get(); p2g, tp2g = P2gp.get()
                        for j in range(NB):
                            t_ = b0 + j
                            P.op("dve", lambda e, p1=p1, j=j, t_=t_: e.tensor_scalar(
                                out=p1[:, j, :], in0=iota_b[:], scalar1=kT3[:, 0, t_:t_ + 1], scalar2=None, op0=ALU.is_equal),
                                r=[T_iota, T_kT3[0]], **({"w": [tp1]} if j == 0 else {"pw": [tp1]}))
                            P.op("dve", lambda e, p2g=p2g, j=j, t_=t_: e.tensor_scalar(
                                out=p2g[:, j, :], in0=iota_b[:], scalar1=kT3[:, 1, t_:t_ + 1], scalar2=kT3[:, 2, t_:t_ + 1],
                                op0=ALU.is_equal, op1=ALU.mult),
                                r=[T_iota, T_kT3[1], T_kT3[2]], **({"w": [tp2g]} if j == 0 else {"pw": [tp2g]}))
                        for q0 in range(0, NB, 4):
                            pg_, tpg_ = pH.get()
                            for q in range(4):
                                P.op("pe", lambda e, q=q, q0=q0, pg_=pg_, p1=p1, p2g=p2g: e.matmul(
                                    pg_[:, q * 128:(q + 1) * 128], lhsT=p2g[:, q0 + q, :], rhs=p1[:, q0 + q, :], start=True, stop=True),
                                    r=[tp1, tp2g], **({"w": [tpg_]} if q == 0 else {"pw": [tpg_]}))
                            P.op("act", lambda e, b0=b0, q0=q0, pg_=pg_: e.copy(out=Gt[:, :, b0 + q0:b0 + q0 + 4],
                                                                              in_=pg_.rearrange("p (t i) -> p i t", t=4)), r=[tpg_], pw=[T_G])
                def sweep_final(i, gen):
                    Y = [py.get() for _ in range(4)]

                    def emit_Y(i1, a_, ta, vt_, tvt, k2):
                        for ts in range(2):
                            for hf in range(2):
                                yb, tyb = Y[ts * 2 + hf]
                                P.op("pe", lambda e, ts=ts, hf=hf, yb=yb, a_=a_, vt_=vt_, k2=k2, i1=i1: e.matmul(
                                    yb, lhsT=a_[:, ts * 128:(ts + 1) * 128], rhs=vt_[:, hf * 512:(hf + 1) * 512],
                                    start=(i1 == 0), stop=(i1 == 127)), r=[ta, tvt], **({"w": [tyb]} if i1 == 0 else {"pw": [tyb]}))
                    pend = None
                    PF = NSW - 2
                    blk = {}

                    def fetch(j):
                        ut_, tut = utp.get(); vt_, tvt = vtp.get()
                        P.dma("sp", ut_, uts[j], dUT[j % NSW], r=[T_uts], w=[tut])
                        P.dma("sp", vt_, vbs[j], dVT[j % NSW], r=[T_vbs], w=[tvt])
                        blk[j] = (ut_, tut, vt_, tvt)
                    for j in range(PF):
                        fetch(j)
                    for i1 in range(128):
                        if gen is not None:
                            next(gen, None)
                        if i1 + PF < 128:
                            fetch(i1 + PF)
                        ut_, tut, vt_, tvt = blk.pop(i1)
                        k2 = i1 % 2
                        ph_, tph = pH.get()
                        for c in range(8):
                            P.op("pe", lambda e, c=c, k2=k2, ph_=ph_, ut_=ut_: e.matmul(ph_[:, 0:256], lhsT=ut_[:, c * 128:(c + 1) * 128], rhs=xnTs[i % 2][:, c, :],
                                                                                     start=(c == 0), stop=(c == 7)),
                                 r=[tut, T_xnTs[i % 2]], **({"w": [tph]} if c == 0 else {"pw": [tph]}))
                        hg, thg = Hgp.get()
                        P.op("act", lambda e, hg=hg, ph_=ph_: e.activation(out=hg, in_=ph_[:, 0:256], func=AF.Gelu), r=[tph], w=[thg])
                        a_, ta = Ap.get()
                        P.op("dve", lambda e, a_=a_, hg=hg, i1=i1: e.tensor_tensor(out=a_, in0=hg, in1=Gt[:, i1, :], op=ALU.mult), r=[thg, T_G], w=[ta])
                        if pend is not None:
                            emit_Y(*pend)
                        pend = (i1, a_, ta, vt_, tvt, k2)
                    emit_Y(*pend)
                    if gen is not None:
                        for _ in gen:
                            pass
                    for ts in range(2):
                        r0 = i * 256 + ts * 128
                        xs, txs = pools["xs"].get()
                        P.dma("sp", xs, h2s[r0:r0 + 128, :], dX[ts % 3], w=[txs])
                        fo, tfo = fin.get()
                        for hf in range(2):
                            yb, tyb = Y[ts * 2 + hf]
                            P.op("dve", lambda e, hf=hf, yb=yb, fo=fo, xs=xs: e.tensor_tensor(
                                out=fo[:, hf * 512:(hf + 1) * 512], in0=xs[:, hf * 512:(hf + 1) * 512], in1=yb, op=ALU.add),
                                r=[txs, tyb], **({"w": [tfo]} if hf == 0 else {"pw": [tfo]}))
                        junk, tj = pools["junk"].get()
                        ss, tss = pools["ss"].get()
                        P.op("act", lambda e, junk=junk, fo=fo, ss=ss: e.activation(out=junk, in_=fo, func=AF.Square, accum_out=ss[:, 0:1]),
                             r=[tfo], w=[tj, tss])
                        rstd_from_ss(ss[:, 0:1], tss, D, ss[:, 1:2], tss)
                        P.op("dve", lambda e, fo=fo, ss=ss: e.scalar_tensor_tensor(out=fo, in0=fo, scalar=ss[:, 1:2], in1=gfin[:], op0=ALU.mult, op1=ALU.mult),
                             r=[tss, T_gfin], w=[tfo])
                        P.dma("sp", out[r0:r0 + 128, :], fo, dOut[(fin.i - 1) % 2], r=[tfo])
                ntile = NT // 256
                for _ in front(0):
                    pass
                for i in range(ntile):
                    gbuild(i)
                    sweep_final(i, front(i + 1) if i + 1 < ntile else None)
            P.barrier()

        P.barrier()
        P.emit()
    return nc


def make_in_maps(inputs, NT, NPF, n_cores, seq):
    x = np.asarray(inputs["x"], dtype=np.float32)
    memf = np.asarray(inputs["mem"], dtype=np.float32)
    shared = {}
    for k in ("norm_mix", "w_in", "conv_w", "dn_conv_w", "dn_a_log", "dn_dt_bias", "dn_norm", "w_out", "norm_x",
              "norm_mem", "w_xq", "w_xkv", "w_xo", "norm_ffn", "w_pq", "peer_keys", "expert_u", "expert_v"):
        shared[k] = np.ascontiguousarray(np.asarray(inputs[k], dtype=np.float32)[0])
    shared["norm_final"] = np.ascontiguousarray(np.asarray(inputs["norm_final"], dtype=np.float32))
    maps = []
    for c in range(n_cores):
        b, s = c // 2, c % 2
        m = dict(shared)
        m["xm"] = np.ascontiguousarray(x[b, s * NT:(s + 1) * NT])
        m["xp"] = np.ascontiguousarray(x[b, 0:NPF]) if s == 1 else np.zeros((NPF, D), np.float32)
        m["mem"] = np.ascontiguousarray(memf[b])
        maps.append(m)
    return maps


def kernel(**inputs):
    NT = NPF = 4096
    nc = build(NT, NPF)
    maps = make_in_maps(inputs, NT, NPF, 8, 8192)
    res = run_bass_kernel_spmd(nc, maps, core_ids=list(range(8)))
    outp = np.zeros((4, 8192, D), np.float32)
    for c in range(8):
        b, s = c // 2, c % 2
        outp[b, s * NT:(s + 1) * NT] = res.results[c]["out"]
    return outp
```

```python
import types
import numpy as np
import concourse.bass as bass
import concourse.mybir as mybir
from concourse.bass_utils import run_bass_kernel_spmd
from contextlib import ExitStack

F32 = mybir.dt.float32
BF16 = mybir.dt.bfloat16
U32 = mybir.dt.uint32
AF = mybir.ActivationFunctionType
ALU = mybir.AluOpType
AX = mybir.AxisListType

ENGS = ("pe", "dve", "act", "pool", "sp")
EPS = 1e-6
D = 1024
NEG = -1.0e30


def _freeze(fn):
    if fn is None or fn.__closure__ is None:
        return fn
    cells = tuple(types.CellType(c.cell_contents) for c in fn.__closure__)
    return types.FunctionType(fn.__code__, fn.__globals__, fn.__name__, fn.__defaults__, cells)


class Tr:
    __slots__ = ("name", "w", "r")

    def __init__(self, name=""):
        self.name = name
        self.w = {}
        self.r = {}


class Prog:
    def __init__(self, nc, es):
        self.nc = nc
        self.es = es
        self.ins = {e: [] for e in ENGS}
        self.known = {e: {} for e in ENGS}
        self.targets = {e: set() for e in ENGS}
        self.esem = {e: es.enter_context(nc.semaphore("sem_" + e)) for e in ENGS if e != "sp"}
        self.dsems = []
        self.dead = False

    def dsem(self):
        n = len(self.dsems) + 1
        s = self.es.enter_context(self.nc.semaphore(f"dsem{n}"))
        d = {"sem": s, "val": 0, "key": ("d", n)}
        self.dsems.append(d)
        return d

    def _need(self, eng, waits, key, val):
        if key == ("e", "pe") and eng == "pe":
            return
        if self.known[eng].get(key, 0) >= val:
            return
        if waits.get(key, 0) < val:
            waits[key] = val

    def _deps(self, eng, r, w, pw):
        waits = {}
        for t in r:
            for k, v in t.w.items():
                self._need(eng, waits, k, v)
        for t in w:
            for k, v in t.r.items():
                self._need(eng, waits, k, v)
            for k, v in t.w.items():
                self._need(eng, waits, k, v)
        for t in pw:
            for k, v in t.r.items():
                self._need(eng, waits, k, v)
        for k, v in waits.items():
            self.known[eng][k] = v
            if k[0] == "e":
                self.targets[k[1]].add(v)
        return waits

    def _commit(self, key, val, r, w, pw):
        for t in r:
            if t.r.get(key, 0) < val:
                t.r[key] = val
        for t in w:
            t.w = {key: val}
            t.r = {}
        for t in pw:
            if t.r:
                t.r = {}
                t.w = {key: val}
            else:
                t.w[key] = val

    def op(self, eng, fn, r=(), w=(), pw=(), xr=()):
        if self.dead:
            return
        waits = self._deps(eng, r, list(w) + list(xr), pw)
        seq = len(self.ins[eng]) + 1
        self.ins[eng].append([_freeze(fn), waits, None])
        self._commit(("e", eng), seq, list(r) + list(xr), w, pw)

    def dma(self, eng, out, in_, ds, r=(), w=(), pw=(), **kw):
        if self.dead:
            return
        waits = self._deps(eng, r, w, pw)
        ds["val"] += 16

        def fn(e):
            return e.dma_start(out=out, in_=in_, **kw)
        self.ins[eng].append([fn, waits, (ds["sem"], 16)])
        self._commit(ds["key"], ds["val"], r, w, pw)

    def barrier(self):
        last = {}
        for e in ENGS:
            if e == "sp":
                continue
            for idx in range(len(self.ins[e]) - 1, -1, -1):
                if self.ins[e][idx][0] is not None:
                    last[("e", e)] = idx + 1
                    break
        for d in self.dsems:
            if d["val"]:
                last[d["key"]] = d["val"]
        for e in ENGS:
            waits = {}
            for k, v in last.items():
                if k == ("e", e):
                    continue
                if self.known[e].get(k, 0) < v:
                    waits[k] = v
                    self.known[e][k] = v
                    if k[0] == "e":
                        self.targets[k[1]].add(v)
            if waits:
                self.ins[e].append([None, waits, None])

    def emit(self):
        nc = self.nc
        rank = {e: {s: i + 1 for i, s in enumerate(sorted(self.targets[e]))}
                for e in ENGS if e != "sp"}
        dmap = {d["key"]: d["sem"] for d in self.dsems}

        def run(name, e):
            rk = rank.get(name, {})
            for idx, (fn, waits, inc) in enumerate(self.ins[name]):
                for k, v in waits.items():
                    if k[0] == "e":
                        e.wait_ge(self.esem[k[1]], rank[k[1]][v])
                    else:
                        e.wait_ge(dmap[k], v)
                if fn is None:
                    continue
                i = fn(e)
                if inc is not None:
                    i.then_inc(inc[0], inc[1])
                elif (idx + 1) in rk:
                    i.then_inc(self.esem[name], 1)

        with nc.Block() as block:
            @block.tensor
            def _(e):
                run("pe", e)

            @block.vector
            def _(e):
                run("dve", e)

            @block.scalar
            def _(e):
                run("act", e)

            @block.gpsimd
            def _(e):
                run("pool", e)

            @block.sync
            def _(e):
                run("sp", e)


class Rot:
    def __init__(self, slots):
        self.slots = slots
        self.i = 0

    def get(self):
        s = self.slots[self.i % len(self.slots)]
        self.i += 1
        return s


class _Stop(Exception):
    pass


import os
DNSTOP = int(os.environ.get("DNSTOP", "99"))


def build(NT, NPF, phases="ABC", dbg=False, stop=99, ext_scratch=False):
    nc = bass.Bass("TRN2", target_bir_lowering=False)
    dt_in = lambda n, s: nc.dram_tensor(n, s, F32, kind="ExternalInput").ap()
    xm = dt_in("xm", [NT, D])
    xp = dt_in("xp", [NPF, D])
    mem = dt_in("mem", [256, D])
    norm_mix = dt_in("norm_mix", [D])
    w_in = dt_in("w_in", [D, 3592])
    conv_w = dt_in("conv_w", [3, 512])
    dn_conv_w = dt_in("dn_conv_w", [4, 1536])
    dn_a_log = dt_in("dn_a_log", [4])
    dn_dt_bias = dt_in("dn_dt_bias", [4])
    dn_norm = dt_in("dn_norm", [128])
    w_out = dt_in("w_out", [D, D])
    norm_x = dt_in("norm_x", [D])
    norm_mem = dt_in("norm_mem", [D])
    w_xq = dt_in("w_xq", [D, D])
    w_xkv = dt_in("w_xkv", [D, 2 * D])
    w_xo = dt_in("w_xo", [D, D])
    norm_ffn = dt_in("norm_ffn", [D])
    w_pq = dt_in("w_pq", [D, D])
    peer_keys = dt_in("peer_keys", [8, 2, 128, 64])
    expert_u = dt_in("expert_u", [16384, D])
    expert_v = dt_in("expert_v", [16384, D])
    norm_final = dt_in("norm_final", [D])
    out = nc.dram_tensor("out", [NT, D], F32, kind="ExternalOutput").ap()
    dbgk = nc.dram_tensor("dbgk", [NT // 256, 128, 3, 256], F32, kind="ExternalOutput").ap() if dbg else None
    kind_s = "ExternalOutput" if (dbg or ext_scratch) else "Internal"
    h1s = nc.dram_tensor("h1s", [NT, D], F32, kind=kind_s).ap()
    h2s = nc.dram_tensor("h2s", [NT, D], F32, kind=kind_s).ap()
    uts = nc.dram_tensor("uts", [128, 128, D], BF16, kind=kind_s).ap()
    vbs = nc.dram_tensor("vbs", [128, 128, D], BF16, kind=kind_s).ap()

    es0 = ExitStack()
    with es0:
        P = Prog(nc, es0)
        T_h1s, T_h2s, T_uts, T_vbs = Tr("h1s"), Tr("h2s"), Tr("uts"), Tr("vbs")

        def sbuf(es, name, shape, dt=F32):
            return es.enter_context(nc.sbuf_tensor(name, shape, dt))

        def psum(es, name, shape, dt=F32):
            return es.enter_context(nc.psum_tensor(name, shape, dt))

        def rot(es, name, n, shape, dt=F32, ps=False):
            if ps and dt == F32 and list(shape) == [128]:
                t = psum(es, name, [128, n, 512], dt)
                return Rot([(t[:, i, 0:128], Tr(f"{name}{i}")) for i in range(n)])
            t = (psum if ps else sbuf)(es, name, [128, n] + list(shape), dt)
            return Rot([(t[:, i], Tr(f"{name}{i}")) for i in range(n)])

        idt = sbuf(es0, "idt", [128, 128]); T_idt = Tr()
        idb = sbuf(es0, "idb", [128, 128], BF16); T_idb = Tr()
        ones_f = sbuf(es0, "ones_f", [128, 128]); T_ones = Tr()
        ones_b = sbuf(es0, "ones_b", [128, 128], BF16); T_onesb = Tr()
        m_ui = sbuf(es0, "m_ui", [128, 128]); T_mui = Tr()
        m_lsbd = sbuf(es0, "m_lsbd", [128, 128]); T_mlsbd = Tr()
        m_off = sbuf(es0, "m_off", [128, 128]); T_moff = Tr()
        iota_f = sbuf(es0, "iota_f", [128, 128]); T_iota = Tr()
        iota_b = sbuf(es0, "iota_b", [128, 128], BF16)

        P.op("pool", lambda e: e.memset(idt[:], 0.0), w=[T_idt])
        P.op("pool", lambda e: e.affine_select(out=idt[:], in_=idt[:], pattern=[[-1, 128]], compare_op=ALU.not_equal,
                                               fill=1.0, base=0, channel_multiplier=1), w=[T_idt])
        P.op("pool", lambda e: e.tensor_copy(out=idb[:], in_=idt[:]), r=[T_idt], w=[T_idb])
        P.op("pool", lambda e: e.memset(ones_f[:], 1.0), w=[T_ones])
        P.op("pool", lambda e: e.memset(ones_b[:], 1.0), w=[T_onesb])
        P.op("pool", lambda e: e.memset(m_ui[:], 1.0), w=[T_mui])
        P.op("pool", lambda e: e.affine_select(out=m_ui[:], in_=m_ui[:], pattern=[[1, 128]], compare_op=ALU.is_ge,
                                               fill=0.0, base=0, channel_multiplier=-1), w=[T_mui])
        P.op("pool", lambda e: e.memset(m_lsbd[:], 1.0), w=[T_mlsbd])
        P.op("pool", lambda e: e.affine_select(out=m_lsbd[:], in_=m_lsbd[:], pattern=[[-1, 128]], compare_op=ALU.is_gt,
                                               fill=0.0, base=0, channel_multiplier=1), w=[T_mlsbd])
        P.op("pool", lambda e: e.memset(m_off[:], 0.0), w=[T_moff])
        P.op("pool", lambda e: e.memset(m_off[64:128, 0:64], 1.0), w=[T_moff])
        P.op("pool", lambda e: e.memset(m_lsbd[64:128, 0:64], 0.0), w=[T_mlsbd])
        P.op("pool", lambda e: e.iota(out=iota_f[:], pattern=[[1, 128]], base=0, channel_multiplier=0,
                                      allow_small_or_imprecise_dtypes=True), w=[T_iota])
        P.op("pool", lambda e: e.tensor_copy(out=iota_b[:], in_=iota_f[:]), r=[T_iota], w=[T_iota])

        dW = P.dsem()
        dX = [P.dsem() for _ in range(3)]
        dS1 = [P.dsem(), P.dsem()]
        dS2 = [P.dsem(), P.dsem()]
        dOut = [P.dsem() for _ in range(2)]

        def load_cols(es, name, aps, ps_slot):
            raw = sbuf(es, name + "_raw", [128, 128]); traw = Tr()
            colT = sbuf(es, name, [128, 128]); tcol = Tr()
            P.op("pool", lambda e: e.memset(raw[:], 0.0), w=[traw])
            r0 = 0
            dsc = P.dsem()
            for a in aps:
                P.dma("sp", raw[r0:r0 + a.shape[0], :], a, dsc, r=[traw], pw=[traw])
                r0 += a.shape[0]
            pp, tpp = ps_slot
            if os.environ.get("NOF32") == "1":
                P.op("act", lambda e: e.copy(out=colT[:], in_=raw[:]), r=[traw], w=[tcol])
            else:
                P.op("pe", lambda e: e.transpose(out=pp, in_=raw[:], identity=idt[:]), r=[traw, T_idt], w=[tpp])
                P.op("act", lambda e: e.copy(out=colT[:], in_=pp), r=[tpp], w=[tcol])
            return colT, tcol

        wstage = rot(es0, "wstage", 2, [512])
        dWs = [P.dsem() for _ in range(2)]

        def load_w(es, name, w, ncols, c0=0):
            npad = (ncols + 127) // 128 * 128
            wt = sbuf(es, name, [128, 8, npad], BF16); t = Tr()
            k = 0
            for c in range(8):
                for n0 in range(0, ncols, 512):
                    n1 = min(n0 + 512, ncols)
                    st, tst = wstage.get()
                    P.dma("sp", st[:, 0:n1 - n0], w[c * 128:(c + 1) * 128, c0 + n0:c0 + n1], dWs[(wstage.i - 1) % 2], w=[tst])
                    eng = "act" if k % 2 == 0 else "pool"
                    if eng == "act":
                        P.op("act", lambda e, st=st, c=c, n0=n0, n1=n1: e.copy(out=wt[:, c, n0:n1], in_=st[:, 0:n1 - n0]),
                             r=[tst], pw=[t])
                    else:
                        P.op("pool", lambda e, st=st, c=c, n0=n0, n1=n1: e.tensor_copy(out=wt[:, c, n0:n1], in_=st[:, 0:n1 - n0]),
                             r=[tst], pw=[t])
                    k += 1
            return wt, t

        def rstd_from_ss(ss_ap, tss, n, rs_ap, trs, shape_note=None):
            P.op("dve", lambda e: e.tensor_scalar(out=rs_ap, in0=ss_ap, scalar1=1.0 / n, scalar2=EPS,
                                                  op0=ALU.mult, op1=ALU.add), r=[tss], w=[trs])
            P.op("act", lambda e: e.activation(out=rs_ap, in_=rs_ap, func=AF.Ln), r=[trs], w=[trs])
            P.op("act", lambda e: e.activation(out=rs_ap, in_=rs_ap, func=AF.Exp, scale=-0.5), r=[trs], w=[trs])

        def norm_T(es_tmp, pools, src_dram_rows, gain, tgain, dst, tdst, col0, dq):
            xs, txs = pools["xs"].get()
            P.dma("sp", xs, src_dram_rows, dq, w=[txs])
            junk, tj = pools["junk"].get()
            ss, tss = pools["ss"].get()
            P.op("act", lambda e: e.activation(out=junk, in_=xs, func=AF.Square, accum_out=ss[:, 0:1]),
                 r=[txs], w=[tj, tss])
            rstd_from_ss(ss[:, 0:1], tss, D, ss[:, 1:2], tss)
            xn, txn = pools["xn"].get()
            P.op("dve", lambda e: e.tensor_scalar(out=xn, in0=xs, scalar1=ss[:, 1:2], scalar2=None, op0=ALU.mult),
                 r=[txs, tss], w=[txn])
            pt, tpt = pools["ptb"].get()
            for c in range(8):
                P.op("pe", lambda e, c=c: e.transpose(out=pt[:, c, :], in_=xn[:, c * 128:(c + 1) * 128], identity=idb[:]),
                     r=[txn, T_idb], pw=[tpt])
            P.op("dve", lambda e: e.tensor_tensor(out=dst[:, :, col0:col0 + 128], in0=pt,
                                                  in1=gain.unsqueeze(2).broadcast_to([128, 8, 128]), op=ALU.mult),
                 r=[tpt, tgain], pw=[tdst])
            return xs, txs

        if "A" in phases:
          try:
            with ExitStack() as es:
                if stop <= -1:
                    P.dead = True
                win, T_win = load_w(es, "win", w_in, 3592)
                wout, T_wout = load_w(es, "wout", w_out, D)
                if stop <= -1:
                    P.dead = False
                pm = rot(es, "pm", 1, [128], ps=True)
                colsA, T_colsA = load_cols(es, "colsA", [norm_mix.rearrange("(c p) -> c p", p=128)]
                                           + [conv_w[k].rearrange("(j p) -> j p", p=128) for k in range(3)]
                                           + [dn_conv_w[k].rearrange("(j p) -> j p", p=128) for k in range(4)]
                                           + [dn_norm.rearrange("(o p) -> o p", o=1)], pm.get())
                g_mix, T_gmix = colsA[:, 0:8], T_colsA
                T_cw = T_dcw = T_dnn = T_colsA
                cwv = lambda j, k: colsA[:, 8 + k * 4 + j:9 + k * 4 + j]
                dcwv = lambda j, k: colsA[:, 20 + k * 12 + j:21 + k * 12 + j]
                dnn = colsA[:, 68:69]
                if stop <= -2:
                    P.dead = True
                if stop <= -3:
                    P.dead = True
                alog = sbuf(es, "alog", [128, 4]); T_alog = Tr()
                P.dma("sp", alog[:], dn_a_log.partition_broadcast(128), P.dsem(), w=[T_alog])
                dtb = sbuf(es, "dtb", [128, 4]); T_dtb = Tr()
                P.dma("sp", dtb[:], dn_dt_bias.partition_broadcast(128), P.dsem(), w=[T_dtb])
                if stop <= -4:
                    P.dead = True
                negA = sbuf(es, "negA", [128, 4]); T_negA = Tr()
                P.op("act", lambda e: e.activation(out=negA[:], in_=alog[:], func=AF.Exp), r=[T_alog], w=[T_negA])
                P.op("dve", lambda e: e.tensor_scalar(out=negA[:], in0=negA[:], scalar1=-1.0, scalar2=None, op0=ALU.mult),
                     r=[T_negA], w=[T_negA])

                if stop <= 0:
                    P.dead = True
                pools = {
                    "xs": rot(es, "xs", 2, [D]), "junk": rot(es, "junk", 1, [D], BF16), "ss": rot(es, "ss", 4, [2]),
                    "xn": rot(es, "xn", 2, [D], BF16), "ptb": rot(es, "ptb", 1, [8, 128], BF16, ps=True),
                }
                xnT = sbuf(es, "xnT", [128, 8, 512], BF16); T_xnT = Tr()
                pj = rot(es, "pj", 2, [512], ps=True)
                pd = rot(es, "pd", 4, [128], ps=True)
                psn = pd
                CH = sbuf(es, "CH", [128, 4, 514]); T_CH = [Tr() for _ in range(4)]
                HALO = sbuf(es, "HALO", [128, 12, 3]); T_HALO = [Tr() for _ in range(12)]
                P.op("pool", lambda e: e.memset(CH[:], 0.0), w=T_CH)
                P.op("pool", lambda e: e.memset(HALO[:], 0.0), w=T_HALO)
                tmpc = rot(es, "tmpc", 1, [512])
                tcv = rot(es, "tcv", 1, [515])
                acc = rot(es, "acc", 2, [512])
                sqb = rot(es, "sqb", 1, [512], BF16)
                rnb = rot(es, "rnb", 1, [512])
                QKV = sbuf(es, "QKV", [128, 12, 512], BF16); T_QKV = [Tr() for _ in range(12)]
                ZS = sbuf(es, "ZS", [128, 4, 512], BF16); T_ZS = [Tr() for _ in range(4)]
                ycat = sbuf(es, "ycat", [128, 8, 512], BF16); T_ycat = [Tr() for _ in range(8)]
                OT = sbuf(es, "OT", [128, 4, 512]); T_OT = [Tr() for _ in range(4)]
                gl = sbuf(es, "gl", [128, 4, 128]); T_gl = Tr()
                beta = sbuf(es, "beta", [128, 4, 4]); T_beta = Tr()
                gg = sbuf(es, "gg", [128, 4, 4]); T_gg = Tr()
                Sf = sbuf(es, "Sf", [128, 4, 128]); T_Sf = [Tr() for _ in range(4)]
                Sb = sbuf(es, "Sb", [128, 4, 128], BF16); T_Sb = [Tr() for _ in range(4)]
                P.op("pool", lambda e: e.memset(Sf[:], 0.0), w=T_Sf)
                P.op("pool", lambda e: e.memset(Sb[:], 0.0), w=T_Sb)
                sm = rot(es, "sm", 24, [16])
                sht = rot(es, "sht", 24, [128])
                lng = rot(es, "lng", 16, [128])
                dbl = rot(es, "dbl", 24, [128])
                b16t = rot(es, "b16t", 32, [128], BF16)
                h1t = rot(es, "h1t", 1, [D])

                def proj_fm(col0, ncol=128):
                    pp, tpp = pj.get()
                    for c in range(8):
                        P.op("pe", lambda e, c=c: e.matmul(pp[0:ncol, :], lhsT=win[:, c, col0:col0 + ncol], rhs=xnT[:, c, :],
                                                          start=(c == 0), stop=(c == 7)),
                             r=[T_win, T_xnT], **({"w": [tpp]} if c == 0 else {"pw": [tpp]}))
                    return pp, tpp

                def dn_chunk(s, main):
                    sl = slice(s * 128, (s + 1) * 128)
                    pg, tpg = pm.get()
                    P.op("pe", lambda e: e.matmul(pg[:, 0:4], lhsT=m_ui[:], rhs=gg[:, s, :], start=True, stop=True),
                         r=[T_mui, T_gg], w=[tpg])
                    sc, tsc = sm.get()
                    P.op("act", lambda e: e.copy(out=sc[:, 0:4], in_=pg[:, 0:4]), r=[tpg], w=[tsc])
                    eg, teg = sm.get()
                    P.op("act", lambda e: e.activation(out=eg[:, 0:4], in_=sc[:, 0:4], func=AF.Exp), r=[tsc], w=[teg])
                    P.op("dve", lambda e: e.tensor_tensor(out=sc[:, 4:8], in0=eg[:, 0:4], in1=beta[:, s, :], op=ALU.mult),
                         r=[teg, T_beta, tsc], w=[tsc])
                    def head(h):
                        tq, tk, tv = T_QKV[h], T_QKV[4 + h], T_QKV[8 + h]
                        qT = QKV[:, h, sl]; kT = QKV[:, 4 + h, sl]; vT = QKV[:, 8 + h, sl]
                        pt, tpt = pools["ptb"].get()
                        P.op("pe", lambda e: e.transpose(out=pt[:, 0, :], in_=kT, identity=idb[:]), r=[tk, T_idb], w=[tpt])
                        P.op("pe", lambda e: e.transpose(out=pt[:, 1, :], in_=vT, identity=idb[:]), r=[tv, T_idb], pw=[tpt])
                        if DNSTOP <= 0:
                            P.dead = True
                        ug, tug = sht.get()
                        P.op("dve", lambda e: e.tensor_scalar(out=ug, in0=m_ui[:], scalar1=gg[:, s, h:h + 1], scalar2=None,
                                                              op0=ALU.mult), r=[T_mui, T_gg], w=[tug])
                        pgb, tpgb = pm.get()
                        P.op("pe", lambda e: e.matmul(pgb, lhsT=ones_f[:], rhs=ug, start=True, stop=True),
                             r=[T_ones, tug], w=[tpgb])
                        if DNSTOP <= 1:
                            P.dead = True
                        elo, telo = sht.get()
                        P.op("dve", lambda e: e.tensor_scalar(out=elo, in0=pgb, scalar1=sc[:, h:h + 1], scalar2=0.0,
                                                              op0=ALU.subtract, op1=ALU.max), r=[tpgb, tsc], w=[telo])
                        P.op("act", lambda e: e.activation(out=elo, in_=elo, func=AF.Exp, scale=-1.0), r=[telo], w=[telo])
                        eup, teup = lng.get()
                        P.op("dve", lambda e: e.tensor_scalar(out=eup, in0=pgb, scalar1=sc[:, h:h + 1], scalar2=0.0,
                                                              op0=ALU.subtract, op1=ALU.min), r=[tpgb, tsc], w=[teup])
                        P.op("act", lambda e: e.activation(out=eup, in_=eup, func=AF.Exp), r=[teup], w=[teup])
                        P.op("pool", lambda e: e.tensor_tensor(out=eup, in0=eup, in1=m_ui[:], op=ALU.mult),
                             r=[teup, T_mui], w=[teup])
                        md, tmd = sht.get()
                        mo, tmo = sht.get()
                        P.op("pool", lambda e: e.tensor_tensor(out=md, in0=elo, in1=m_lsbd[:], op=ALU.mult),
                             r=[telo, T_mlsbd], w=[tmd])
                        P.op("pool", lambda e: e.tensor_tensor(out=mo, in0=elo, in1=m_off[:], op=ALU.mult),
                             r=[telo, T_moff], w=[tmo])
                        if DNSTOP <= 2:
                            P.dead = True
                        s2, ts2 = sm.get()
                        P.op("dve", lambda e: e.tensor_copy(out=s2[:, 0:1], in_=pgb[:, 127:128]), r=[tpgb], w=[ts2])
                        P.op("act", lambda e: e.activation(out=s2[:, 1:2], in_=sc[:, h:h + 1], func=AF.Exp, scale=-1.0,
                                                           bias=s2[:, 0:1]), r=[tsc, ts2], w=[ts2])
                        P.op("act", lambda e: e.activation(out=s2[:, 2:3], in_=s2[:, 0:1], func=AF.Exp), r=[ts2], w=[ts2])
                        if main:
                            egb, tegb = lng.get()
                            P.op("act", lambda e: e.activation(out=egb, in_=pgb, func=AF.Exp), xr=[tpgb], w=[tegb])
                        if DNSTOP <= 3:
                            P.dead = True
                        kbe, tkbe = b16t.get(); kdec, tkdec = b16t.get(); vb, tvb = b16t.get()
                        P.op("dve", lambda e: e.tensor_scalar(out=kbe, in0=pt[:, 0, :], scalar1=sc[:, 4 + h:5 + h], scalar2=None,
                                                              op0=ALU.mult), r=[tpt, tsc], w=[tkbe])
                        P.op("dve", lambda e: e.tensor_scalar(out=kdec, in0=pt[:, 0, :], scalar1=s2[:, 1:2], scalar2=None, op0=ALU.mult),
                             r=[tpt, ts2], w=[tkdec])
                        P.op("dve", lambda e: e.tensor_scalar(out=vb, in0=pt[:, 1, :], scalar1=beta[:, s, h:h + 1], scalar2=None,
                                                              op0=ALU.mult), r=[tpt, T_beta], w=[tvb])
                        if DNSTOP <= 4:
                            P.dead = True
                        pkk, tpkk = pd.get()
                        P.op("pe", lambda e: e.matmul(pkk, lhsT=kT, rhs=kT, start=True, stop=True), r=[tk], w=[tpkk])
                        yield
                        Ad, tAd = sht.get(); Ao, tAo = lng.get()
                        P.op("dve", lambda e: e.scalar_tensor_tensor(out=Ad, in0=pkk, scalar=beta[:, s, h:h + 1], in1=md,
                                                                     op0=ALU.mult, op1=ALU.mult), r=[tpkk, T_beta, tmd], w=[tAd])
                        P.op("dve", lambda e: e.scalar_tensor_tensor(out=Ao, in0=pkk, scalar=beta[:, s, h:h + 1], in1=mo,
                                                                     op0=ALU.mult, op1=ALU.mult), r=[tpkk, T_beta, tmo], w=[tAo])
                        if DNSTOP <= 5:
                            P.dead = True
                        pat, tpat = pd.get()
                        P.op("pe", lambda e: e.transpose(out=pat, in_=Ad, identity=idt[:]), r=[tAd, T_idt], w=[tpat])
                        yield
                        PT_, tPT = sht.get(); RT, tRT = sht.get()
                        P.op("dve", lambda e: e.tensor_copy(out=PT_, in_=pat), r=[tpat], w=[tPT])
                        P.op("dve", lambda e: e.tensor_tensor(out=RT, in0=idt[:], in1=pat, op=ALU.subtract),
                             r=[T_idt, tpat], w=[tRT])
                        Pk, tPk = Ad, tAd
                        for lvl in range(1, 6):
                            pp1, tpp1 = pd.get()
                            P.op("pe", lambda e, a=PT_, b=Pk, o=pp1: e.matmul(o, lhsT=a, rhs=b, start=True, stop=True),
                                 r=[tPT, tPk], w=[tpp1])
                            yield
                            nPk, tnPk = dbl.get()
                            P.op("act", lambda e, o=nPk, i=pp1: e.copy(out=o, in_=i), r=[tpp1], w=[tnPk])
                            if lvl < 5:
                                pp2, tpp2 = pd.get()
                                P.op("pe", lambda e, a=Pk, b=PT_, o=pp2: e.matmul(o, lhsT=a, rhs=b, start=True, stop=True),
                                     r=[tPT, tPk], w=[tpp2])
                                yield
                                nPT, tnPT = dbl.get()
                                P.op("dve", lambda e, o=nPT, i=pp2: e.tensor_copy(out=o, in_=i), r=[tpp2], w=[tnPT])
                            pp3, tpp3 = pd.get()
                            P.op("pe", lambda e, a=nPk, b=RT, o=pp3: e.matmul(o, lhsT=a, rhs=b, start=True, stop=True),
                                 r=[tnPk, tRT], w=[tpp3])
                            yield
                            nRT, tnRT = dbl.get()
                            P.op("dve", lambda e, o=nRT, a=RT, b=pp3: e.tensor_tensor(out=o, in0=a, in1=b, op=ALU.add),
                                 r=[tRT, tpp3], w=[tnRT])
                            Pk, tPk = nPk, tnPk
                            if lvl < 5:
                                PT_, tPT = nPT, tnPT
                            RT, tRT = nRT, tnRT
                        if DNSTOP <= 6:
                            P.dead = True
                        ptd, tptd = pd.get()
                        P.op("pe", lambda e: e.transpose(out=ptd, in_=RT, identity=idt[:]), r=[tRT, T_idt], w=[tptd])
                        yield
                        Td, tTd = sht.get()
                        P.op("act", lambda e: e.copy(out=Td, in_=ptd), r=[tptd], w=[tTd])
                        pm1, tpm1 = pd.get()
                        P.op("pe", lambda e: e.matmul(pm1, lhsT=Ao, rhs=RT, start=True, stop=True), r=[tAo, tRT], w=[tpm1])
                        yield
                        M1, tM1 = sht.get()
                        P.op("act", lambda e: e.copy(out=M1, in_=pm1), r=[tpm1], w=[tM1])
                        pm2, tpm2 = pd.get()
                        P.op("pe", lambda e: e.matmul(pm2, lhsT=Td, rhs=M1, start=True, stop=True), r=[tTd, tM1], w=[tpm2])
                        yield
                        TT, tTT = b16t.get()
                        P.op("dve", lambda e: e.tensor_tensor(out=TT, in0=RT, in1=pm2, op=ALU.subtract), r=[tRT, tpm2], w=[tTT])
                        if DNSTOP <= 7:
                            P.dead = True
                        pu, tpu = pd.get()
                        P.op("pe", lambda e: e.matmul(pu, lhsT=TT, rhs=vb, start=True, stop=True), r=[tTT, tvb], w=[tpu])
                        yield
                        u, tu = lng.get()
                        P.op("act", lambda e: e.copy(out=u, in_=pu), r=[tpu], w=[tu])
                        pw_, tpw = pd.get()
                        P.op("pe", lambda e: e.matmul(pw_, lhsT=kbe, rhs=TT, start=True, stop=True), r=[tkbe, tTT], w=[tpw])
                        yield
                        wT, twT = b16t.get()
                        P.op("act", lambda e: e.copy(out=wT, in_=pw_), r=[tpw], w=[twT])
                        if main:
                            pqk, tpqk = pd.get()
                            P.op("pe", lambda e: e.matmul(pqk, lhsT=kT, rhs=qT, start=True, stop=True), r=[tk, tq], w=[tpqk])
                            yield
                            aqk, taqk = b16t.get()
                            P.op("dve", lambda e: e.tensor_tensor(out=aqk, in0=pqk, in1=eup, op=ALU.mult),
                                 r=[tpqk, teup], w=[taqk])
                            qd, tqd = b16t.get()
                            P.op("dve", lambda e: e.tensor_tensor(out=qd, in0=qT, in1=egb, op=ALU.mult),
                                 r=[tq, tegb], w=[tqd])
                        if DNSTOP <= 8:
                            P.dead = True
                        pws, tpws = psn.get()
                        P.op("pe", lambda e: e.matmul(pws, lhsT=wT, rhs=Sb[:, h, :], start=True, stop=True),
                             r=[twT, T_Sb[h]], w=[tpws])
                        yield
                        vn, tvn = b16t.get()
                        P.op("dve", lambda e: e.tensor_tensor(out=vn, in0=u, in1=pws, op=ALU.subtract), r=[tu, tpws], w=[tvn])
                        if main:
                            po, tpo = psn.get()
                            P.op("pe", lambda e: e.matmul(po, lhsT=Sb[:, h, :], rhs=qd, start=True, stop=False),
                                 r=[T_Sb[h], tqd], w=[tpo])
                            yield
                            P.op("pe", lambda e: e.matmul(po, lhsT=vn, rhs=aqk, start=False, stop=True),
                                 r=[tvn, taqk], pw=[tpo])
                            yield
                            P.op("act", lambda e: e.copy(out=OT[:, h, sl], in_=po), r=[tpo], pw=[T_OT[h]])
                        pds, tpds = psn.get()
                        P.op("pe", lambda e: e.matmul(pds, lhsT=kdec, rhs=vn, start=True, stop=True), r=[tkdec, tvn], w=[tpds])
                        yield
                        P.op("dve", lambda e: e.scalar_tensor_tensor(out=Sf[:, h, :], in0=Sf[:, h, :], scalar=s2[:, 2:3], in1=pds,
                                                                     op0=ALU.mult, op1=ALU.add),
                             r=[tpds, ts2], w=[T_Sf[h]])
                        P.op("pool", lambda e: e.tensor_copy(out=Sb[:, h, :], in_=Sf[:, h, :]), r=[T_Sf[h]], w=[T_Sb[h]])

                    for h0 in (0,):
                        gens = [head(h_) for h_ in range(4)]
                        while gens:
                            for g_ in list(gens):
                                try:
                                    next(g_)
                                except StopIteration:
                                    gens.remove(g_)
                print("SBUF remaining after phase A alloc:", nc.sbuf_bytes_remaining)
                tiles = [("p", i) for i in range(NPF // 512)] + [("m", i) for i in range(NT // 512)]
                for ti, (kind, i) in enumerate(tiles):
                    main = kind == "m"
                    full = main or (ti + 1 < len(tiles) and tiles[ti + 1][0] == "m")
                    src = xm if main else xp
                    for s in range(4):
                        r0 = i * 512 + s * 128
                        norm_T(es, pools, src[r0:r0 + 128, :], g_mix, T_gmix, xnT, T_xnT, s * 128, dX[s % 3])
                    if stop <= 1:
                        P.dead = True
                    for s in range(4):
                        pg, tpg = pm.get()
                        for c in range(8):
                            P.op("pe", lambda e, c=c, s=s: e.matmul(pg, lhsT=xnT[:, c, s * 128:(s + 1) * 128],
                                                                  rhs=win[:, c, 3464:3592], start=(c == 0), stop=(c == 7)),
                                 r=[T_win, T_xnT], **({"w": [tpg]} if c == 0 else {"pw": [tpg]}))
                        if os.environ.get("SUB2") != "c":
                            P.op("dve", lambda e, s=s: e.tensor_copy(out=gl[:, s, :], in_=pg), r=[tpg], pw=[T_gl])
                    if stop == 2 and NPF == 512:
                        if os.environ.get("SUB") == "a":
                            P.dead = True
                    P.op("act", lambda e: e.activation(out=beta[:], in_=gl[:, :, 120:124], func=AF.Exp, scale=-1.0), r=[T_gl], w=[T_beta])
                    P.op("dve", lambda e: e.tensor_scalar(out=beta[:], in0=beta[:], scalar1=1.0, scalar2=None, op0=ALU.add),
                         r=[T_beta], w=[T_beta])
                    P.op("dve", lambda e: e.reciprocal(out=beta[:], in_=beta[:]), r=[T_beta], w=[T_beta])
                    P.op("dve", lambda e: e.tensor_tensor(out=gg[:], in0=gl[:, :, 124:128],
                                                          in1=dtb[:].unsqueeze(1).broadcast_to([128, 4, 4]), op=ALU.add),
                         r=[T_gl, T_dtb], w=[T_gg])
                    P.op("act", lambda e: e.activation(out=gg[:], in_=gg[:], func=AF.Exp), r=[T_gg], w=[T_gg])
                    P.op("dve", lambda e: e.tensor_scalar(out=gg[:], in0=gg[:], scalar1=1.0, scalar2=None, op0=ALU.add),
                         r=[T_gg], w=[T_gg])
                    P.op("act", lambda e: e.activation(out=gg[:], in_=gg[:], func=AF.Ln), r=[T_gg], w=[T_gg])
                    P.op("dve", lambda e: e.tensor_tensor(out=gg[:], in0=gg[:],
                                                          in1=negA[:].unsqueeze(1).broadcast_to([128, 4, 4]), op=ALU.mult),
                         r=[T_gg, T_negA], w=[T_gg])
                    if stop <= 2:
                        P.dead = True
                    if full:
                        for j in range(4):
                            pc, tpc = proj_fm(512 + j * 128)
                            tc_, ttc = tmpc.get()
                            P.op("act", lambda e, o=tc_, i_=pc: e.copy(out=o, in_=i_), r=[tpc], w=[ttc])
                            ph, tph = proj_fm(1024 + j * 128)
                            P.op("dve", lambda e, j=j, a=tc_, b=ph: e.tensor_tensor(out=CH[:, j, 2:514], in0=a, in1=b, op=ALU.mult),
                                 r=[ttc, tph], w=[T_CH[j]])
                            if main:
                                a_, ta = acc.get()
                                P.op("dve", lambda e, j=j, a_=a_: e.tensor_scalar(out=a_, in0=CH[:, j, 0:512], scalar1=cwv(j, 0),
                                                                                   scalar2=None, op0=ALU.mult), r=[T_CH[j], T_cw], w=[ta])
                                for k in (1, 2):
                                    P.op("dve", lambda e, j=j, a_=a_, k=k: e.scalar_tensor_tensor(
                                        out=a_, in0=CH[:, j, k:k + 512], scalar=cwv(j, k), in1=a_, op0=ALU.mult, op1=ALU.add),
                                        r=[T_CH[j], T_cw], w=[ta])
                                pb, tpb = proj_fm(j * 128)
                                P.op("dve", lambda e, j=j, a_=a_, pb=pb: e.tensor_tensor(out=ycat[:, j, :], in0=a_, in1=pb, op=ALU.mult),
                                     r=[ta, tpb], w=[T_ycat[j]])
                            P.op("pool", lambda e, j=j: e.tensor_copy(out=CH[:, j, 0:2], in_=CH[:, j, 512:514]), r=[], w=[T_CH[j]])
                    if stop <= 3:
                        P.dead = True
                    pend_l2 = []
                    for j in range(12):
                        pq, tpq = proj_fm(1536 + j * 128)
                        while pend_l2:
                            pend_l2.pop(0)()
                        tv_, ttv = tcv.get()
                        P.op("act", lambda e, o=tv_, i_=pq: e.copy(out=o[:, 3:515], in_=i_), r=[tpq], w=[ttv])
                        P.op("pool", lambda e, o=tv_, j=j: e.tensor_copy(out=o[:, 0:3], in_=HALO[:, j, :]), r=[T_HALO[j], ttv], pw=[ttv])
                        a_, ta = acc.get()
                        P.op("dve", lambda e, a_=a_, t=tv_, j=j: e.tensor_scalar(out=a_, in0=t[:, 0:512], scalar1=dcwv(j, 0),
                                                                               scalar2=None, op0=ALU.mult), r=[ttv, T_dcw], w=[ta])
                        for k in (1, 2, 3):
                            P.op("dve", lambda e, a_=a_, t=tv_, j=j, k=k: e.scalar_tensor_tensor(
                                out=a_, in0=t[:, k:k + 512], scalar=dcwv(j, k), in1=a_, op0=ALU.mult, op1=ALU.add),
                                r=[ttv, T_dcw], w=[ta])
                        P.op("pool", lambda e, t=tv_, j=j: e.tensor_copy(out=HALO[:, j, :], in_=t[:, 512:515]), r=[ttv], w=[T_HALO[j]])
                        if j >= 8:
                            P.op("act", lambda e, a_=a_, j=j: e.activation(out=QKV[:, j, :], in_=a_, func=AF.Silu), r=[ta], w=[T_QKV[j]])
                        else:
                            P.op("act", lambda e, a_=a_: e.activation(out=a_, in_=a_, func=AF.Silu), r=[ta], w=[ta])
                            def fin(j=j, a_=a_, ta=ta):
                                sq, tsq = sqb.get()
                                P.op("pool", lambda e, a_=a_, sq=sq: e.tensor_tensor(out=sq, in0=a_, in1=a_, op=ALU.mult), r=[ta], w=[tsq])
                                pss, tpss = pj.get()
                                P.op("pe", lambda e, sq=sq, o=pss: e.matmul(o, lhsT=ones_b[:], rhs=sq, start=True, stop=True),
                                     r=[T_onesb, tsq], w=[tpss])
                                rn, trn = rnb.get()
                                P.op("dve", lambda e, rn=rn, i_=pss: e.tensor_scalar(out=rn, in0=i_, scalar1=EPS, scalar2=None, op0=ALU.add),
                                     r=[tpss], w=[trn])
                                P.op("act", lambda e, rn=rn: e.activation(out=rn, in_=rn, func=AF.Ln), r=[trn], w=[trn])
                                P.op("act", lambda e, rn=rn: e.activation(out=rn, in_=rn, func=AF.Exp, scale=-0.5), r=[trn], w=[trn])
                                qs = (128.0 ** -0.5) if j < 4 else 1.0
                                P.op("dve", lambda e, a_=a_, rn=rn, j=j, qs=qs: e.scalar_tensor_tensor(
                                    out=QKV[:, j, :], in0=a_, scalar=qs, in1=rn, op0=ALU.mult, op1=ALU.mult), r=[ta, trn], w=[T_QKV[j]])
                            pend_l2.append(fin)
                    while pend_l2:
                        pend_l2.pop(0)()
                    if main:
                        for j in range(4):
                            pz, tpz = proj_fm(3072 + j * 128)
                            P.op("act", lambda e, j=j, i_=pz: e.activation(out=ZS[:, j, :], in_=i_, func=AF.Silu), r=[tpz], w=[T_ZS[j]])
                    if stop <= 4:
                        P.dead = True
                    for s in range(4):
                        dn_chunk(s, main)
                    if stop <= 5:
                        P.dead = True
                    if not main:
                        continue
                    for h in range(4):
                        sq, tsq = sqb.get()
                        P.op("act", lambda e, h=h, sq=sq: e.activation(out=sq, in_=OT[:, h, :], func=AF.Square), r=[T_OT[h]], w=[tsq])
                        pss, tpss = pj.get()
                        P.op("pe", lambda e, sq=sq, o=pss: e.matmul(o, lhsT=ones_b[:], rhs=sq, start=True, stop=True),
                             r=[T_onesb, tsq], w=[tpss])
                        rn, trn = rnb.get()
                        rstd_from_ss(pss, tpss, 128, rn, trn)
                        P.op("dve", lambda e, h=h, rn=rn: e.tensor_tensor(out=rn, in0=rn, in1=OT[:, h, :], op=ALU.mult),
                             r=[T_OT[h], trn], w=[trn])
                        P.op("dve", lambda e, h=h, rn=rn: e.scalar_tensor_tensor(out=ycat[:, 4 + h, :], in0=rn, scalar=dnn,
                                                                               in1=ZS[:, h, :], op0=ALU.mult, op1=ALU.mult),
                             r=[trn, T_dnn, T_ZS[h]], w=[T_ycat[4 + h]])
                    for s in range(4):
                        r0 = i * 512 + s * 128
                        xs, txs = pools["xs"].get()
                        P.dma("sp", xs, xm[r0:r0 + 128, :], dX[s % 3], w=[txs])
                        ho, tho = h1t.get()
                        for hf in range(2):
                            pp, tpp = pj.get()
                            for c in range(8):
                                P.op("pe", lambda e, c=c, s=s, hf=hf, pp=pp: e.matmul(
                                    pp, lhsT=ycat[:, c, s * 128:(s + 1) * 128], rhs=wout[:, c, hf * 512:(hf + 1) * 512],
                                    start=(c == 0), stop=(c == 7)), r=[T_wout, T_ycat[c]], **({"w": [tpp]} if c == 0 else {"pw": [tpp]}))
                            P.op("dve", lambda e, hf=hf, pp=pp, ho=ho, xs=xs: e.tensor_tensor(
                                out=ho[:, hf * 512:(hf + 1) * 512], in0=xs[:, hf * 512:(hf + 1) * 512], in1=pp, op=ALU.add),
                                r=[txs, tpp], **({"w": [tho]} if hf == 0 else {"pw": [tho]}))
                        P.dma("sp", h1s[r0:r0 + 128, :], ho, dS1[0], r=[tho], pw=[T_h1s])
          except _Stop:
            pass
          P.dead = False
          P.barrier()

        if "B" in phases:
            with ExitStack() as es:
                pmB = rot(es, "pmB", 1, [128], ps=True)
                colsB, T_colsB = load_cols(es, "colsB", [norm_x.rearrange("(c p) -> c p", p=128),
                                                         norm_mem.rearrange("(c p) -> c p", p=128)], pmB.get())
                g_x, g_mem = colsB[:, 0:8], colsB[:, 8:16]
                wq, T_wq = load_w(es, "wq", w_xq, D)
                wo, T_wo = load_w(es, "wo", w_xo, D)
                pools = {
                    "xs": rot(es, "xsB", 2, [D]), "junk": rot(es, "junkB", 1, [D], BF16), "ss": rot(es, "ssB", 4, [2]),
                    "xn": rot(es, "xnB", 2, [D], BF16), "ptb": rot(es, "ptbB", 1, [8, 128], BF16, ps=True),
                }
                pj = rot(es, "pjB", 2, [512], ps=True)
                pS = rot(es, "pSB", 2, [512], ps=True)
                KT = sbuf(es, "KT", [128, 8, 256], BF16); T_KT = Tr()
                Vm = sbuf(es, "Vm", [128, 2, D], BF16); T_Vm = Tr()
                with ExitStack() as es2:
                    wkv, T_wkv = load_w(es2, "wkv", w_xkv, 2 * D)
                    memT = sbuf(es2, "memT", [128, 8, 256], BF16); T_memT = Tr()
                    for mc in range(2):
                        norm_T(es2, pools, mem[mc * 128:(mc + 1) * 128, :], g_mem, T_colsB, memT, T_memT, mc * 128, dX[mc])
                    for oc in range(8):
                        pp, tpp = pj.get()
                        for c in range(8):
                            P.op("pe", lambda e, c=c, oc=oc, pp=pp: e.matmul(pp[:, 0:256], lhsT=wkv[:, c, oc * 128:(oc + 1) * 128],
                                                                           rhs=memT[:, c, :], start=(c == 0), stop=(c == 7)),
                                 r=[T_wkv, T_memT], **({"w": [tpp]} if c == 0 else {"pw": [tpp]}))
                        P.op("act", lambda e, oc=oc, pp=pp: e.activation(out=KT[:, oc, :], in_=pp[:, 0:256], func=AF.Copy, scale=0.0625),
                             r=[tpp], pw=[T_KT])
                    for mc in range(2):
                        for hf in range(2):
                            pp, tpp = pj.get()
                            for c in range(8):
                                P.op("pe", lambda e, c=c, mc=mc, hf=hf, pp=pp: e.matmul(
                                    pp, lhsT=memT[:, c, mc * 128:(mc + 1) * 128], rhs=wkv[:, c, D + hf * 512:D + (hf + 1) * 512],
                                    start=(c == 0), stop=(c == 7)), r=[T_wkv, T_memT], **({"w": [tpp]} if c == 0 else {"pw": [tpp]}))
                            P.op("act", lambda e, mc=mc, hf=hf, pp=pp: e.copy(out=Vm[:, mc, hf * 512:(hf + 1) * 512], in_=pp),
                                 r=[tpp], pw=[T_Vm])
                    P.barrier()
                xnT = sbuf(es, "xnTB", [128, 8, 512], BF16); T_xnT = Tr()
                qT = sbuf(es, "qTB", [128, 8, 512], BF16); T_qT = [Tr() for _ in range(8)]
                PTm = sbuf(es, "PTm", [128, 4, 2, 512], BF16); T_PTm = [Tr() for _ in range(4)]
                OTb = sbuf(es, "OTb", [128, 8, 512], BF16); T_OTb = [Tr() for _ in range(8)]
                smB = rot(es, "smB", 8, [4])
                Ef = rot(es, "Ef", 2, [256])
                Pn = rot(es, "Pn", 2, [256], BF16)
                h2t = rot(es, "h2t", 2, [D])
                pend_b = []
                for i in range(NT // 512):
                    for s in range(4):
                        r0 = i * 512 + s * 128
                        norm_T(es, pools, h1s[r0:r0 + 128, :], g_x, T_colsB, xnT, T_xnT, s * 128, dX[s % 3])
                    for oc in range(8):
                        pp, tpp = pj.get()
                        for c in range(8):
                            P.op("pe", lambda e, c=c, oc=oc, pp=pp: e.matmul(pp, lhsT=wq[:, c, oc * 128:(oc + 1) * 128], rhs=xnT[:, c, :],
                                                                           start=(c == 0), stop=(c == 7)),
                                 r=[T_wq, T_xnT], **({"w": [tpp]} if c == 0 else {"pw": [tpp]}))
                        P.op("act", lambda e, oc=oc, pp=pp: e.copy(out=qT[:, oc, :], in_=pp), r=[tpp], w=[T_qT[oc]])
                    for s in range(4):
                        sl = slice(s * 128, (s + 1) * 128)
                        for h in range(4):
                            ps_, tps = pS.get()
                            for c2 in range(2):
                                P.op("pe", lambda e, c2=c2, h=h, ps_=ps_, sl=sl: e.matmul(
                                    ps_[:, 0:256], lhsT=qT[:, 2 * h + c2, sl], rhs=KT[:, 2 * h + c2, :], start=(c2 == 0), stop=(c2 == 1)),
                                    r=[T_qT[2 * h + c2], T_KT], **({"w": [tps]} if c2 == 0 else {"pw": [tps]}))
                            while pend_b:
                                pend_b.pop(0)()
                            m_, tm = smB.get()
                            P.op("dve", lambda e, m_=m_, ps_=ps_: e.reduce_max(out=m_[:, 0:1], in_=ps_[:, 0:256], axis=AX.X), r=[tps], w=[tm])
                            P.op("dve", lambda e, m_=m_: e.tensor_scalar(out=m_[:, 1:2], in0=m_[:, 0:1], scalar1=-1.0, scalar2=None, op0=ALU.mult),
                                 r=[tm], w=[tm])
                            ef, tef = Ef.get()
                            P.op("act", lambda e, ef=ef, ps_=ps_, m_=m_: e.activation(out=ef, in_=ps_[:, 0:256], func=AF.Exp, bias=m_[:, 1:2],
                                                                                   accum_out=m_[:, 2:3]), xr=[tps], r=[tm], w=[tef, tm])
                            P.op("dve", lambda e, m_=m_: e.reciprocal(out=m_[:, 3:4], in_=m_[:, 2:3]), r=[tm], w=[tm])
                            pn, tpn = Pn.get()
                            P.op("dve", lambda e, pn=pn, ef=ef, m_=m_: e.tensor_scalar(out=pn, in0=ef, scalar1=m_[:, 3:4], scalar2=None, op0=ALU.mult),
                                 r=[tef, tm], w=[tpn])
                            def fin_t(h=h, pn=pn, tpn=tpn, sl=sl):
                                pt, tpt = pools["ptb"].get()
                                for mc in range(2):
                                    P.op("pe", lambda e, mc=mc, pt=pt, pn=pn: e.transpose(out=pt[:, mc, :], in_=pn[:, mc * 128:(mc + 1) * 128], identity=idb[:]),
                                         r=[tpn, T_idb], **({"w": [tpt]} if mc == 0 else {"pw": [tpt]}))
                                P.op("dve", lambda e, h=h, pt=pt, sl=sl: e.tensor_copy(out=PTm[:, h, :, sl], in_=pt[:, 0:2, :]), r=[tpt], pw=[T_PTm[h]])
                            pend_b.append(fin_t)
                    while pend_b:
                        pend_b.pop(0)()
                    for h in range(4):
                        for dvc in range(2):
                            pp, tpp = pj.get()
                            for mc in range(2):
                                P.op("pe", lambda e, mc=mc, h=h, dvc=dvc, pp=pp: e.matmul(
                                    pp, lhsT=Vm[:, mc, h * 256 + dvc * 128:h * 256 + (dvc + 1) * 128], rhs=PTm[:, h, mc, :],
                                    start=(mc == 0), stop=(mc == 1)), r=[T_Vm, T_PTm[h]], **({"w": [tpp]} if mc == 0 else {"pw": [tpp]}))
                            P.op("act", lambda e, h=h, dvc=dvc, pp=pp: e.copy(out=OTb[:, 2 * h + dvc, :], in_=pp), r=[tpp], w=[T_OTb[2 * h + dvc]])
                    for s in range(4):
                        r0 = i * 512 + s * 128
                        xs, txs = pools["xs"].get()
                        P.dma("sp", xs, h1s[r0:r0 + 128, :], dX[s % 3], w=[txs])
                        ho, tho = h2t.get()
                        for hf in range(2):
                            pp, tpp = pj.get()
                            for c in range(8):
                                P.op("pe", lambda e, c=c, s=s, hf=hf, pp=pp: e.matmul(
                                    pp, lhsT=OTb[:, c, s * 128:(s + 1) * 128], rhs=wo[:, c, hf * 512:(hf + 1) * 512],
                                    start=(c == 0), stop=(c == 7)), r=[T_wo, T_OTb[c]], **({"w": [tpp]} if c == 0 else {"pw": [tpp]}))
                            P.op("dve", lambda e, hf=hf, pp=pp, ho=ho, xs=xs: e.tensor_tensor(
                                out=ho[:, hf * 512:(hf + 1) * 512], in0=xs[:, hf * 512:(hf + 1) * 512], in1=pp, op=ALU.add),
                                r=[txs, tpp], **({"w": [tho]} if hf == 0 else {"pw": [tho]}))
                        P.dma("sp", h2s[r0:r0 + 128, :], ho, dS2[(h2t.i - 1) % 2], r=[tho], pw=[T_h2s])
            P.barrier()

        if "C" in phases:
            with ExitStack() as es:
                NB = 8
                pmisT = psum(es, "pmisC", [128, 512]); T_pmis = Tr()
                pmis = Rot([(pmisT[:, 0:128], T_pmis)])
                pfr = Rot([(pmisT[:, 0:512], T_pmis)])
                ptbC = rot(es, "ptbC", 1, [8, 128], BF16, ps=True)
                py = rot(es, "pyC", 4, [512], ps=True)
                pH = rot(es, "pHC", 2, [512], ps=True)
                colsC, T_colsC = load_cols(es, "colsC", [norm_ffn.rearrange("(c p) -> c p", p=128)], pmis.get())
                g_ffn = colsC[:, 0:8]
                gfin = sbuf(es, "gfin", [128, D]); T_gfin = Tr()
                P.dma("sp", gfin[:], norm_final.partition_broadcast(128), P.dsem(), w=[T_gfin])
                wpq, T_wpq = load_w(es, "wpq", w_pq, D)
                pools = {
                    "xs": rot(es, "xsC", 2, [D]), "junk": rot(es, "junkC", 1, [D], BF16), "ss": rot(es, "ssC", 4, [2]),
                    "xn": rot(es, "xnC", 2, [D], BF16), "ptb": ptbC,
                }
                keysBD = sbuf(es, "keysBD", [128, 8, 256], BF16); T_kbd = Tr()
                P.op("pool", lambda e: e.memset(keysBD[:], 0.0), w=[T_kbd])
                kraw = rot(es, "kraw", 2, [128])
                dK = [P.dsem(), P.dsem()]
                for h in range(8):
                    kr, tkr = kraw.get()
                    for p_ in range(2):
                        P.dma("sp", kr[:, p_ * 64:(p_ + 1) * 64], peer_keys[h, p_], dK[h % 2], **({"w": [tkr]} if p_ == 0 else {"pw": [tkr]}))
                    pk, tpk = pmis.get()
                    P.op("pe", lambda e, kr=kr, pk=pk: e.transpose(out=pk, in_=kr, identity=idt[:]), r=[tkr, T_idt], w=[tpk])
                    P.op("act", lambda e, h=h, pk=pk: e.copy(out=keysBD[0:64, h, 0:128], in_=pk[0:64, :]), r=[tpk, T_kbd], pw=[T_kbd])
                    P.op("act", lambda e, h=h, pk=pk: e.copy(out=keysBD[64:128, h, 128:256], in_=pk[64:128, :]), r=[tpk], pw=[T_kbd])
                with ExitStack() as es2:
                    NS = 4
                    usp = rot(es2, "usp", NS, [D]); vst = rot(es2, "vst", NS, [D])
                    ub = rot(es2, "ub", NS, [D], BF16)
                    utt = rot(es2, "utt", NS, [8, 128], BF16)
                    vb_ = rot(es2, "vbp", NS, [D], BF16)
                    dU = [P.dsem() for _ in range(NS)]; dV = [P.dsem() for _ in range(NS)]
                    dUo = [P.dsem() for _ in range(NS)]; dVo = [P.dsem() for _ in range(NS)]
                    loaded = {}

                    def pre_load(i1):
                        us, tus = usp.get()
                        P.dma("sp", us, expert_u[i1 * 128:(i1 + 1) * 128, :], dU[i1 % NS], w=[tus])
                        vs, tvs = vst.get()
                        P.dma("sp", vs, expert_v[i1 * 128:(i1 + 1) * 128, :], dV[i1 % NS], w=[tvs])
                        loaded[i1] = (us, tus, vs, tvs)

                    def pre_block(i1):
                        us, tus, vs, tvs = loaded.pop(i1)
                        u16, tu16 = ub.get()
                        P.op("act", lambda e: e.copy(out=u16, in_=us), r=[tus], w=[tu16])
                        pt, tpt = ptbC.get()
                        for c in range(8):
                            P.op("pe", lambda e, c=c: e.transpose(out=pt[:, c, :], in_=u16[:, c * 128:(c + 1) * 128], identity=idb[:]),
                                 r=[tu16, T_idb], **({"w": [tpt]} if c == 0 else {"pw": [tpt]}))
                        ut_, tut = utt.get()
                        P.op("dve", lambda e: e.tensor_copy(out=ut_, in_=pt), r=[tpt], w=[tut])
                        v16, tv16 = vb_.get()
                        P.op("dve", lambda e: e.tensor_copy(out=v16, in_=vs), r=[tvs], w=[tv16])
                        P.dma("sp", uts[i1].rearrange("p (c e) -> p c e", c=8), ut_, dUo[i1 % NS], r=[tut], pw=[T_uts])
                        P.dma("sp", vbs[i1], v16, dVo[i1 % NS], r=[tv16], pw=[T_vbs])

                    for i1 in range(2):
                        pre_load(i1)
                    for i1 in range(128):
                        if i1 + 2 < 128:
                            pre_load(i1 + 2)
                        pre_block(i1)
                    P.barrier()
                Gt = sbuf(es, "Gt", [128, 128, 256], BF16); T_G = Tr()
                xnTs = [sbuf(es, f"xnTC{k}", [128, 8, 256], BF16) for k in range(2)]; T_xnTs = [Tr(), Tr()]
                qT = sbuf(es, "qTC", [128, 8, 256], BF16); T_qT = [Tr() for _ in range(8)]
                sc = sbuf(es, "scC", [128, 8, 256]); T_sc = [Tr() for _ in range(8)]
                a1v = sbuf(es, "a1v", [128, 8, 2, 16]); T_a1v = Tr()
                a1i = sbuf(es, "a1i", [128, 8, 2, 16], U32); T_a1i = Tr()
                a1f = sbuf(es, "a1f", [128, 8, 2, 16]); T_a1f = Tr()
                wk128 = rot(es, "wk128", 2, [128])
                cand = sbuf(es, "cand", [128, 8, 256]); T_cand = Tr()
                cwk = rot(es, "cwk", 2, [256])
                oh = sbuf(es, "ohC", [128, 8, 256]); T_oh = Tr()
                bv = sbuf(es, "bv", [128, 8, 16]); T_bv = Tr()
                bp = sbuf(es, "bp", [128, 8, 16], U32); T_bp = Tr()
                au = sbuf(es, "au", [128, 8, 16], U32); T_au = Tr()
                posf = sbuf(es, "posf", [128, 8, 16]); T_posf = Tr()
                af = sbuf(es, "af", [128, 8, 16]); T_af = Tr()
                bf_ = sbuf(es, "bf_", [128, 8, 16]); T_bf = Tr()
                ge = sbuf(es, "ge", [128, 8, 16]); T_ge = Tr()
                gs = sbuf(es, "gs", [128, 8, 2]); T_gs = Tr()
                tk3 = rot(es, "tk3", 3, [128])
                kT3 = sbuf(es, "kT3", [128, 3, 256]); T_kT3 = [Tr() for _ in range(3)]
                P1p = rot(es, "P1p", 2, [NB, 128], BF16)
                P2gp = rot(es, "P2gp", 2, [NB, 128], BF16)
                NSW = 6
                utp = rot(es, "utp", NSW, [D], BF16)
                vtp = rot(es, "vtp", NSW, [D], BF16)
                dUT = [P.dsem() for _ in range(NSW)]; dVT = [P.dsem() for _ in range(NSW)]
                Hgp = rot(es, "Hgp", 3, [256], BF16)
                Ap = rot(es, "Ap", 4, [256], BF16)
                fin = rot(es, "fin", 2, [D])
                print("SBUF remaining after phase C alloc:", nc.sbuf_bytes_remaining)
                cand4 = cand[:].rearrange("p h (a b) -> p h a b", a=16)
                oh4 = oh[:].rearrange("p h (a b) -> p h a b", a=16)
                iota16 = iota_f[:, 0:16].unsqueeze(1).unsqueeze(1).broadcast_to([128, 8, 16, 16])

                def front(i):
                    for s in range(2):
                        r0 = i * 256 + s * 128
                        norm_T(es, pools, h2s[r0:r0 + 128, :], g_ffn, T_colsC, xnTs[i % 2], T_xnTs[i % 2], s * 128, dX[s % 3])
                        yield
                    for oc in range(8):
                        yield
                        pp, tpp = pfr.get()
                        for c in range(8):
                            P.op("pe", lambda e, c=c, oc=oc, pp=pp: e.matmul(pp[:, 0:256], lhsT=wpq[:, c, oc * 128:(oc + 1) * 128], rhs=xnTs[i % 2][:, c, :],
                                                                           start=(c == 0), stop=(c == 7)),
                                 r=[T_wpq, T_xnTs[i % 2]], **({"w": [tpp]} if c == 0 else {"pw": [tpp]}))
                        P.op("act", lambda e, oc=oc, pp=pp: e.copy(out=qT[:, oc, :], in_=pp[:, 0:256]), r=[tpp], w=[T_qT[oc]])
                    for s in range(2):
                        sl = slice(s * 128, (s + 1) * 128)
                        for h in range(8):
                            yield
                            pp, tpp = pfr.get()
                            P.op("pe", lambda e, h=h, pp=pp, sl=sl: e.matmul(pp[:, 0:256], lhsT=qT[:, h, sl], rhs=keysBD[:, h, :], start=True, stop=True),
                                 r=[T_qT[h], T_kbd], w=[tpp])
                            P.op("act", lambda e, h=h, pp=pp: e.copy(out=sc[:, h, :], in_=pp[:, 0:256]), r=[tpp], w=[T_sc[h]])
                        for h in range(8):
                            for p_ in range(2):
                                yield
                                src = sc[:, h, p_ * 128:(p_ + 1) * 128]
                                P.op("dve", lambda e, h=h, p_=p_, src=src: e.max(out=a1v[:, h, p_, 0:8], in_=src), r=[T_sc[h]], pw=[T_a1v])
                                P.op("dve", lambda e, h=h, p_=p_, src=src: e.max_index(out=a1i[:, h, p_, 0:8], in_max=a1v[:, h, p_, 0:8], in_values=src),
                                     r=[T_sc[h], T_a1v], pw=[T_a1i])
                                wk, twk = wk128.get()
                                P.op("dve", lambda e, h=h, p_=p_, src=src, wk=wk: e.match_replace(out=wk, in_to_replace=a1v[:, h, p_, 0:8], in_values=src, imm_value=NEG),
                                     r=[T_sc[h], T_a1v], w=[twk])
                                P.op("dve", lambda e, h=h, p_=p_, wk=wk: e.max(out=a1v[:, h, p_, 8:16], in_=wk), r=[twk], pw=[T_a1v])
                                P.op("dve", lambda e, h=h, p_=p_, wk=wk: e.max_index(out=a1i[:, h, p_, 8:16], in_max=a1v[:, h, p_, 8:16], in_values=wk),
                                     r=[twk, T_a1v], pw=[T_a1i])
                        P.op("dve", lambda e: e.tensor_tensor(out=cand4, in0=a1v[:, :, 0, :].unsqueeze(3).broadcast_to([128, 8, 16, 16]),
                                                              in1=a1v[:, :, 1, :].unsqueeze(2).broadcast_to([128, 8, 16, 16]), op=ALU.add),
                             r=[T_a1v], w=[T_cand])
                        for h in range(8):
                            yield
                            csrc = cand[:, h, :]
                            P.op("dve", lambda e, h=h, csrc=csrc: e.max(out=bv[:, h, 0:8], in_=csrc), r=[T_cand], pw=[T_bv])
                            P.op("dve", lambda e, h=h, csrc=csrc: e.max_index(out=bp[:, h, 0:8], in_max=bv[:, h, 0:8], in_values=csrc),
                                 r=[T_cand, T_bv], pw=[T_bp])
                            ck, tck = cwk.get()
                            P.op("dve", lambda e, h=h, csrc=csrc, ck=ck: e.match_replace(out=ck, in_to_replace=bv[:, h, 0:8], in_values=csrc, imm_value=NEG),
                                 r=[T_cand, T_bv], w=[tck])
                            P.op("dve", lambda e, h=h, ck=ck: e.max(out=bv[:, h, 8:16], in_=ck), r=[tck], pw=[T_bv])
                            P.op("dve", lambda e, h=h, ck=ck: e.max_index(out=bp[:, h, 8:16], in_max=bv[:, h, 8:16], in_values=ck),
                                 r=[tck, T_bv], pw=[T_bp])
                        yield
                        P.op("dve", lambda e: e.tensor_single_scalar(out=au[:], in_=bp[:], scalar=4, op=ALU.logical_shift_right), r=[T_bp], w=[T_au])
                        P.op("dve", lambda e: e.tensor_copy(out=posf[:], in_=bp[:]), r=[T_bp], w=[T_posf])
                        P.op("dve", lambda e: e.tensor_copy(out=af[:], in_=au[:]), r=[T_au], w=[T_af])
                        P.op("dve", lambda e: e.scalar_tensor_tensor(out=bf_[:], in0=af[:], scalar=-16.0, in1=posf[:], op0=ALU.mult, op1=ALU.add),
                             r=[T_af, T_posf], w=[T_bf])
                        P.op("dve", lambda e: e.tensor_copy(out=a1f[:], in_=a1i[:]), r=[T_a1i], w=[T_a1f])
                        idx_t = []
                        for which, (sel, tsel) in enumerate(((af, T_af), (bf_, T_bf))):
                            P.op("dve", lambda e, sel=sel: e.tensor_tensor(out=oh4, in0=iota16, in1=sel[:].unsqueeze(3).broadcast_to([128, 8, 16, 16]),
                                                                          op=ALU.is_equal), r=[T_iota, tsel], w=[T_oh])
                            P.op("dve", lambda e, which=which: e.tensor_tensor(out=oh4, in0=oh4,
                                                                              in1=a1f[:, :, which, :].unsqueeze(2).broadcast_to([128, 8, 16, 16]), op=ALU.mult),
                                 r=[T_a1f], w=[T_oh])
                            it, tit = tk3.get()
                            P.op("dve", lambda e, it=it: e.tensor_reduce(out=it.rearrange("p (h j) -> p h j", h=8), in_=oh4, axis=AX.X, op=ALU.add),
                                 r=[T_oh], w=[tit])
                            idx_t.append((it, tit))
                        yield
                        P.op("dve", lambda e: e.tensor_tensor(out=ge[:], in0=bv[:], in1=bv[:, :, 0:1].broadcast_to([128, 8, 16]), op=ALU.subtract),
                             r=[T_bv], w=[T_ge])
                        P.op("act", lambda e: e.activation(out=ge[:], in_=ge[:], func=AF.Exp), r=[T_ge], w=[T_ge])
                        P.op("dve", lambda e: e.tensor_reduce(out=gs[:, :, 0], in_=ge[:], axis=AX.X, op=ALU.add), r=[T_ge], w=[T_gs])
                        P.op("dve", lambda e: e.reciprocal(out=gs[:, :, 1], in_=gs[:, :, 0]), r=[T_gs], w=[T_gs])
                        gt_, tgt = tk3.get()
                        P.op("dve", lambda e, gt_=gt_: e.tensor_tensor(out=gt_.rearrange("p (h j) -> p h j", h=8), in0=ge[:],
                                                                      in1=gs[:, :, 1:2].broadcast_to([128, 8, 16]), op=ALU.mult),
                             r=[T_ge, T_gs], w=[tgt])
                        idx_t.append((gt_, tgt))
                        yield
                        for q, (it, tit) in enumerate(idx_t):
                            pk, tpk = pmis.get()
                            P.op("pe", lambda e, it=it, pk=pk: e.transpose(out=pk, in_=it, identity=idt[:]), r=[tit, T_idt], w=[tpk])
                            P.op("act", lambda e, q=q, pk=pk, sl=sl: e.copy(out=kT3[:, q, sl], in_=pk), r=[tpk], pw=[T_kT3[q]])
                def gbuild(i):
                    for b0 in range(0, 256 if os.environ.get("SKIP_GBUILD") != "1" else NB, NB):
                        p1, tp1 = P1p.get(); p2g, tp2g = P2gp.get()
                        for j in range(NB):
                            t_ = b0 + j
                            P.op("dve", lambda e, p1=p1, j=j, t_=t_: e.tensor_scalar(
                                out=p1[:, j, :], in0=iota_b[:], scalar1=kT3[:, 0, t_:t_ + 1], scalar2=None, op0=ALU.is_equal),
                                r=[T_iota, T_kT3[0]], **({"w": [tp1]} if j == 0 else {"pw": [tp1]}))
                            P.op("dve", lambda e, p2g=p2g, j=j, t_=t_: e.tensor_scalar(
                                out=p2g[:, j, :], in0=iota_b[:], scalar1=kT3[:, 1, t_:t_ + 1], scalar2=kT3[:, 2, t_:t_ + 1],
                                op0=ALU.is_equal, op1=ALU.mult),
                                r=[T_iota, T_kT3[1], T_kT3[2]], **({"w": [tp2g]} if j == 0 else {"pw": [tp2g]}))
                        for q0 in range(0, NB, 4):
                            pg_, tpg_ = pH.get()
                            for q in range(4):
                                P.op("pe", lambda e, q=q, q0=q0, pg_=pg_, p1=p1, p2g=p2g: e.matmul(
                                    pg_[:, q * 128:(q + 1) * 128], lhsT=p2g[:, q0 + q, :], rhs=p1[:, q0 + q, :], start=True, stop=True),
                                    r=[tp1, tp2g], **({"w": [tpg_]} if q == 0 else {"pw": [tpg_]}))
                            P.op("act", lambda e, b0=b0, q0=q0, pg_=pg_: e.copy(out=Gt[:, :, b0 + q0:b0 + q0 + 4],
                                                                              in_=pg_.rearrange("p (t i) -> p i t", t=4)), r=[tpg_], pw=[T_G])
                def sweep_final(i, gen):
                    Y = [py.get() for _ in range(4)]

                    def emit_Y(i1, a_, ta, vt_, tvt, k2):
                        for ts in range(2):
                            for hf in range(2):
                                yb, tyb = Y[ts * 2 + hf]
                                P.op("pe", lambda e, ts=ts, hf=hf, yb=yb, a_=a_, vt_=vt_, k2=k2, i1=i1: e.matmul(
                                    yb, lhsT=a_[:, ts * 128:(ts + 1) * 128], rhs=vt_[:, hf * 512:(hf + 1) * 512],
                                    start=(i1 == 0), stop=(i1 == 127)), r=[ta, tvt], **({"w": [tyb]} if i1 == 0 else {"pw": [tyb]}))
                    pend = []
                    PF = NSW - 3
                    blk = {}

                    def fetch(j):
                        ut_, tut = utp.get(); vt_, tvt = vtp.get()
                        P.dma("sp", ut_, uts[j], dUT[j % NSW], r=[T_uts], w=[tut])
                        P.dma("sp", vt_, vbs[j], dVT[j % NSW], r=[T_vbs], w=[tvt])
                        blk[j] = (ut_, tut, vt_, tvt)
                    for j in range(PF):
                        fetch(j)
                    for i1 in range(128):
                        if gen is not None:
                            next(gen, None)
                        if i1 + PF < 128:
                            fetch(i1 + PF)
                        ut_, tut, vt_, tvt = blk.pop(i1)
                        k2 = i1 % 2
                        ph_, tph = pH.get()
                        for c in range(8):
                            P.op("pe", lambda e, c=c, k2=k2, ph_=ph_, ut_=ut_: e.matmul(ph_[:, 0:256], lhsT=ut_[:, c * 128:(c + 1) * 128], rhs=xnTs[i % 2][:, c, :],
                                                                                     start=(c == 0), stop=(c == 7)),
                                 r=[tut, T_xnTs[i % 2]], **({"w": [tph]} if c == 0 else {"pw": [tph]}))
                        hg, thg = Hgp.get()
                        P.op("act", lambda e, hg=hg, ph_=ph_: e.activation(out=hg, in_=ph_[:, 0:256], func=AF.Gelu), r=[tph], w=[thg])
                        a_, ta = Ap.get()
                        P.op("dve", lambda e, a_=a_, hg=hg, i1=i1: e.tensor_tensor(out=a_, in0=hg, in1=Gt[:, i1, :], op=ALU.mult), r=[thg, T_G], w=[ta])
                        pend.append((i1, a_, ta, vt_, tvt, k2))
                        if len(pend) > 2:
                            emit_Y(*pend.pop(0))
                    while pend:
                        emit_Y(*pend.pop(0))
                    if gen is not None:
                        for _ in gen:
                            pass
                    for ts in range(2):
                        r0 = i * 256 + ts * 128
                        xs, txs = pools["xs"].get()
                        P.dma("sp", xs, h2s[r0:r0 + 128, :], dX[ts % 3], w=[txs])
                        fo, tfo = fin.get()
                        for hf in range(2):
                            yb, tyb = Y[ts * 2 + hf]
                            P.op("dve", lambda e, hf=hf, yb=yb, fo=fo, xs=xs: e.tensor_tensor(
                                out=fo[:, hf * 512:(hf + 1) * 512], in0=xs[:, hf * 512:(hf + 1) * 512], in1=yb, op=ALU.add),
                                r=[txs, tyb], **({"w": [tfo]} if hf == 0 else {"pw": [tfo]}))
                        junk, tj = pools["junk"].get()
                        ss, tss = pools["ss"].get()
                        P.op("act", lambda e, junk=junk, fo=fo, ss=ss: e.activation(out=junk, in_=fo, func=AF.Square, accum_out=ss[:, 0:1]),
                             r=[tfo], w=[tj, tss])
                        rstd_from_ss(ss[:, 0:1], tss, D, ss[:, 1:2], tss)
                        P.op("dve", lambda e, fo=fo, ss=ss: e.scalar_tensor_tensor(out=fo, in0=fo, scalar=ss[:, 1:2], in1=gfin[:], op0=ALU.mult, op1=ALU.mult),
                             r=[tss, T_gfin], w=[tfo])
                        P.dma("sp", out[r0:r0 + 128, :], fo, dOut[(fin.i - 1) % 2], r=[tfo])
                ntile = NT // 256
                for _ in front(0):
                    pass
                for i in range(ntile):
                    gbuild(i)
                    sweep_final(i, front(i + 1) if i + 1 < ntile else None)
            P.barrier()

        P.barrier()
        P.emit()
    return nc


def make_in_maps(inputs, NT, NPF, n_cores, seq):
    x = np.asarray(inputs["x"], dtype=np.float32)
    memf = np.asarray(inputs["mem"], dtype=np.float32)
    shared = {}
    for k in ("norm_mix", "w_in", "conv_w", "dn_conv_w", "dn_a_log", "dn_dt_bias", "dn_norm", "w_out", "norm_x",
              "norm_mem", "w_xq", "w_xkv", "w_xo", "norm_ffn", "w_pq", "peer_keys", "expert_u", "expert_v"):
        shared[k] = np.ascontiguousarray(np.asarray(inputs[k], dtype=np.float32)[0])
    shared["norm_final"] = np.ascontiguousarray(np.asarray(inputs["norm_final"], dtype=np.float32))
    maps = []
    for c in range(n_cores):
        b, s = c // 2, c % 2
        m = dict(shared)
        m["xm"] = np.ascontiguousarray(x[b, s * NT:(s + 1) * NT])
        m["xp"] = np.ascontiguousarray(x[b, 0:NPF]) if s == 1 else np.zeros((NPF, D), np.float32)
        m["mem"] = np.ascontiguousarray(memf[b])
        maps.append(m)
    return maps


def kernel(**inputs):
    NT = NPF = 4096
    nc = build(NT, NPF)
    maps = make_in_maps(inputs, NT, NPF, 8, 8192)
    res = run_bass_kernel_spmd(nc, maps, core_ids=list(range(8)))
    outp = np.zeros((4, 8192, D), np.float32)
    for c in range(8):
        b, s = c // 2, c % 2
        outp[b, s * NT:(s + 1) * NT] = res.results[c]["out"]
    return outp
```
